# Optimizing a Trainium2 kernel written in Bass

```python
import jax
import jax.numpy as jnp
from jax import lax

D_MODEL = 1024
BATCH = 8
SEQ = 4096
DEPTH = 2

CTX_LEN = 256
GRID_W = 64
N_Q_HEADS = 8
N_KV_HEADS = 2
HEAD_DIM = 128
Q_GROUPS = N_Q_HEADS // N_KV_HEADS
ATTN_W = N_Q_HEADS * HEAD_DIM
KV_W = N_KV_HEADS * HEAD_DIM
Q_BLOCK = 128
ROPE_THETA = 10000.0
SCONV_W = D_MODEL
SCONV_K = 3
LRU_W = D_MODEL
LRU_BLOCKS = 8
LRU_BS = LRU_W // LRU_BLOCKS
LRU_CONV_K = 4
LRU_C = 8.0
N_BRANCH = 3
FFN_W = -(-8 * D_MODEL // (3 * 256)) * 256
COL_SIZES = (ATTN_W, KV_W, KV_W, SCONV_W, SCONV_W, SCONV_W, LRU_W, LRU_W, N_BRANCH * D_MODEL)
N_IN = sum(COL_SIZES)
EPS = 1e-6

kernel_name = "hybrid_gated_branch_dit_block"


def rmsnorm(x, g):
    xf = x.astype(jnp.float32)
    xf = xf * lax.rsqrt(jnp.mean(xf * xf, axis=-1, keepdims=True) + EPS)
    return xf.astype(x.dtype) * g


def modulate(x, g, shift, scale):
    return rmsnorm(x, g) * (1 + scale) + shift


def split_cols(z):
    out, start = [], 0
    for size in COL_SIZES:
        out.append(z[..., start:start + size])
        start += size
    return out


def dwconv(x, w, b, left):
    k_width, length = w.shape[0], x.shape[1]
    xp = jnp.pad(x, ((0, 0), (left, k_width - 1 - left), (0, 0)))
    y = b
    for k in range(k_width):
        y = y + xp[:, k:k + length] * w[k]
    return y


def axial_rope(x, row, col):
    half, quarter = HEAD_DIM // 2, HEAD_DIM // 4
    inv_freq = ROPE_THETA ** (-jnp.arange(quarter, dtype=jnp.float32) / quarter)

    def rot(xh, pos):
        ang = pos.astype(jnp.float32)[:, None] * inv_freq
        cos, sin = jnp.cos(ang)[None, :, None, :], jnp.sin(ang)[None, :, None, :]
        x1 = xh[..., :quarter].astype(jnp.float32)
        x2 = xh[..., quarter:].astype(jnp.float32)
        return jnp.concatenate([x1 * cos - x2 * sin, x2 * cos + x1 * sin], axis=-1)

    out = jnp.concatenate([rot(x[..., :half], row), rot(x[..., half:], col)], axis=-1)
    return out.astype(x.dtype)


def gqa(qg, k, v):
    s = jnp.einsum("bqhgd,bkhd->bhgqk", qg, k).astype(jnp.float32)
    p = jax.nn.softmax(s, axis=-1).astype(v.dtype)
    return jnp.einsum("bhgqk,bkhd->bqhgd", p, v)


def attend_full(q, k, v):
    b, length = q.shape[:2]
    qg = q.reshape(b, length, N_KV_HEADS, Q_GROUPS, HEAD_DIM) * (HEAD_DIM ** -0.5)
    return gqa(qg, k, v).reshape(b, length, ATTN_W)


def attend_blocked(q, k, v):
    b, length = q.shape[:2]
    nb = length // Q_BLOCK
    qb = q.reshape(b, nb, Q_BLOCK, N_KV_HEADS, Q_GROUPS, HEAD_DIM) * (HEAD_DIM ** -0.5)
    qb = jnp.moveaxis(qb, 1, 0)
    o = lax.map(lambda qblk: gqa(qblk, k, v), qb)
    return jnp.moveaxis(o, 0, 1).reshape(b, length, ATTN_W)


def rglru_coeffs(x, wa, ba, wx, bx, lam):
    b, length, _ = x.shape
    xf = x.astype(jnp.float32)
    xb = xf.reshape(b, length, LRU_BLOCKS, LRU_BS)
    r = jax.nn.sigmoid(jnp.einsum("btnd,nde->btne", xb, wa.astype(jnp.float32)).reshape(b, length, LRU_W) + ba)
    i = jax.nn.sigmoid(jnp.einsum("btnd,nde->btne", xb, wx.astype(jnp.float32)).reshape(b, length, LRU_W) + bx)
    log_a = -LRU_C * r * jax.nn.softplus(-lam.astype(jnp.float32))
    a = jnp.exp(log_a)
    drive = jnp.sqrt(-jnp.expm1(2.0 * log_a)) * (i * xf)
    return a, drive


def linear_scan(a, u, reverse):
    def combine(e1, e2):
        a1, b1 = e1
        a2, b2 = e2
        return a1 * a2, a2 * b1 + b2
    return lax.associative_scan(combine, (a, u), reverse=reverse, axis=1)


def rglru_bidir(xc, xl, wa, ba, wx, bx, lam):
    yc, yl = [], []
    for d, rev in enumerate((False, True)):
        ac, uc = rglru_coeffs(xc, wa[d], ba[d], wx[d], bx[d], lam[d])
        _, hc = linear_scan(ac, uc, rev)
        h0 = hc[:, 0] if rev else hc[:, -1]
        al, ul = rglru_coeffs(xl, wa[d], ba[d], wx[d], bx[d], lam[d])
        cum_a, hl = linear_scan(al, ul, rev)
        yc.append(hc)
        yl.append(cum_a * h0[:, None] + hl)
    return (yc[0] + yc[1]).astype(xc.dtype), (yl[0] + yl[1]).astype(xl.dtype)


def merge_branches(attn, sconv, lru, gates, w_attn_out, w_sconv_out, w_lru_out, w_merge_out):
    g = jax.nn.sigmoid(gates.astype(jnp.float32)).astype(attn.dtype)
    merged = (g[..., :D_MODEL] * (attn @ w_attn_out)
              + g[..., D_MODEL:2 * D_MODEL] * (sconv @ w_sconv_out)
              + g[..., 2 * D_MODEL:] * (lru @ w_lru_out))
    return merged @ w_merge_out


def swiglu(h, w_ffn_in, w_ffn_out):
    gu = h @ w_ffn_in
    return (jax.nn.silu(gu[..., :FFN_W]) * gu[..., FFN_W:]) @ w_ffn_out


def setup_inputs(seed: int = 0) -> dict:
    key = jax.random.key(seed)
    ks = jax.random.split(key, 28)
    f32 = jnp.float32

    def nrm(k, shape, scale):
        return jax.random.normal(k, shape, f32) * scale

    def gain(k, shape):
        return 1.0 + 0.01 * jax.random.normal(k, shape, f32)

    a0 = jax.random.uniform(ks[20], (DEPTH, 2, LRU_W), f32, minval=0.9, maxval=0.999)
    return {
        "x": nrm(ks[0], (BATCH, SEQ, D_MODEL), 1.0),
        "c": nrm(ks[1], (BATCH, D_MODEL), 1.0),
        "ctx": nrm(ks[2], (BATCH, CTX_LEN, D_MODEL), 1.0),
        "c_ctx": nrm(ks[3], (D_MODEL,), 1.0),
        "w_mod": nrm(ks[4], (DEPTH, D_MODEL, 6 * D_MODEL), D_MODEL ** -0.5),
        "b_mod": nrm(ks[5], (DEPTH, 6 * D_MODEL), 0.01),
        "norm1_g": gain(ks[6], (DEPTH, D_MODEL)),
        "norm2_g": gain(ks[7], (DEPTH, D_MODEL)),
        "w_in": nrm(ks[8], (DEPTH, D_MODEL, N_IN), D_MODEL ** -0.5),
        "q_norm_g": gain(ks[9], (DEPTH, HEAD_DIM)),
        "k_norm_g": gain(ks[10], (DEPTH, HEAD_DIM)),
        "w_attn_out": nrm(ks[11], (DEPTH, ATTN_W, D_MODEL), ATTN_W ** -0.5),
        "sconv_w": nrm(ks[12], (DEPTH, SCONV_K, SCONV_W), SCONV_K ** -0.5),
        "sconv_b": nrm(ks[13], (DEPTH, SCONV_W), 0.01),
        "w_sconv_out": nrm(ks[14], (DEPTH, SCONV_W, D_MODEL), SCONV_W ** -0.5),
        "lru_conv_w": nrm(ks[15], (DEPTH, LRU_CONV_K, LRU_W), LRU_CONV_K ** -0.5),
        "lru_conv_b": nrm(ks[16], (DEPTH, LRU_W), 0.01),
        "lru_wa": nrm(ks[17], (DEPTH, 2, LRU_BLOCKS, LRU_BS, LRU_BS), LRU_BS ** -0.5),
        "lru_ba": nrm(ks[18], (DEPTH, 2, LRU_W), 0.01),
        "lru_wx": nrm(ks[19], (DEPTH, 2, LRU_BLOCKS, LRU_BS, LRU_BS), LRU_BS ** -0.5),
        "lru_bx": nrm(ks[21], (DEPTH, 2, LRU_W), 0.01),
        "lru_lambda": jnp.log(a0) - jnp.log1p(-a0),
        "w_lru_out": nrm(ks[22], (DEPTH, LRU_W, D_MODEL), LRU_W ** -0.5),
        "w_merge_out": nrm(ks[23], (DEPTH, D_MODEL, D_MODEL), D_MODEL ** -0.5),
        "w_ffn_in": nrm(ks[24], (DEPTH, D_MODEL, 2 * FFN_W), D_MODEL ** -0.5),
        "w_ffn_out": nrm(ks[25], (DEPTH, FFN_W, D_MODEL), FFN_W ** -0.5),
        "final_g": gain(ks[26], (D_MODEL,)),
    }


def reference(x, c, ctx, c_ctx, w_mod, b_mod, norm1_g, norm2_g, w_in, q_norm_g, k_norm_g,
              w_attn_out, sconv_w, sconv_b, w_sconv_out, lru_conv_w, lru_conv_b, lru_wa, lru_ba,
              lru_wx, lru_bx, lru_lambda, w_lru_out, w_merge_out, w_ffn_in, w_ffn_out, final_g):
    b, n_lat, _ = x.shape
    n_ctx = ctx.shape[1]
    rows = n_lat // GRID_W
    row = jnp.repeat(jnp.arange(rows, dtype=jnp.int32), GRID_W)
    col = jnp.tile(jnp.arange(GRID_W, dtype=jnp.int32), rows)
    silu_c = jax.nn.silu(c)
    silu_cc = jax.nn.silu(c_ctx)
    xl, xc = x, ctx
    for l in range(DEPTH):
        last = l == DEPTH - 1
        mod_l = jnp.split((silu_c @ w_mod[l] + b_mod[l])[:, None, :], 6, axis=-1)
        mod_c = jnp.split(silu_cc @ w_mod[l] + b_mod[l], 6, axis=-1)

        hl = modulate(xl, norm1_g[l], mod_l[0], mod_l[1])
        hc = modulate(xc, norm1_g[l], mod_c[0], mod_c[1])
        ql, kl, vl, sbl, scl, sxl, rxl, rgl, gtl = split_cols(hl @ w_in[l])
        qc, kc, vc, sbc, scc, sxc, rxc, rgc, gtc = split_cols(hc @ w_in[l])

        ql = axial_rope(rmsnorm(ql.reshape(b, n_lat, N_Q_HEADS, HEAD_DIM), q_norm_g[l]), row, col)
        kl = axial_rope(rmsnorm(kl.reshape(b, n_lat, N_KV_HEADS, HEAD_DIM), k_norm_g[l]), row, col)
        kc = rmsnorm(kc.reshape(b, n_ctx, N_KV_HEADS, HEAD_DIM), k_norm_g[l])
        vl = vl.reshape(b, n_lat, N_KV_HEADS, HEAD_DIM)
        vc = vc.reshape(b, n_ctx, N_KV_HEADS, HEAD_DIM)
        k_all = jnp.concatenate([kc, kl], axis=1)
        v_all = jnp.concatenate([vc, vl], axis=1)
        attn_l = attend_blocked(ql, k_all, v_all)

        sconv_l = sbl * dwconv(scl * sxl, sconv_w[l], sconv_b[l], SCONV_K // 2)

        lru_xc = dwconv(rxc, lru_conv_w[l], lru_conv_b[l], LRU_CONV_K // 2)
        lru_xl = dwconv(rxl, lru_conv_w[l], lru_conv_b[l], LRU_CONV_K // 2)
        lru_c, lru_l = rglru_bidir(lru_xc, lru_xl, lru_wa[l], lru_ba[l], lru_wx[l], lru_bx[l], lru_lambda[l])
        lru_l = lru_l * jax.nn.gelu(rgl)

        xl = xl + mod_l[2] * merge_branches(attn_l, sconv_l, lru_l, gtl, w_attn_out[l],
                                            w_sconv_out[l], w_lru_out[l], w_merge_out[l])

        if not last:
            qc = rmsnorm(qc.reshape(b, n_ctx, N_Q_HEADS, HEAD_DIM), q_norm_g[l])
            attn_c = attend_full(qc, kc, vc)
            sconv_c = sbc * dwconv(scc * sxc, sconv_w[l], sconv_b[l], SCONV_K // 2)
            lru_c = lru_c * jax.nn.gelu(rgc)
            xc = xc + mod_c[2] * merge_branches(attn_c, sconv_c, lru_c, gtc, w_attn_out[l],
                                                w_sconv_out[l], w_lru_out[l], w_merge_out[l])
            xc = xc + mod_c[5] * swiglu(modulate(xc, norm2_g[l], mod_c[3], mod_c[4]),
                                        w_ffn_in[l], w_ffn_out[l])

        xl = xl + mod_l[5] * swiglu(modulate(xl, norm2_g[l], mod_l[3], mod_l[4]),
                                    w_ffn_in[l], w_ffn_out[l])
    return rmsnorm(xl, final_g)
```

```python
import contextlib
import numpy as np
import concourse.bass as bass
import concourse.mybir as mybir
from concourse.ap import AP
from concourse.bass_utils import run_bass_kernel_spmd

F32 = mybir.dt.float32
BF16 = mybir.dt.bfloat16
ALU = mybir.AluOpType
AF = mybir.ActivationFunctionType

ENGS = ("pe", "act", "dve", "pool", "sp")
N_DMA_SEMS = 48
SB_LO = 16512
SB_HI = 229376 - 4096


class T:
    __slots__ = ("w", "r")

    def __init__(self):
        self.w = None
        self.r = {}


class Buf:
    def __init__(self, name, handle, space):
        self.name = name
        self.h = handle
        self.space = space
        self.whole = T()
        self.cells = {}
        self.epoch = -1

    def __getitem__(self, k):
        return self.h[k]


class Op:
    __slots__ = ("eng", "fn", "deps", "need_inc", "val", "dma")

    def __init__(self, eng, fn, deps, dma=None):
        self.eng = eng
        self.fn = fn
        self.deps = deps
        self.need_inc = False
        self.val = None
        self.dma = dma


class Prog:
    def __init__(self):
        self.nc = bass.Bass("TRN2", target_bir_lowering=False)
        self.stack = contextlib.ExitStack()
        self.ops = {e: [] for e in ENGS}
        self.dma_nextq = {}
        self.top_cache = {}
        self.dma_cnt = [0] * N_DMA_SEMS
        self.dma_last_ev = [None] * N_DMA_SEMS
        self.epoch = 0
        self.persist_ptr = SB_LO
        self.ptr = SB_LO
        self.top = SB_HI
        self.nalloc = 0

    def dram(self, name, shape, dtype, kind):
        return Buf(name, self.nc.dram_tensor(name, list(shape), dtype, kind=kind), "dram")

    @staticmethod
    def _bytes(shape, dtype):
        n = 1
        for s in shape[1:]:
            n *= s
        return n * (4 if dtype == F32 else 2)

    def _at(self, name, shape, dtype, off):
        self.nalloc += 1
        h = self.nc.alloc_sbuf_tensor_at("%s_%d" % (name, self.nalloc), list(shape), dtype, offset=off)
        return Buf(name, h, "sbuf")

    def persist(self, name, shape, dtype):
        assert self.ptr == self.persist_ptr
        nb = (self._bytes(shape, dtype) + 63) // 64 * 64
        b = self._at(name, shape, dtype, self.persist_ptr)
        self.persist_ptr += nb
        self.ptr = self.persist_ptr
        return b

    def sb(self, name, shape, dtype):
        nb = (self._bytes(shape, dtype) + 63) // 64 * 64
        assert self.ptr + nb <= self.top, "SBUF overflow at %s: %d + %d > %d" % (name, self.ptr, nb, self.top)
        b = self._at(name, shape, dtype, self.ptr)
        self.ptr += nb
        return b

    def sb_top(self, name, shape, dtype):
        nb = (self._bytes(shape, dtype) + 63) // 64 * 64
        self.top -= nb
        assert self.top >= self.ptr
        key = (name, tuple(shape), str(dtype), self.top)
        if key not in self.top_cache:
            self.top_cache[key] = self._at(name, shape, dtype, self.top)
        return self.top_cache[key]

    def psum(self, name, shape, dtype):
        h = self.stack.enter_context(self.nc.psum_tensor(name, list(shape), dtype))
        return Buf(name, h, "psum")

    def _norm(self, x):
        if isinstance(x, Buf):
            b, key = x, None
        else:
            b, key = x
        if b.epoch != self.epoch:
            b.epoch = self.epoch
            b.whole = T()
            b.cells = {}
        return b, key

    def _collect(self, reads, writes):
        deps = {}

        def add(ev):
            if ev is not None and deps.get(ev[0], -1) < ev[1]:
                deps[ev[0]] = ev[1]

        def add_r(t):
            for k, v in t.r.items():
                if deps.get(k, -1) < v:
                    deps[k] = v

        for x in reads:
            b, key = self._norm(x)
            add(b.whole.w)
            if key is None:
                for t in b.cells.values():
                    add(t.w)
            else:
                t = b.cells.get(key)
                if t is not None:
                    add(t.w)
        for x in writes:
            b, key = self._norm(x)
            add(b.whole.w)
            add_r(b.whole)
            if key is None:
                for t in b.cells.values():
                    add(t.w)
                    add_r(t)
            else:
                t = b.cells.get(key)
                if t is not None:
                    add(t.w)
                    add_r(t)
        return deps

    def _record(self, reads, writes, ev):
        k, v = ev
        for x in reads:
            b, key = self._norm(x)
            t = b.whole if key is None else b.cells.setdefault(key, T())
            if t.r.get(k, -1) < v:
                t.r[k] = v
        for x in writes:
            b, key = self._norm(x)
            if key is None:
                b.whole.w = ev
                b.whole.r = {}
                b.cells = {}
            else:
                t = b.cells.setdefault(key, T())
                t.w = ev
                t.r = {}

    def op(self, eng, fn, r=(), w=(), extra=()):
        deps = self._collect(r, w)
        for k, v in extra:
            if deps.get(k, -1) < v:
                deps[k] = v
        lst = self.ops[eng]
        if eng == "pe":
            deps.pop("pe", None)
        o = Op(eng, fn, list(deps.items()))
        self._record(r, w, (eng, len(lst)))
        lst.append(o)
        return o

    def dma(self, q, out, in_, r=(), w=(), **kw):
        deps = self._collect(r, w)
        lo, hi = {"sp": (0, 24), "act": (24, 36)}.get(q, (36, N_DMA_SEMS))
        i = self.dma_nextq.get(q, lo)
        self.dma_nextq[q] = lo + (i + 1 - lo) % (hi - lo)
        prev = self.dma_last_ev[i]
        if prev is not None and deps.get(prev[0], -1) < prev[1]:
            deps[prev[0]] = prev[1]
        self.dma_cnt[i] += 1
        val = 16 * self.dma_cnt[i]
        ev = (("dma", i), val)
        self.dma_last_ev[i] = ev
        fn = (lambda e: e.dma_start(out=out, in_=in_, **kw))
        o = Op(q, fn, list(deps.items()), dma=(i, val))
        self.ops[q].append(o)
        self._record(r, w, ev)
        return o

    def phase(self):
        B = Buf("bar", None, "none")
        for e in ENGS:
            if e == "sp":
                extra = [ev for ev in self.dma_last_ev if ev is not None]
                self.op(e, lambda eng: eng.nop(), w=[(B, e)], extra=extra)
            else:
                self.op(e, lambda eng: eng.drain(), w=[(B, e)])
        for e in ENGS:
            self.op(e, lambda eng: eng.nop(), r=[B])
        self.epoch += 1
        self.ptr = self.persist_ptr
        self.top = SB_HI

    def mm(self, out, lhsT, rhs, start, stop, r, w):
        return self.op("pe", lambda e: e.matmul(out, lhsT=lhsT, rhs=rhs, start=start, stop=stop), r, w)

    def tr(self, out, in_, ident, r, w):
        return self.op("pe", lambda e: e.transpose(out=out, in_=in_, identity=ident), r, w)

    def actv(self, out, in_, func, r, w, scale=None, bias=None, accum_out=None):
        kw = {}
        if scale is not None:
            kw["scale"] = scale
        if bias is not None:
            kw["bias"] = bias
        if accum_out is not None:
            kw["accum_out"] = accum_out
        return self.op("act", lambda e: e.activation(out=out, in_=in_, func=func, **kw), r, w)

    def tt(self, eng, out, in0, in1, op, r, w):
        return self.op(eng, lambda e: e.tensor_tensor(out=out, in0=in0, in1=in1, op=op), r, w)

    def ts(self, eng, out, in0, s1, s2, op0, op1, r, w):
        if s2 is None:
            return self.op(eng, lambda e: e.tensor_scalar(out=out, in0=in0, scalar1=s1, scalar2=None, op0=op0), r, w)
        return self.op(eng, lambda e: e.tensor_scalar(out=out, in0=in0, scalar1=s1, scalar2=s2, op0=op0, op1=op1), r, w)

    def stt(self, out, in0, scalar, in1, op0, op1, r, w, accum_out=None):
        if accum_out is not None:
            return self.op("dve", lambda e: e.scalar_tensor_tensor(out=out, in0=in0, scalar=scalar, in1=in1, op0=op0,
                                                                   op1=op1, accum_out=accum_out), r, w)
        return self.op("dve", lambda e: e.scalar_tensor_tensor(out=out, in0=in0, scalar=scalar, in1=in1, op0=op0, op1=op1), r, w)

    def scan(self, out, d0, d1, init, r, w):
        return self.op("dve", lambda e: e.tensor_tensor_scan(out=out, data0=d0, data1=d1, initial=init,
                                                             op0=ALU.mult, op1=ALU.add), r, w)

    def cp(self, eng, out, in_, r, w):
        if eng == "act":
            return self.op("act", lambda e: e.activation(out=out, in_=in_, func=AF.Copy), r, w)
        return self.op(eng, lambda e: e.tensor_copy(out=out, in_=in_), r, w)

    def recip(self, out, in_, r, w):
        return self.op("dve", lambda e: e.reciprocal(out=out, in_=in_), r, w)

    def memset(self, eng, ap, val, w):
        return self.op(eng, lambda e: e.memset(ap, val), (), w)

    def finish(self):
        nc = self.nc
        for e in ENGS:
            for o in self.ops[e]:
                for k, v in o.deps:
                    if isinstance(k, str):
                        self.ops[k][v].need_inc = True
        for e in ENGS:
            c = 0
            for o in self.ops[e]:
                if o.dma is None and o.need_inc:
                    c += 1
                    o.val = c
        sems = {e: self.stack.enter_context(nc.semaphore("s_" + e)) for e in ENGS}
        dsems = [self.stack.enter_context(nc.semaphore("s_dma%d" % i)) for i in range(N_DMA_SEMS)]
        final_waits = [(("dma", i), 16 * self.dma_cnt[i]) for i in range(N_DMA_SEMS) if self.dma_cnt[i]]

        def emit(ename, eobj):
            seen = {}
            for o in self.ops[ename]:
                for k, v in o.deps:
                    if isinstance(k, str):
                        val = self.ops[k][v].val
                        sem = sems[k]
                    else:
                        val = v
                        sem = dsems[k[1]]
                    if seen.get(k, -1) >= val:
                        continue
                    seen[k] = val
                    eobj.wait_ge(sem, val)
                ins = o.fn(eobj)
                if o.dma is not None:
                    ins.then_inc(dsems[o.dma[0]], 16)
                elif o.need_inc:
                    ins.then_inc(sems[ename], 1)
            if ename == "sp":
                for k, val in final_waits:
                    if seen.get(k, -1) < val:
                        eobj.wait_ge(dsems[k[1]], val)

        with nc.Block() as block:
            @block.tensor
            def _(eng):
                emit("pe", eng)

            @block.scalar
            def _(eng):
                emit("act", eng)

            @block.vector
            def _(eng):
                emit("dve", eng)

            @block.gpsimd
            def _(eng):
                emit("pool", eng)

            @block.sync
            def _(eng):
                emit("sp", eng)
        self.stack.close()
        return nc


def mk_ap(buf, offset, pattern):
    return AP(buf.h, offset, [list(p) for p in pattern])


def bcast_col(col_ap, n):
    return AP(col_ap.tensor, col_ap.offset, [list(col_ap.ap[0]), [0, n]])


class Ring:
    def __init__(self, bufs):
        self.bufs = bufs
        self.i = 0

    def next(self):
        b = self.bufs[self.i % len(self.bufs)]
        self.i += 1
        return b


DM = 1024
SEQ = 4096
NCTX = 256
NA = SEQ + NCTX
NIN = 9728
FFW = 2816
PVL = 138
NPV = 2 * PVL + 8
C_Q, C_K, C_V, C_SB, C_SC, C_SX, C_RX, C_RG, C_GT = 0, 1024, 1280, 1536, 2560, 3584, 4608, 5632, 6656
EPS = 1e-6


def a_tiles(do_ctx):
    tl = []
    if do_ctx:
        tl.append((0, 256, False, 0))
    for i in range(8):
        tl.append((256 + 512 * i, 512, True, 512 * i))
    return tl


def build(nlayers=2, stop_after=None, dbg=()):
    P = Prog()
    kin = "ExternalInput"

    def DI(name, shape, dt=F32):
        return P.dram(name, shape, dt, kin)

    def DS(name, shape, dt):
        return P.dram(name, shape, dt, "ExternalOutput" if name in dbg else "Internal")

    x_d = DI("x", [SEQ, DM]); ctx_d = DI("ctx", [NCTX, DM]); cc_d = DI("cc", [128, 16]); pv_d = DI("pv", [128, NPV])
    bmod_d = DI("bmod", [2, 6 * DM]); fg_d = DI("fgrow", [1, DM]); cos_d = DI("cost", [128, SEQ]); sin_d = DI("sint", [128, SEQ])
    cm_d = DI("cm", [128, 256])
    wmod_d = DI("w_mod", [2, DM, 6 * DM]); win_d = DI("w_in", [2, DM, NIN])
    wao_d = DI("w_ao", [2, DM, DM]); wso_d = DI("w_so", [2, DM, DM]); wlo_d = DI("w_lo", [2, DM, DM]); wmo_d = DI("w_mo", [2, DM, DM])
    wfi_d = DI("w_fi", [2, DM, 2 * FFW]); wfo_d = DI("w_fo", [2, FFW, DM])
    lwa_d = DI("lwa", [2, 2, 8, 128, 128]); lwx_d = DI("lwx", [2, 2, 8, 128, 128])
    y_d = P.dram("y", [SEQ, DM], F32, "ExternalOutput")

    XS = DS("XS", [SEQ, DM], F32)
    QS = DS("QS", [8, 128, NA], BF16); KS = DS("KS", [2, 128, NA], BF16); VS = DS("VS", [128, 34, 256], BF16)
    AT = DS("ATs", [8, 128, NA], BF16); SC = DS("SCs", [8, 128, NA], BF16); LR = DS("LRs", [8, 128, NA], BF16)
    GS = DS("GSs", [24, 128, NA], BF16); FA = DS("FAs", [22, 128, NA], BF16); GRD = DS("GRD", [8, 128, DM], F32)
    HTD = DS("HTD", [128, 8, NA], BF16)
    XCD = DS("XCD", [128, 2, DM], F32)

    PS = [P.psum("ps%d" % i, [128, 512], F32) for i in range(8)]

    ident = P.persist("ident", [128, 128], F32); perm = P.persist("perm", [128, 128], F32)
    ones_f = P.persist("ones_f", [128, 128], F32); ones_b = P.persist("ones_b", [128, 128], BF16)
    eps_t = P.persist("eps_t", [128, 1], F32); one_t = P.persist("one_t", [128, 1], F32)
    perm_b = P.persist("perm_b", [128, 128], BF16)
    pv = P.persist("pv", [128, NPV], F32); cc = P.persist("cc", [128, 16], F32); scc = P.persist("scc", [128, 16], F32)
    XC = P.persist("XC", [128, 2, DM], F32)
    MV = P.persist("MV", [128, 2 * 2 * 4 * 8], F32)
    LV = P.persist("LV", [128, 2 * 3 * 16], F32)
    QG = P.persist("QG", [128, 2], F32)

    def mvc(l, who, vi, j0=0, n=8):
        o = ((l * 2 + who) * 4 + vi) * 8 + j0
        return MV[:, o:o + n]

    def lvc(l, which, d, n):
        o = (l * 3 + which) * 16 + d * 8 + n
        return LV[:, o:o + 1]

    def pvc(l, off, n=1):
        return pv[:, l * PVL + off:l * PVL + off + n]

    P.dma("sp", ident[:, :], cm_d[:, 0:128], w=[ident]); P.dma("sp", perm[:, :], cm_d[:, 128:256], w=[perm])
    P.dma("sp", pv[:, :], pv_d[:, :], w=[pv]); P.dma("sp", cc[:, :], cc_d[:, :], w=[cc])
    P.dma("sp", XC[:, :, :], ctx_d.h.rearrange("(t p) f -> p t f", p=128), w=[XC])
    P.memset("dve", ones_f[:, :], 1.0, [ones_f]); P.memset("dve", ones_b[:, :], 1.0, [ones_b])
    P.memset("dve", eps_t[:, :], EPS, [eps_t]); P.memset("dve", one_t[:, :], 1.0, [one_t])
    P.actv(scc[:, :], cc[:, :], AF.Silu, [cc], [scc])
    P.cp("dve", perm_b[:, :], perm[:, :], [perm], [perm_b])
    for l in range(2):
        P.ts("dve", QG[:, l:l + 1], pvc(l, 136), 128 ** -0.5, None, ALU.mult, None, [pv], [QG])
        P.ts("dve", LV[:, (l * 3 + 0) * 16:(l * 3 + 0) * 16 + 16], pvc(l, 88, 16), 0.5, None, ALU.mult, None, [pv], [LV])
        P.ts("dve", LV[:, (l * 3 + 1) * 16:(l * 3 + 1) * 16 + 16], pvc(l, 104, 16), 0.5, None, ALU.mult, None, [pv], [LV])
    tmpl = P.persist("tmpl", [128, 32], F32); tser = P.persist("tser", [128, 32], F32)
    ttmp = P.persist("ttmp", [128, 32], F32); tmsk = P.persist("tmsk", [128, 32], F32); tln = P.persist("tln", [128, 32], F32)
    for l in range(2):
        P.actv(tmpl[:, l * 16:l * 16 + 16], pvc(l, 120, 16), AF.Exp, [pv], [tmpl], scale=-1.0)
    P.actv(tln[:, :], tmpl[:, :], AF.Ln, [tmpl, one_t], [tln], bias=one_t[:, 0:1])
    P.memset("dve", tser[:, :], 1.0 / 8, [tser])
    for c in (1.0 / 7, 1.0 / 6, 1.0 / 5, 1.0 / 4, 1.0 / 3, 1.0 / 2, 1.0):
        P.tt("dve", ttmp[:, :], tser[:, :], tmpl[:, :], ALU.mult, [tser, tmpl], [ttmp])
        P.ts("dve", tser[:, :], ttmp[:, :], -1.0, c, ALU.mult, ALU.add, [ttmp], [tser])
    P.tt("dve", tser[:, :], tser[:, :], tmpl[:, :], ALU.mult, [tser, tmpl], [tser])
    P.ts("dve", tmsk[:, :], tmpl[:, :], 0.25, None, ALU.is_lt, None, [tmpl], [tmsk])
    P.tt("dve", ttmp[:, :], tser[:, :], tln[:, :], ALU.subtract, [tser, tln], [ttmp])
    P.tt("dve", ttmp[:, :], ttmp[:, :], tmsk[:, :], ALU.mult, [ttmp, tmsk], [ttmp])
    P.tt("dve", tln[:, :], tln[:, :], ttmp[:, :], ALU.add, [tln, ttmp], [tln])
    for l in range(2):
        P.ts("dve", LV[:, (l * 3 + 2) * 16:(l * 3 + 2) * 16 + 16], tln[:, l * 16:l * 16 + 16], -4.0, None, ALU.mult, None, [tln], [LV])

    win_v = win_d.h.rearrange("l (kc p) n -> l p kc n", p=128)

    def stop(tag):
        return stop_after == tag

    def phase_mod(l):
        P.phase()
        bm = P.sb("bm", [128, 6 * DM], F32)
        P.dma("sp", bm[:, :], mk_ap(bmod_d, l * 6 * DM, [[0, 128], [1, 6 * DM]]), w=[bm])
        wr = Ring([P.sb("wm%d" % i, [128, 8, 512], BF16) for i in range(3)])
        gr = Ring([P.sb("gt%d" % i, [128, 512], F32) for i in range(2)])
        tr_ = Ring([P.sb("tm%d" % i, [128, 512], F32) for i in range(2)])
        jr = Ring([P.sb("jk%d" % i, [128, 128], F32) for i in range(2)])
        wv = wmod_d.h.rearrange("l (kc p) n -> l p kc n", p=128)
        sccb = P.sb("sccb", [128, 16, 128], BF16)
        a0_ = scc[:, 0:16]
        P.cp("dve", sccb[:, :, :], AP(a0_.tensor, a0_.offset, [list(a0_.ap[0]), [1, 16], [0, 128]]), [scc], [sccb])
        for ti in range(12):
            v, hf = divmod(ti, 2)
            wt = wr.next()
            P.dma("pool", wt[:, :, :], wv[l, :, :, ti * 512:(ti + 1) * 512], w=[wt])
            for who in range(2):
                ps = PS[(ti * 2 + who) % 4]
                for kc in range(8):
                    P.mm(ps[:, :], sccb[:, who * 8 + kc, :], wt[:, kc, :], kc == 0, kc == 7, [sccb, wt], [ps])
                bsl = bm[:, ti * 512:(ti + 1) * 512]
                if v in (2, 5):
                    gt = gr.next()
                    P.tt("dve", gt[:, :], ps[:, :], bsl, ALU.add, [ps, bm], [gt])
                    gi = l * 4 + who * 2 + (1 if v == 5 else 0)
                    P.dma("sp", GRD[gi, :, hf * 512:(hf + 1) * 512], gt[:, :], r=[gt], w=[(GRD, (gi, hf))])
                else:
                    vi = {0: 0, 1: 1, 3: 2, 4: 3}[v]
                    tm = tr_.next()
                    P.tt("dve", tm[:, :], ps[:, :], bsl, ALU.add, [ps, bm], [tm])
                    for blk in range(4):
                        jk = jr.next()
                        j = hf * 4 + blk
                        P.stt(jk[:, :], tm[:, blk * 128:(blk + 1) * 128], 1.0, ident[:, :], ALU.mult, ALU.mult,
                              [tm, ident], [jk, (MV, (l, who, vi, j))], accum_out=mvc(l, who, vi, j, 1))
        t8 = P.sb("t8", [128, 8], F32)
        for who in range(2):
            for vi, goff in ((1, 0), (3, 8)):
                P.ts("dve", t8[:, :], mvc(l, who, vi), 1.0, None, ALU.add, None, [MV], [t8])
                P.tt("dve", mvc(l, who, vi), t8[:, :], pvc(l, goff, 8), ALU.mult, [t8, pv], [MV])

    def phase_norm(l, which, src, do_ctx):
        P.phase()
        HT = P.sb_top("HT", [128, 8, NA], BF16)
        vi_sh, vi_gs = (0, 1) if which == 1 else (2, 3)
        xr = Ring([P.sb("xt%d" % i, [128, DM], F32) for i in range(4)])
        xnr = Ring([P.sb("xn%d" % i, [128, DM], F32) for i in range(3)])
        sqj = P.sb("sqj", [128, DM], BF16)
        sm = Ring([P.sb("sm%d" % i, [128, 1], F32) for i in range(12)])
        tiles = ([(1, 0), (1, 1)] if do_ctx else []) + [(0, t) for t in range(32)]

        def stats(n):
            who, tt_ = tiles[n]
            if who == 0:
                xt = xr.next()
                P.dma("sp", xt[:, :], src[tt_ * 128:(tt_ + 1) * 128, :], w=[xt])
                xin, xdep = xt[:, :], xt
            else:
                xin, xdep = XC[:, tt_, :], XC
            ss = sm.next(); sd = sm.next(); rs = sm.next()
            P.actv(sqj[:, :], xin, AF.Square, [xdep], [sqj, ss], accum_out=ss[:, 0:1])
            P.actv(sd[:, :], ss[:, :], AF.Sqrt, [ss, eps_t], [sd], scale=1.0 / DM, bias=eps_t[:, 0:1])
            P.recip(rs[:, :], sd[:, :], [sd], [rs])
            xn = xnr.next()
            P.ts("dve", xn[:, :], xin, rs[:, 0:1], None, ALU.mult, None, [xdep, rs], [xn])
            return xn

        def tpose(n, xn):
            who, tt_ = tiles[n]
            col0 = (256 + tt_ * 128) if who == 0 else tt_ * 128
            for half in range(2):
                ps = PS[(2 * n + half) % 8]
                for q in range(4):
                    kc = half * 4 + q
                    P.tr(ps[:, q * 128:(q + 1) * 128], xn[:, kc * 128:(kc + 1) * 128], ident[:, :], [xn, ident], [ps])
                for q in range(4):
                    kc = half * 4 + q
                    o = HT[:, kc, col0:col0 + 128]
                    scl = mvc(l, who, vi_gs, kc, 1); shf = mvc(l, who, vi_sh, kc, 1)
                    if half == 0:
                        P.actv(o, ps[:, q * 128:(q + 1) * 128], AF.Identity, [ps, MV], [(HT, (n, kc))], scale=scl, bias=shf)
                    else:
                        P.ts("dve", o, ps[:, q * 128:(q + 1) * 128], scl, shf, ALU.mult, ALU.add, [ps, MV], [(HT, (n, kc))])
        prev = stats(0)
        for n in range(1, len(tiles)):
            cur = stats(n)
            tpose(n - 1, prev)
            prev = cur
        tpose(len(tiles) - 1, prev)
        return HT

    def keep_ht():
        return P.sb_top("HT", [128, 8, NA], BF16)

    def group(ps, n, wfn, wdep, HT, a0):
        for kc in range(8):
            P.mm(ps[:, 0:n], wfn(kc), HT[:, kc, a0:a0 + n], kc == 0, kc == 7, [wdep], [ps])

    def wload(buf3, l, col0, ncols, slot=None):
        dst = buf3[:, :, 0:ncols] if slot is None else buf3[:, slot, :, 0:ncols]
        P.dma("pool", dst, win_v[l, :, :, col0:col0 + ncols], w=[buf3])

    def phase_sconv_gates(l, do_ctx):
        P.phase()
        HT = keep_ht()
        tiles = a_tiles(do_ctx)
        W = NA + 4
        Mb = P.sb("Mb", [128, W], F32); Ab = P.sb("Ab", [128, W], F32)
        OBr = Ring([P.sb("OB%d" % i, [128, NA], BF16) for i in range(2)])
        scr = Ring([P.sb("sce%d" % i, [128, 512], F32) for i in range(3)])
        wbufs = [P.sb("ws%d" % i, [128, 3, 8, 128], BF16) for i in range(3)]
        for c in (0, 257, 258, W - 1):
            P.memset("pool", Mb[:, c:c + 1], 0.0, [Mb])

        def mcol(a0, is_lat):
            return (259 + (a0 - 256)) if is_lat else (1 + a0)

        def load(j):
            wb = wbufs[j % 3]
            for s, cb in enumerate((C_SB, C_SC, C_SX)):
                wload(wb, l, cb + j * 128, 128, slot=s)
            return wb
        loaded = {0: load(0), 1: load(1)}
        pi = 0
        for j in range(8):
            wb = loaded[j]
            if j + 2 < 8:
                loaded[j + 2] = load(j + 2)
            for (a0, n, is_lat, t0) in tiles:
                psc = PS[pi % 8]; psx = PS[(pi + 1) % 8]; pi += 2
                group(psc, n, lambda kc: wb[:, 1, kc, :], wb, HT, a0)
                group(psx, n, lambda kc: wb[:, 2, kc, :], wb, HT, a0)
                sce = scr.next()
                P.cp("act", sce[:, 0:n], psc[:, 0:n], [psc], [sce])
                m0 = mcol(a0, is_lat)
                P.tt("dve", Mb[:, m0:m0 + n], psx[:, 0:n], sce[:, 0:n], ALU.mult, [psx, sce], [(Mb, a0)])
            lo, hi = (1, W - 1) if do_ctx else (259, W - 1)
            L = hi - lo
            w0 = pvc(l, 16 + 0 * 8 + j); w1 = pvc(l, 16 + 1 * 8 + j); w2 = pvc(l, 16 + 2 * 8 + j); bb = pvc(l, 40 + j)
            P.ts("dve", Ab[:, lo:hi], Mb[:, lo - 1:hi - 1], w0, bb, ALU.mult, ALU.add, [Mb, pv], [Ab])
            P.stt(Ab[:, lo:hi], Mb[:, lo:hi], w1, Ab[:, lo:hi], ALU.mult, ALU.add, [Mb, pv, Ab], [Ab])
            P.stt(Ab[:, lo:hi], Mb[:, lo + 1:hi + 1], w2, Ab[:, lo:hi], ALU.mult, ALU.add, [Mb, pv, Ab], [Ab])
            OB = OBr.next()
            for (a0, n, is_lat, t0) in tiles:
                psb = PS[pi % 8]; pi += 1
                group(psb, n, lambda kc: wb[:, 0, kc, :], wb, HT, a0)
                m0 = mcol(a0, is_lat)
                P.tt("dve", OB[:, a0:a0 + n], psb[:, 0:n], Ab[:, m0:m0 + n], ALU.mult, [psb, Ab], [(OB, a0)])
            c0 = 0 if do_ctx else 256
            P.dma("sp", SC[j, :, c0:NA], OB[:, c0:NA], r=[OB], w=[(SC, j)])
        gw = [P.sb("gw%d" % i, [128, 8, 512], BF16) for i in range(2)]
        wload(gw[0], l, C_GT, 512)
        for g4 in range(6):
            if g4 + 1 < 6:
                wload(gw[(g4 + 1) % 2], l, C_GT + (g4 + 1) * 512, 512)
            wb = gw[g4 % 2]
            for s in range(4):
                ch = g4 * 4 + s
                OB = OBr.next()
                for (a0, n, is_lat, t0) in tiles:
                    ps = PS[pi % 8]; pi += 1
                    group(ps, n, lambda kc: wb[:, kc, s * 128:(s + 1) * 128], wb, HT, a0)
                    P.actv(OB[:, a0:a0 + n], ps[:, 0:n], AF.Sigmoid, [ps], [(OB, a0)])
                c0 = 0 if do_ctx else 256
                P.dma("sp", GS[ch, :, c0:NA], OB[:, c0:NA], r=[OB], w=[(GS, ch)])

    def phase_lru(l, do_ctx):
        P.phase()
        HT = keep_ht()
        tiles = a_tiles(True)
        W = NA + 6
        U = P.sb("U", [128, W], F32); M = P.sb("M", [128, W], F32)
        A_ = [P.sb("A%d" % d, [128, W], F32) for d in range(2)]
        D_ = [P.sb("D%d" % d, [128, W], F32) for d in range(2)]
        UB = P.sb("UB", [128, W], BF16)
        wrx = [P.sb("wrx%d" % i, [128, 2, 8, 128], BF16) for i in range(2)]
        wl = [P.sb("wl%d" % i, [128, 2, 2, 128], BF16) for i in range(2)]

        def lcol(a0, is_lat):
            return (261 + (a0 - 256)) if is_lat else (2 + a0)

        def load(n):
            wb = wrx[n % 2]; w2 = wl[n % 2]
            wload(wb, l, C_RX + n * 128, 128, slot=0)
            wload(wb, l, C_RG + n * 128, 128, slot=1)
            P.dma("pool", w2[:, 0, :, :], lwa_d.h.rearrange("l d n p e -> l n p d e")[l, n], w=[w2])
            P.dma("pool", w2[:, 1, :, :], lwx_d.h.rearrange("l d n p e -> l n p d e")[l, n], w=[w2])
            return wb, w2

        def rev(b_, c0, nn):
            a_ = b_[:, c0:c0 + nn]
            return AP(a_.tensor, a_.offset + nn - 1, [list(a_.ap[0]), [-1, nn]])
        for b_ in (A_[0], A_[1], D_[0], D_[1]):
            P.memset("pool", b_[:, 0:W], 0.0, [b_])
        nxt = load(0)
        pi = 0
        LO, HI = 2, W - 1
        for n in range(8):
            wb, w2 = nxt
            if n + 1 < 8:
                nxt = load(n + 1)
            RX = D_[1]
            for (c, k) in ((0, 2), (258, 3), (W - 1, 1)):
                P.memset("pool", RX[:, c:c + k], 0.0, [RX])
            for (a0, nn, is_lat, t0) in tiles:
                ps = PS[pi % 8]; pi += 1
                group(ps, nn, lambda kc: wb[:, 0, kc, :], wb, HT, a0)
                c0 = lcol(a0, is_lat)
                if pi % 2:
                    P.cp("act", RX[:, c0:c0 + nn], ps[:, 0:nn], [ps], [(RX, a0)])
                else:
                    P.cp("dve", RX[:, c0:c0 + nn], ps[:, 0:nn], [ps], [(RX, a0)])
            wc = [pvc(l, 48 + k * 8 + n) for k in range(4)]
            P.ts("dve", U[:, LO:HI], RX[:, LO - 2:HI - 2], wc[0], pvc(l, 80 + n), ALU.mult, ALU.add, [RX, pv], [U])
            for k in (1, 2, 3):
                P.stt(U[:, LO:HI], RX[:, LO - 2 + k:HI - 2 + k], wc[k], U[:, LO:HI], ALU.mult, ALU.add, [RX, pv, U], [U])
            P.cp("act", UB[:, LO:HI], U[:, LO:HI], [U], [UB])
            for d in range(2):
                A = A_[d]; D = D_[d]
                for (a0, nn, is_lat, t0) in tiles:
                    c0 = lcol(a0, is_lat)
                    pr = PS[pi % 8]; pz = PS[(pi + 1) % 8]; pi += 2
                    P.mm(pr[:, 0:nn], w2[:, 0, d, :], UB[:, c0:c0 + nn], True, True, [w2, UB], [pr])
                    P.mm(pz[:, 0:nn], w2[:, 1, d, :], UB[:, c0:c0 + nn], True, True, [w2, UB], [pz])
                    P.actv(A[:, c0:c0 + nn], pr[:, 0:nn], AF.Tanh, [pr, LV], [(A, a0)], scale=0.5, bias=lvc(l, 0, d, n))
                    P.actv(D[:, c0:c0 + nn], pz[:, 0:nn], AF.Tanh, [pz, LV], [(D, a0)], scale=0.5, bias=lvc(l, 1, d, n))
                P.actv(A[:, LO:HI], A[:, LO:HI], AF.Exp, [A, LV], [A], scale=lvc(l, 2, d, n), bias=lvc(l, 2, d, n))
                P.stt(D[:, LO:HI], D[:, LO:HI], 1.0, U[:, LO:HI], ALU.add, ALU.mult, [D, U], [D])
                P.actv(M[:, LO:HI], A[:, LO:HI], AF.Square, [A], [M])
                P.ts("dve", M[:, LO:HI], M[:, LO:HI], 1.0, None, ALU.min, None, [M], [M])
                P.actv(M[:, LO:HI], M[:, LO:HI], AF.Sqrt, [M, one_t], [M], scale=-1.0, bias=one_t[:, 0:1])
                P.stt(D[:, LO:HI], D[:, LO:HI], 0.5, M[:, LO:HI], ALU.mult, ALU.mult, [D, M], [D])
                if d == 0:
                    P.scan(D[:, 2:258], A[:, 2:258], D[:, 2:258], 0.0, [A, D], [D])
                    P.scan(D[:, 261:4357], A[:, 261:4357], D[:, 261:4357], D[:, 257:258], [A, D], [D])
                else:
                    P.scan(rev(D, 2, 256), rev(A, 2, 256), rev(D, 2, 256), 0.0, [A, D], [D])
                    P.scan(rev(D, 261, 4096), rev(A, 261, 4096), rev(D, 261, 4096), D[:, 2:3], [A, D], [D])
            G = A_[0]; Y = A_[1]
            gt = tiles if do_ctx else tiles[1:]
            for (a0, nn, is_lat, t0) in gt:
                ps = PS[pi % 8]; pi += 1
                group(ps, nn, lambda kc: wb[:, 1, kc, :], wb, HT, a0)
                c0 = lcol(a0, is_lat)
                P.actv(G[:, c0:c0 + nn], ps[:, 0:nn], AF.Gelu_apprx_tanh, [ps], [(G, a0)])
            P.tt("dve", D_[0][:, LO:HI], D_[0][:, LO:HI], D_[1][:, LO:HI], ALU.add, [D_[0], D_[1]], [D_[0]])
            P.tt("dve", Y[:, LO:HI], D_[0][:, LO:HI], G[:, LO:HI], ALU.mult, [D_[0], G], [Y])
            if do_ctx:
                P.dma("pool", LR[n, :, 0:256], Y[:, 2:258], r=[Y], w=[(LR, (n, 0))])
            P.dma("pool", LR[n, :, 256:NA], Y[:, 261:4357], r=[Y], w=[(LR, (n, 1))])

    def phase_qkv(l, do_ctx):
        P.phase()
        HT = keep_ht()
        cosb = P.sb("cosb", [128, SEQ], F32); sinb = P.sb("sinb", [128, SEQ], F32)
        P.dma("sp", cosb[:, :], cos_d[:, :], w=[cosb]); P.dma("sp", sinb[:, :], sin_d[:, :], w=[sinb])
        OBr = Ring([P.sb("OBq%d" % i, [128, NA], BF16) for i in range(2)])
        VB = P.sb("VB", [128, 34, 256], BF16)
        wv_ = P.sb("wv", [128, 8, 256], BF16)
        wk = P.sb("wk", [128, 8, 256], BF16)
        wq = [P.sb("wq%d" % i, [128, 8, 512], BF16) for i in range(2)]
        mkr = lambda nm, k, dt: Ring([P.sb("%s%d" % (nm, i), [128, 512], dt) for i in range(k)])
        sqr = mkr("sq", 3, BF16); sdr = mkr("sd", 2, F32); rsr = mkr("rs", 2, F32); qnr = mkr("qn", 4, F32)
        qbr = mkr("qnb", 3, BF16); t1r = mkr("t1", 2, F32); t2r = mkr("t2", 2, F32)
        wload(wk, l, C_K, 256); wload(wv_, l, C_V, 256); wload(wq[0], l, C_Q, 512); wload(wq[1], l, C_Q + 512, 512)

        def run_items(items):
            N = len(items)
            stt_ = {}

            def A(i):
                it = items[i]
                ps = PS[i % 4]
                group(ps, it["n"], it["wfn"], it["wdep"], HT, it["a0"])
                sq = sqr.next()
                P.actv(sq[:, 0:it["n"]], ps[:, 0:it["n"]], AF.Square, [ps], [sq])
                stt_[i] = {"ps": ps, "sq": sq}

            def B(i):
                it = items[i]; n = it["n"]; st = stt_[i]
                pss = PS[4 + i % 2]
                P.mm(pss[:, 0:n], ones_b[:, :], st["sq"][:, 0:n], True, True, [ones_b, st["sq"]], [pss])
                sd = sdr.next(); rs = rsr.next()
                P.actv(sd[:, 0:n], pss[:, 0:n], AF.Ln, [pss, eps_t], [sd], scale=1.0 / 128, bias=eps_t[:, 0:1])
                P.actv(rs[:, 0:n], sd[:, 0:n], AF.Exp, [sd], [rs], scale=-0.5)
                if not it["lat"]:
                    P.stt(it["out"], st["ps"][:, 0:n], it["g"], rs[:, 0:n], ALU.mult, ALU.mult, [st["ps"], it["gdep"], rs], it["outw"])
                else:
                    qn = qnr.next(); qb = qbr.next()
                    P.stt(qn[:, 0:n], st["ps"][:, 0:n], it["g"], rs[:, 0:n], ALU.mult, ALU.mult, [st["ps"], it["gdep"], rs], [qn])
                    st["qn"] = qn; st["qb"] = qb

            def Cc(i):
                it = items[i]; n = it["n"]; st = stt_[i]
                if it["lat"]:
                    P.cp("act", st["qb"][:, 0:n], st["qn"][:, 0:n], [st["qn"]], [st["qb"]])

            def C(i):
                it = items[i]; n = it["n"]; st = stt_.pop(i)
                if it["lat"]:
                    t0 = it["t0"]
                    pq = PS[6 + i % 2]
                    P.mm(pq[:, 0:n], perm_b[:, :], st["qb"][:, 0:n], True, True, [perm_b, st["qb"]], [pq])
                    t1 = t1r.next(); t2 = t2r.next()
                    P.tt("pool", t1[:, 0:n], st["qn"][:, 0:n], cosb[:, t0:t0 + n], ALU.mult, [st["qn"], cosb], [t1])
                    P.tt("dve", t2[:, 0:n], pq[:, 0:n], sinb[:, t0:t0 + n], ALU.mult, [pq, sinb], [t2])
                    P.tt("pool", it["out"], t1[:, 0:n], t2[:, 0:n], ALU.add, [t1, t2], it["outw"])
                if it.get("done"):
                    it["done"]()
            for idx in range(N + 3):
                if idx < N:
                    A(idx)
                if 0 <= idx - 1 < N:
                    B(idx - 1)
                if 0 <= idx - 2 < N:
                    Cc(idx - 2)
                if 0 <= idx - 3 < N:
                    C(idx - 3)

        tiles_all = a_tiles(True)
        items = []
        for h in range(2):
            OB = OBr.next()
            for ti, (a0, n, is_lat, t0) in enumerate(tiles_all):
                it = dict(n=n, a0=a0, lat=is_lat, t0=t0, wfn=(lambda kc, h=h: wk[:, kc, h * 128:(h + 1) * 128]), wdep=wk,
                          g=pvc(l, 137), gdep=pv, out=OB[:, a0:a0 + n], outw=[(OB, a0)])
                if ti == len(tiles_all) - 1:
                    it["done"] = (lambda h=h, OB=OB: P.dma("sp", KS[h, :, :], OB[:, :], r=[OB], w=[(KS, h)]))
                items.append(it)
        run_items(items)
        for tt_ in range(34):
            ps = PS[tt_ % 4]
            for kc in range(8):
                P.mm(ps[:, 0:256], HT[:, kc, tt_ * 128:(tt_ + 1) * 128], wv_[:, kc, :], kc == 0, kc == 7, [wv_], [ps])
            if tt_ % 2:
                P.cp("act", VB[:, tt_, :], ps[:, 0:256], [ps], [(VB, tt_)])
            else:
                P.cp("dve", VB[:, tt_, :], ps[:, 0:256], [ps], [(VB, tt_)])
        P.dma("sp", VS[:, :, :], VB[:, :, :], r=[VB], w=[VS])
        tiles_q = a_tiles(do_ctx)
        c0 = 0 if do_ctx else 256
        items = []
        for h in range(8):
            OB = OBr.next() if h % 2 == 0 else OBr.bufs[1]
            OB = OBr.bufs[h % 2]
            wb = wq[h // 4]
            for ti, (a0, n, is_lat, t0) in enumerate(tiles_q):
                it = dict(n=n, a0=a0, lat=is_lat, t0=t0, wfn=(lambda kc, h=h, wb=wb: wb[:, kc, (h % 4) * 128:(h % 4 + 1) * 128]), wdep=wb,
                          g=QG[:, l:l + 1], gdep=QG, out=OB[:, a0:a0 + n], outw=[(OB, a0)])
                if ti == len(tiles_q) - 1:
                    it["done"] = (lambda h=h, OB=OB: P.dma("sp", QS[h, :, c0:NA], OB[:, c0:NA], r=[OB], w=[(QS, h)]))
                items.append(it)
        run_items(items)

    def phase_att(l, do_ctx):
        P.phase()
        KT = P.sb("KT", [128, 2, NA], BF16); V = P.sb("V", [128, 34, 256], BF16)
        P.dma("sp", KT[:, :, :], KS.h.rearrange("h p c -> p h c"), w=[KT])
        P.dma("sp", V[:, :, :], VS[:, :, :], w=[V])
        qr = Ring([P.sb("qb%d" % i, [128, 4, 128], BF16) for i in range(3)])
        ptr = Ring([P.sb("pt%d" % i, [128, 512], BF16) for i in range(10)])
        accr = {e: Ring([P.sb("acc%s%d" % (e, i), [128, 512], F32) for i in range(2)]) for e in ("dve", "pool")}
        rdr = Ring([P.sb("rd%d" % i, [128, 512], F32) for i in range(2)])
        obr = Ring([P.sb("ob%d" % i, [128, 4, 128], BF16) for i in range(2)])
        QSv = QS.h.rearrange("h p c -> p h c"); ATv = AT.h.rearrange("h p c -> p h c")
        blocks = []
        for g in range(2):
            if do_ctx:
                blocks += [(g, 0, 2), (g, 128, 2)]
            blocks += [(g, 256 + 128 * qb, 34) for qb in range(32)]
        qbufs = {}

        def qload(i):
            g, c0, nk = blocks[i]
            qb = qr.next()
            P.dma("sp", qb[:, :, :], QSv[:, 4 * g:4 * g + 4, c0:c0 + 128], w=[qb])
            qbufs[i] = qb
        qload(0); qload(1)
        si = 0

        def finalize(pO, pD, acc, used, g, c0, bi):
            for ui, e in enumerate(used):
                P.mm(pD[:, :], ones_f[:, :], acc[e][:, :], False, ui == len(used) - 1, [ones_f, acc[e]], [pD])
            rd = rdr.next(); ob = obr.next()
            P.recip(rd[:, :], pD[:, :], [pD], [rd])
            P.tt("dve", ob[:, :, :].rearrange("p h c -> p (h c)"), pO[:, :], rd[:, :], ALU.mult, [pO, rd], [ob])
            P.dma("sp", ATv[:, 4 * g:4 * g + 4, c0:c0 + 128], ob[:, :, :], r=[ob], w=[(AT, bi)])
        fin_prev = None
        fin_next = None
        for bi, (g, c0, nk) in enumerate(blocks):
            fin_prev = fin_next
            if bi + 2 < len(blocks):
                qload(bi + 2)
            qb = qbufs.pop(bi)
            qflat = qb[:, :, :].rearrange("p h c -> p (h c)")
            pO = PS[3 + bi % 2]; pD = PS[5 + bi % 2]
            acc = {"dve": accr["dve"].next(), "pool": accr["pool"].next()}
            inited = {"dve": False, "pool": False}
            plan = {}
            j = 0
            for k0 in range(nk):
                if k0 % 4 == 0:
                    plan[k0] = "pe"
                else:
                    plan[k0] = "pool" if j % 2 == 0 else "dve"
                    j += 1
            n_acc = len(set(v for v in plan.values() if v != "pe"))
            last_pe = max(k for k, v in plan.items() if v == "pe")
            pts = {}
            for kc in range(nk + 2):
                if kc == min(6, nk + 1) and fin_prev is not None:
                    finalize(*fin_prev)
                    fin_prev = None
                if kc < nk:
                    pS = PS[si % 3]; si += 1
                    P.mm(pS[:, :], KT[:, g, kc * 128:(kc + 1) * 128], qflat, True, True, [KT, qb], [pS])
                    pt = ptr.next()
                    P.actv(pt[:, :], pS[:, :], AF.Exp, [pS], [pt])
                    pts[kc] = pt
                k0 = kc - 2
                if k0 >= 0:
                    pt0 = pts.pop(k0)
                    P.mm(pO[:, :], V[:, k0, g * 128:(g + 1) * 128], pt0[:, :], k0 == 0, k0 == nk - 1, [V, pt0], [pO])
                    e = plan[k0]
                    if e == "pe":
                        P.mm(pD[:, :], ones_b[:, :], pt0[:, :], k0 == 0, (n_acc == 0 and k0 == last_pe), [ones_b, pt0], [pD])
                    else:
                        ac = acc[e]
                        if not inited[e]:
                            P.cp(e, ac[:, :], pt0[:, :], [pt0], [ac])
                            inited[e] = True
                        else:
                            P.tt(e, ac[:, :], ac[:, :], pt0[:, :], ALU.add, [ac, pt0], [ac])
            used = [e for e in ("dve", "pool") if inited[e]]
            fin_next = (pO, pD, acc, used, g, c0, bi)
            if bi == len(blocks) - 1:
                finalize(*fin_next)

    def residual_tile(l, who, s_rows, pso_fn, Grep, xsrc, last_final, fgrep, sm, xr, xor_, tmr, sqj):
        if who == 0:
            xt = xr.next()
            P.dma("sp", xt[:, :], xsrc[s_rows:s_rows + 128, :], r=[(xsrc, s_rows)], w=[xt])
            xin, xdep = xt, xt
            xo = xor_.next()
            xo_ap = lambda ch: xo[:, ch * 512:(ch + 1) * 512]
            xin_ap = lambda ch: xt[:, ch * 512:(ch + 1) * 512]
            xow = [xo]
        else:
            tt_ = s_rows // 128
            xo = None
            xo_ap = lambda ch: XC[:, tt_, ch * 512:(ch + 1) * 512]
            xin_ap = xo_ap
            xdep = XC
            xow = [XC]
        for ch in range(2):
            pso = pso_fn(ch)
            tm = tmr.next()
            P.tt("dve", tm[:, :], pso[:, :], Grep[:, ch * 512:(ch + 1) * 512], ALU.mult, [pso, Grep], [tm])
            P.tt("pool", xo_ap(ch), tm[:, :], xin_ap(ch), ALU.add, [tm, xdep], xow)
        if who == 0:
            if last_final:
                ss = sm.next(); sd = sm.next(); rs = sm.next()
                P.actv(sqj[:, :], xo[:, :], AF.Square, [xo], [sqj, ss], accum_out=ss[:, 0:1])
                P.actv(sd[:, :], ss[:, :], AF.Sqrt, [ss, eps_t], [sd], scale=1.0 / DM, bias=eps_t[:, 0:1])
                P.recip(rs[:, :], sd[:, :], [sd], [rs])
                yo = xr.next()
                P.stt(yo[:, :], xo[:, :], rs[:, 0:1], fgrep[:, :], ALU.mult, ALU.mult, [xo, rs, fgrep], [yo])
                P.dma("act", y_d[s_rows:s_rows + 128, :], yo[:, :], r=[yo], w=[(y_d, s_rows)])
            else:
                P.dma("act", XS[s_rows:s_rows + 128, :], xo[:, :], r=[xo], w=[(XS, s_rows)])

    def phase_merge(l, do_ctx):
        P.phase()
        Wt = {}
        for nm, d in (("a", wao_d), ("s", wso_d), ("l", wlo_d), ("m", wmo_d)):
            Wt[nm] = P.sb("W" + nm, [128, 8, DM], BF16)
        wviews = {nm: d.h.rearrange("l (kc p) n -> l p kc n", p=128) for nm, d in (("a", wao_d), ("s", wso_d), ("l", wlo_d), ("m", wmo_d))}
        for j4 in range(4):
            for nm in ("a", "s", "l"):
                P.dma("pool", Wt[nm][:, :, j4 * 256:(j4 + 1) * 256], wviews[nm][l, :, :, j4 * 256:(j4 + 1) * 256], w=[(Wt[nm], j4)])
        for hh in range(2):
            P.dma("pool", Wt["m"][:, :, hh * 512:(hh + 1) * 512], wviews["m"][l, :, :, hh * 512:(hh + 1) * 512], w=[(Wt["m"], hh)])
        G1 = [P.sb("G1_%d" % w_, [128, DM], F32) for w_ in range(2)]
        P.dma("sp", G1[0][:, :], GRD[l * 4 + 0, :, :], w=[G1[0]])
        if do_ctx:
            P.dma("sp", G1[1][:, :], GRD[l * 4 + 2, :, :], w=[G1[1]])
        inr = {k: Ring([P.sb("in%s%d" % (k, i), [128, 8, 512], BF16) for i in range(2)]) for k in ("a", "s", "l")}
        gjr = Ring([P.sb("gj%d" % i, [128, 3, 512], BF16) for i in range(3)])
        mr = {k: Ring([P.sb("m%s%d" % (k, i), [128, 512], F32) for i in range(2)]) for k in ("1", "2", "3", "12")}
        mgr = Ring([P.sb("mg%d" % i, [128, 8, 512], BF16) for i in range(2)])
        xr = Ring([P.sb("xt%d" % i, [128, DM], F32) for i in range(3)])
        xor_ = Ring([P.sb("xo%d" % i, [128, DM], F32) for i in range(2)])
        tmr = Ring([P.sb("tm%d" % i, [128, 512], F32) for i in range(2)])
        sm = Ring([P.sb("sm%d" % i, [128, 1], F32) for i in range(6)])
        srcv = {"a": AT.h.rearrange("h p c -> p h c"), "s": SC.h.rearrange("h p c -> p h c"), "l": LR.h.rearrange("h p c -> p h c")}
        GSv = GS.h.rearrange("(b j) p c -> j p b c", b=3)
        xsrc = x_d if l == 0 else XS
        tiles = a_tiles(do_ctx)
        ins = {}

        def tload(i):
            a0, n, is_lat, t0 = tiles[i]
            d = {}
            for k in ("a", "s", "l"):
                b = inr[k].next()
                P.dma("sp", b[:, :, 0:n], srcv[k][:, :, a0:a0 + n], w=[b])
                d[k] = b
            ins[i] = d
        tload(0)
        oi = 0

        def out_stage(mg, n, is_lat, t0):
            who = 0 if is_lat else 1
            for s in range(n // 128):
                def pso_fn(ch, s=s):
                    nonlocal oi
                    ps = PS[6 + oi % 2]; oi += 1
                    for kc in range(8):
                        P.mm(ps[:, :], mg[:, kc, s * 128:(s + 1) * 128], Wt["m"][:, kc, ch * 512:(ch + 1) * 512], kc == 0, kc == 7, [mg, (Wt["m"], ch)], [ps])
                    return ps
                rows = (t0 + s * 128) if is_lat else s * 128
                residual_tile(l, who, rows, pso_fn, G1[who], xsrc, False, None, sm, xr, xor_, tmr, None)
        pending = None
        for ti, (a0, n, is_lat, t0) in enumerate(tiles):
            if ti + 1 < len(tiles):
                tload(ti + 1)
            cur = ins.pop(ti)
            mg = mgr.next()
            for j in range(8):
                gj = gjr.next()
                P.dma("sp", gj[:, :, 0:n], GSv[j, :, :, a0:a0 + n], w=[gj])
                pss = {}
                for bi_, k in enumerate(("a", "s", "l")):
                    ps = PS[(j % 2) * 3 + bi_]
                    for kc in range(8):
                        P.mm(ps[:, 0:n], Wt[k][:, kc, j * 128:(j + 1) * 128], cur[k][:, kc, 0:n], kc == 0, kc == 7, [(Wt[k], j // 2), cur[k]], [ps])
                    pss[k] = ps
                m1 = mr["1"].next(); m2 = mr["2"].next(); m3 = mr["3"].next(); m12 = mr["12"].next()
                P.tt("dve", m1[:, 0:n], pss["a"][:, 0:n], gj[:, 0, 0:n], ALU.mult, [pss["a"], gj], [m1])
                P.tt("dve", m2[:, 0:n], pss["s"][:, 0:n], gj[:, 1, 0:n], ALU.mult, [pss["s"], gj], [m2])
                P.tt("dve", m3[:, 0:n], pss["l"][:, 0:n], gj[:, 2, 0:n], ALU.mult, [pss["l"], gj], [m3])
                P.tt("pool", m12[:, 0:n], m1[:, 0:n], m2[:, 0:n], ALU.add, [m1, m2], [m12])
                P.tt("pool", mg[:, j, 0:n], m12[:, 0:n], m3[:, 0:n], ALU.add, [m12, m3], [(mg, j)])
            if pending is not None:
                out_stage(*pending)
            pending = (mg, n, is_lat, t0)
        out_stage(*pending)

    def phase_ffn1(l, do_ctx):
        P.phase()
        HT = keep_ht()
        tiles = a_tiles(do_ctx)
        wb_ = [P.sb("wf%d" % i, [128, 2, 8, 128], BF16) for i in range(3)]
        OBr = Ring([P.sb("OBf%d" % i, [128, NA], BF16) for i in range(2)])
        sgr = Ring([P.sb("sg%d" % i, [128, 512], F32) for i in range(3)])
        wv = wfi_d.h.rearrange("l (kc p) n -> l p kc n", p=128)

        def load(j):
            wb = wb_[j % 3]
            P.dma("pool", wb[:, 0, :, :], wv[l, :, :, j * 128:(j + 1) * 128], w=[wb])
            P.dma("pool", wb[:, 1, :, :], wv[l, :, :, FFW + j * 128:FFW + (j + 1) * 128], w=[wb])
            return wb
        loaded = {0: load(0), 1: load(1)}
        pi = 0
        for j in range(22):
            wb = loaded.pop(j)
            if j + 2 < 22:
                loaded[j + 2] = load(j + 2)
            OB = OBr.next()
            for (a0, n, is_lat, t0) in tiles:
                pg = PS[pi % 8]; pu = PS[(pi + 1) % 8]; pi += 2
                group(pg, n, lambda kc: wb[:, 0, kc, :], wb, HT, a0)
                group(pu, n, lambda kc: wb[:, 1, kc, :], wb, HT, a0)
                sg = sgr.next()
                P.actv(sg[:, 0:n], pg[:, 0:n], AF.Silu, [pg], [sg])
                P.tt("dve", OB[:, a0:a0 + n], pu[:, 0:n], sg[:, 0:n], ALU.mult, [pu, sg], [(OB, a0)])
            c0 = 0 if do_ctx else 256
            P.dma("sp", FA[j, :, c0:NA], OB[:, c0:NA], r=[OB], w=[(FA, j)])

    def phase_ffn2(l, do_ctx, final):
        P.phase()
        WO = P.sb("WO", [128, 22, DM], BF16)
        vw = wfo_d.h.rearrange("l (kc p) n -> l p kc n", p=128)
        for q4 in range(4):
            P.dma("pool", WO[:, :, q4 * 256:(q4 + 1) * 256], vw[l, :, :, q4 * 256:(q4 + 1) * 256], w=[(WO, q4)])
        G2 = [P.sb("G2_%d" % w_, [128, DM], F32) for w_ in range(2)]
        P.dma("sp", G2[0][:, :], GRD[l * 4 + 1, :, :], w=[G2[0]])
        if do_ctx:
            P.dma("sp", G2[1][:, :], GRD[l * 4 + 3, :, :], w=[G2[1]])
        fgrep = None; sqj = None
        if final:
            fgrep = P.sb("fgrep", [128, DM], F32)
            P.dma("sp", fgrep[:, :], mk_ap(fg_d, 0, [[0, 128], [1, DM]]), w=[fgrep])
            sqj = P.sb("sqj", [128, DM], BF16)
        far = Ring([P.sb("fa%d" % i, [128, 22, 512], BF16) for i in range(2)])
        xr = Ring([P.sb("xt%d" % i, [128, DM], F32) for i in range(4)])
        xor_ = Ring([P.sb("xo%d" % i, [128, DM], F32) for i in range(2)])
        tmr = Ring([P.sb("tm%d" % i, [128, 512], F32) for i in range(2)])
        sm = Ring([P.sb("sm%d" % i, [128, 1], F32) for i in range(9)])
        FAv = FA.h.rearrange("j p c -> p j c")
        tiles = a_tiles(do_ctx)
        ins = {}

        def tload(i):
            a0, n, is_lat, t0 = tiles[i]
            b = far.next()
            P.dma("sp", b[:, :, 0:n], FAv[:, :, a0:a0 + n], w=[b])
            ins[i] = b
        tload(0)
        oi = 0
        for ti, (a0, n, is_lat, t0) in enumerate(tiles):
            if ti + 1 < len(tiles):
                tload(ti + 1)
            fa = ins.pop(ti)
            who = 0 if is_lat else 1
            for s in range(n // 128):
                def pso_fn(ch, s=s):
                    nonlocal oi
                    ps = PS[oi % 8]; oi += 1
                    for kc in range(22):
                        P.mm(ps[:, :], fa[:, kc, s * 128:(s + 1) * 128], WO[:, kc, ch * 512:(ch + 1) * 512], kc == 0, kc == 21, [fa, (WO, 2 * ch), (WO, 2 * ch + 1)], [ps])
                    return ps
                rows = (t0 + s * 128) if is_lat else s * 128
                residual_tile(l, who, rows, pso_fn, G2[who], XS, final, fgrep, sm, xr, xor_, tmr, sqj)

    def dump_ht():
        P.phase()
        HT = keep_ht()
        P.dma("sp", HTD[:, :, :], HT[:, :, :], w=[HTD])
        P.dma("sp", XCD[:, :, :], XC[:, :, :], w=[XCD])

    def run_all():
        if stop('const'):
            P.phase()
            P.dma('sp', GRD[0, :, 0:128], ident[:, :], w=[GRD])
            return
        for l in range(nlayers):
            last = l == 1
            do_ctx = not last
            phase_mod(l)
            if stop("mod%d" % l): return
            phase_norm(l, 1, x_d if l == 0 else XS, True)
            if stop("n1_%d" % l): dump_ht(); return
            phase_sconv_gates(l, do_ctx)
            if stop("sc%d" % l): return
            phase_lru(l, do_ctx)
            if stop("lru%d" % l): return
            phase_qkv(l, do_ctx)
            if stop("qkv%d" % l): return
            phase_att(l, do_ctx)
            if stop("att%d" % l): return
            phase_merge(l, do_ctx)
            if stop("mg%d" % l): dump_ht(); return
            phase_norm(l, 2, XS, do_ctx)
            if stop("n2_%d" % l): dump_ht(); return
            phase_ffn1(l, do_ctx)
            if stop("f1_%d" % l): return
            phase_ffn2(l, do_ctx, last)
            if stop("f2_%d" % l): dump_ht(); return
    run_all()
    nops = {e: len(P.ops[e]) for e in ENGS}
    nc = P.finish()
    return nc, nops


def _rope_tables():
    quarter = 32
    inv_freq = (10000.0 ** (-np.arange(quarter, dtype=np.float32) / quarter)).astype(np.float32)
    t = np.arange(SEQ)
    row = (t // 64).astype(np.float32); colp = (t % 64).astype(np.float32)
    cos = np.zeros((128, SEQ), np.float32); sin = np.zeros((128, SEQ), np.float32)
    for d in range(128):
        pos = row if d < 64 else colp
        j = d % 64
        i = j % 32
        ang = (pos * inv_freq[i]).astype(np.float32)
        cos[d] = np.cos(ang)
        sin[d] = np.sin(ang) * (-1.0 if j < 32 else 1.0)
    return cos, sin


def _consts():
    cm = np.zeros((128, 256), np.float32)
    cm[:, :128] = np.eye(128, dtype=np.float32)
    for m in range(128):
        j = m % 64
        k = m + 32 if j < 32 else m - 32
        cm[k, 128 + m] = 1.0
    return cm


def _fm(v):
    v = np.asarray(v, np.float32).reshape(-1, 8, 128)
    return np.ascontiguousarray(v.transpose(2, 0, 1).reshape(128, -1))


def make_in_maps(inp):
    f = lambda a: np.ascontiguousarray(np.asarray(a, np.float32))
    pvs = []
    for l in range(2):
        cols = [_fm(inp["norm1_g"][l]), _fm(inp["norm2_g"][l]), _fm(inp["sconv_w"][l]), _fm(inp["sconv_b"][l]),
                _fm(inp["lru_conv_w"][l]), _fm(inp["lru_conv_b"][l]), _fm(inp["lru_ba"][l]), _fm(inp["lru_bx"][l]),
                _fm(inp["lru_lambda"][l]), f(inp["q_norm_g"][l]).reshape(128, 1), f(inp["k_norm_g"][l]).reshape(128, 1)]
        pvs.append(np.concatenate(cols, axis=1))
        assert pvs[-1].shape[1] == PVL
    pv = np.ascontiguousarray(np.concatenate(pvs + [_fm(inp["final_g"])], axis=1))
    cos, sin = _rope_tables()
    shared = {
        "pv": pv, "bmod": f(inp["b_mod"]), "fgrow": f(inp["final_g"]).reshape(1, DM), "cost": cos, "sint": sin, "cm": _consts(),
        "w_mod": f(inp["w_mod"]), "w_in": f(inp["w_in"]), "w_ao": f(inp["w_attn_out"]), "w_so": f(inp["w_sconv_out"]),
        "w_lo": f(inp["w_lru_out"]), "w_mo": f(inp["w_merge_out"]), "w_fi": f(inp["w_ffn_in"]), "w_fo": f(inp["w_ffn_out"]),
        "lwa": f(inp["lru_wa"]), "lwx": f(inp["lru_wx"]),
    }
    maps = []
    ccx = _fm(inp["c_ctx"])
    for b in range(8):
        m = dict(shared)
        m["x"] = f(inp["x"][b]); m["ctx"] = f(inp["ctx"][b])
        m["cc"] = np.ascontiguousarray(np.concatenate([_fm(inp["c"][b]), ccx], axis=1))
        maps.append(m)
    return maps


_CACHE = {}


def kernel(**inputs):
    if "nc" not in _CACHE:
        _CACHE["nc"] = build()[0]
    nc = _CACHE["nc"]
    maps = make_in_maps(inputs)
    res = run_bass_kernel_spmd(nc, maps, core_ids=list(range(8)))
    return np.stack([np.asarray(r["y"], np.float32) for r in res.results], axis=0)
```

```python
import contextlib
import numpy as np
import concourse.bass as bass
import concourse.mybir as mybir
from concourse.ap import AP
from concourse.bass_utils import run_bass_kernel_spmd

F32 = mybir.dt.float32
BF16 = mybir.dt.bfloat16
ALU = mybir.AluOpType
AF = mybir.ActivationFunctionType

ENGS = ("pe", "act", "dve", "pool", "sp")
N_DMA_SEMS = 48
SB_LO = 16512
SB_HI = 229376 - 4096


class T:
    __slots__ = ("w", "r")

    def __init__(self):
        self.w = None
        self.r = {}


class Buf:
    def __init__(self, name, handle, space):
        self.name = name
        self.h = handle
        self.space = space
        self.whole = T()
        self.cells = {}
        self.epoch = -1

    def __getitem__(self, k):
        return self.h[k]


class Op:
    __slots__ = ("eng", "fn", "deps", "need_inc", "val", "dma")

    def __init__(self, eng, fn, deps, dma=None):
        self.eng = eng
        self.fn = fn
        self.deps = deps
        self.need_inc = False
        self.val = None
        self.dma = dma


class Prog:
    def __init__(self):
        self.nc = bass.Bass("TRN2", target_bir_lowering=False)
        self.stack = contextlib.ExitStack()
        self.ops = {e: [] for e in ENGS}
        self.dma_nextq = {}
        self.top_cache = {}
        self.dma_cnt = [0] * N_DMA_SEMS
        self.dma_last_ev = [None] * N_DMA_SEMS
        self.epoch = 0
        self.persist_ptr = SB_LO
        self.ptr = SB_LO
        self.top = SB_HI
        self.nalloc = 0

    def dram(self, name, shape, dtype, kind):
        return Buf(name, self.nc.dram_tensor(name, list(shape), dtype, kind=kind), "dram")

    @staticmethod
    def _bytes(shape, dtype):
        n = 1
        for s in shape[1:]:
            n *= s
        return n * (4 if dtype == F32 else 2)

    def _at(self, name, shape, dtype, off):
        self.nalloc += 1
        h = self.nc.alloc_sbuf_tensor_at("%s_%d" % (name, self.nalloc), list(shape), dtype, offset=off)
        return Buf(name, h, "sbuf")

    def persist(self, name, shape, dtype):
        assert self.ptr == self.persist_ptr
        nb = (self._bytes(shape, dtype) + 63) // 64 * 64
        b = self._at(name, shape, dtype, self.persist_ptr)
        self.persist_ptr += nb
        self.ptr = self.persist_ptr
        return b

    def sb(self, name, shape, dtype):
        nb = (self._bytes(shape, dtype) + 63) // 64 * 64
        assert self.ptr + nb <= self.top, "SBUF overflow at %s: %d + %d > %d" % (name, self.ptr, nb, self.top)
        b = self._at(name, shape, dtype, self.ptr)
        self.ptr += nb
        return b

    def sb_top(self, name, shape, dtype):
        nb = (self._bytes(shape, dtype) + 63) // 64 * 64
        self.top -= nb
        assert self.top >= self.ptr
        key = (name, tuple(shape), str(dtype), self.top)
        if key not in self.top_cache:
            self.top_cache[key] = self._at(name, shape, dtype, self.top)
        return self.top_cache[key]

    def psum(self, name, shape, dtype):
        h = self.stack.enter_context(self.nc.psum_tensor(name, list(shape), dtype))
        return Buf(name, h, "psum")

    def _norm(self, x):
        if isinstance(x, Buf):
            b, key = x, None
        else:
            b, key = x
        if b.epoch != self.epoch:
            b.epoch = self.epoch
            b.whole = T()
            b.cells = {}
        return b, key

    def _collect(self, reads, writes):
        deps = {}

        def add(ev):
            if ev is not None and deps.get(ev[0], -1) < ev[1]:
                deps[ev[0]] = ev[1]

        def add_r(t):
            for k, v in t.r.items():
                if deps.get(k, -1) < v:
                    deps[k] = v

        for x in reads:
            b, key = self._norm(x)
            add(b.whole.w)
            if key is None:
                for t in b.cells.values():
                    add(t.w)
            else:
                t = b.cells.get(key)
                if t is not None:
                    add(t.w)
        for x in writes:
            b, key = self._norm(x)
            add(b.whole.w)
            add_r(b.whole)
            if key is None:
                for t in b.cells.values():
                    add(t.w)
                    add_r(t)
            else:
                t = b.cells.get(key)
                if t is not None:
                    add(t.w)
                    add_r(t)
        return deps

    def _record(self, reads, writes, ev):
        k, v = ev
        for x in reads:
            b, key = self._norm(x)
            t = b.whole if key is None else b.cells.setdefault(key, T())
            if t.r.get(k, -1) < v:
                t.r[k] = v
        for x in writes:
            b, key = self._norm(x)
            if key is None:
                b.whole.w = ev
                b.whole.r = {}
                b.cells = {}
            else:
                t = b.cells.setdefault(key, T())
                t.w = ev
                t.r = {}

    def op(self, eng, fn, r=(), w=(), extra=()):
        deps = self._collect(r, w)
        for k, v in extra:
            if deps.get(k, -1) < v:
                deps[k] = v
        lst = self.ops[eng]
        if eng == "pe":
            deps.pop("pe", None)
        o = Op(eng, fn, list(deps.items()))
        self._record(r, w, (eng, len(lst)))
        lst.append(o)
        return o

    def dma(self, q, out, in_, r=(), w=(), **kw):
        deps = self._collect(r, w)
        lo, hi = {"sp": (0, 24), "act": (24, 36)}.get(q, (36, N_DMA_SEMS))
        i = self.dma_nextq.get(q, lo)
        self.dma_nextq[q] = lo + (i + 1 - lo) % (hi - lo)
        prev = self.dma_last_ev[i]
        if prev is not None and deps.get(prev[0], -1) < prev[1]:
            deps[prev[0]] = prev[1]
        self.dma_cnt[i] += 1
        val = 16 * self.dma_cnt[i]
        ev = (("dma", i), val)
        self.dma_last_ev[i] = ev
        fn = (lambda e: e.dma_start(out=out, in_=in_, **kw))
        o = Op(q, fn, list(deps.items()), dma=(i, val))
        self.ops[q].append(o)
        self._record(r, w, ev)
        return o

    def phase(self):
        B = Buf("bar", None, "none")
        for e in ENGS:
            if e == "sp":
                extra = [ev for ev in self.dma_last_ev if ev is not None]
                self.op(e, lambda eng: eng.nop(), w=[(B, e)], extra=extra)
            else:
                self.op(e, lambda eng: eng.drain(), w=[(B, e)])
        for e in ENGS:
            self.op(e, lambda eng: eng.nop(), r=[B])
        self.epoch += 1
        self.ptr = self.persist_ptr
        self.top = SB_HI

    def mm(self, out, lhsT, rhs, start, stop, r, w):
        return self.op("pe", lambda e: e.matmul(out, lhsT=lhsT, rhs=rhs, start=start, stop=stop), r, w)

    def tr(self, out, in_, ident, r, w):
        return self.op("pe", lambda e: e.transpose(out=out, in_=in_, identity=ident), r, w)

    def actv(self, out, in_, func, r, w, scale=None, bias=None, accum_out=None):
        kw = {}
        if scale is not None:
            kw["scale"] = scale
        if bias is not None:
            kw["bias"] = bias
        if accum_out is not None:
            kw["accum_out"] = accum_out
        return self.op("act", lambda e: e.activation(out=out, in_=in_, func=func, **kw), r, w)

    def tt(self, eng, out, in0, in1, op, r, w):
        return self.op(eng, lambda e: e.tensor_tensor(out=out, in0=in0, in1=in1, op=op), r, w)

    def ts(self, eng, out, in0, s1, s2, op0, op1, r, w):
        if s2 is None:
            return self.op(eng, lambda e: e.tensor_scalar(out=out, in0=in0, scalar1=s1, scalar2=None, op0=op0), r, w)
        return self.op(eng, lambda e: e.tensor_scalar(out=out, in0=in0, scalar1=s1, scalar2=s2, op0=op0, op1=op1), r, w)

    def stt(self, out, in0, scalar, in1, op0, op1, r, w, accum_out=None):
        if accum_out is not None:
            return self.op("dve", lambda e: e.scalar_tensor_tensor(out=out, in0=in0, scalar=scalar, in1=in1, op0=op0,
                                                                   op1=op1, accum_out=accum_out), r, w)
        return self.op("dve", lambda e: e.scalar_tensor_tensor(out=out, in0=in0, scalar=scalar, in1=in1, op0=op0, op1=op1), r, w)

    def scan(self, out, d0, d1, init, r, w):
        return self.op("dve", lambda e: e.tensor_tensor_scan(out=out, data0=d0, data1=d1, initial=init,
                                                             op0=ALU.mult, op1=ALU.add), r, w)

    def cp(self, eng, out, in_, r, w):
        if eng == "act":
            return self.op("act", lambda e: e.activation(out=out, in_=in_, func=AF.Copy), r, w)
        return self.op(eng, lambda e: e.tensor_copy(out=out, in_=in_), r, w)

    def recip(self, out, in_, r, w):
        return self.op("dve", lambda e: e.reciprocal(out=out, in_=in_), r, w)

    def memset(self, eng, ap, val, w):
        return self.op(eng, lambda e: e.memset(ap, val), (), w)

    def finish(self):
        nc = self.nc
        for e in ENGS:
            for o in self.ops[e]:
                for k, v in o.deps:
                    if isinstance(k, str):
                        self.ops[k][v].need_inc = True
        for e in ENGS:
            c = 0
            for o in self.ops[e]:
                if o.dma is None and o.need_inc:
                    c += 1
                    o.val = c
        sems = {e: self.stack.enter_context(nc.semaphore("s_" + e)) for e in ENGS}
        dsems = [self.stack.enter_context(nc.semaphore("s_dma%d" % i)) for i in range(N_DMA_SEMS)]
        final_waits = [(("dma", i), 16 * self.dma_cnt[i]) for i in range(N_DMA_SEMS) if self.dma_cnt[i]]

        def emit(ename, eobj):
            seen = {}
            for o in self.ops[ename]:
                for k, v in o.deps:
                    if isinstance(k, str):
                        val = self.ops[k][v].val
                        sem = sems[k]
                    else:
                        val = v
                        sem = dsems[k[1]]
                    if seen.get(k, -1) >= val:
                        continue
                    seen[k] = val
                    eobj.wait_ge(sem, val)
                ins = o.fn(eobj)
                if o.dma is not None:
                    ins.then_inc(dsems[o.dma[0]], 16)
                elif o.need_inc:
                    ins.then_inc(sems[ename], 1)
            if ename == "sp":
                for k, val in final_waits:
                    if seen.get(k, -1) < val:
                        eobj.wait_ge(dsems[k[1]], val)

        with nc.Block() as block:
            @block.tensor
            def _(eng):
                emit("pe", eng)

            @block.scalar
            def _(eng):
                emit("act", eng)

            @block.vector
            def _(eng):
                emit("dve", eng)

            @block.gpsimd
            def _(eng):
                emit("pool", eng)

            @block.sync
            def _(eng):
                emit("sp", eng)
        self.stack.close()
        return nc


def mk_ap(buf, offset, pattern):
    return AP(buf.h, offset, [list(p) for p in pattern])


def bcast_col(col_ap, n):
    return AP(col_ap.tensor, col_ap.offset, [list(col_ap.ap[0]), [0, n]])


class Ring:
    def __init__(self, bufs):
        self.bufs = bufs
        self.i = 0

    def next(self):
        b = self.bufs[self.i % len(self.bufs)]
        self.i += 1
        return b


DM = 1024
SEQ = 4096
NCTX = 256
NA = SEQ + NCTX
NIN = 9728
FFW = 2816
PVL = 138
NPV = 2 * PVL + 8
C_Q, C_K, C_V, C_SB, C_SC, C_SX, C_RX, C_RG, C_GT = 0, 1024, 1280, 1536, 2560, 3584, 4608, 5632, 6656
EPS = 1e-6


def a_tiles(do_ctx):
    tl = []
    if do_ctx:
        tl.append((0, 256, False, 0))
    for i in range(8):
        tl.append((256 + 512 * i, 512, True, 512 * i))
    return tl


def build(nlayers=2, stop_after=None, dbg=()):
    P = Prog()
    kin = "ExternalInput"

    def DI(name, shape, dt=F32):
        return P.dram(name, shape, dt, kin)

    def DS(name, shape, dt):
        return P.dram(name, shape, dt, "ExternalOutput" if name in dbg else "Internal")

    x_d = DI("x", [SEQ, DM]); ctx_d = DI("ctx", [NCTX, DM]); cc_d = DI("cc", [128, 16]); pv_d = DI("pv", [128, NPV])
    bmod_d = DI("bmod", [2, 6 * DM]); fg_d = DI("fgrow", [1, DM]); cos_d = DI("cost", [128, SEQ]); sin_d = DI("sint", [128, SEQ])
    cm_d = DI("cm", [128, 256])
    wmod_d = DI("w_mod", [2, DM, 6 * DM]); win_d = DI("w_in", [2, DM, NIN])
    wao_d = DI("w_ao", [2, DM, DM]); wso_d = DI("w_so", [2, DM, DM]); wlo_d = DI("w_lo", [2, DM, DM]); wmo_d = DI("w_mo", [2, DM, DM])
    wfi_d = DI("w_fi", [2, DM, 2 * FFW]); wfo_d = DI("w_fo", [2, FFW, DM])
    lwa_d = DI("lwa", [2, 2, 8, 128, 128]); lwx_d = DI("lwx", [2, 2, 8, 128, 128])
    y_d = P.dram("y", [SEQ, DM], F32, "ExternalOutput")

    XS = DS("XS", [SEQ, DM], F32)
    QS = DS("QS", [8, 128, NA], BF16); KS = DS("KS", [2, 128, NA], BF16); VS = DS("VS", [128, 34, 256], BF16)
    AT = DS("ATs", [8, 128, NA], BF16); SC = DS("SCs", [8, 128, NA], BF16); LR = DS("LRs", [8, 128, NA], BF16)
    GS = DS("GSs", [24, 128, NA], BF16); FA = DS("FAs", [22, 128, NA], BF16); GRD = DS("GRD", [8, 128, DM], F32)
    HTD = DS("HTD", [128, 8, NA], BF16)
    XCD = DS("XCD", [128, 2, DM], F32)

    PS = [P.psum("ps%d" % i, [128, 512], F32) for i in range(8)]

    ident = P.persist("ident", [128, 128], F32); perm = P.persist("perm", [128, 128], F32)
    ones_f = P.persist("ones_f", [128, 128], F32); ones_b = P.persist("ones_b", [128, 128], BF16)
    eps_t = P.persist("eps_t", [128, 1], F32); one_t = P.persist("one_t", [128, 1], F32)
    perm_b = P.persist("perm_b", [128, 128], BF16)
    pv = P.persist("pv", [128, NPV], F32); cc = P.persist("cc", [128, 16], F32); scc = P.persist("scc", [128, 16], F32)
    XC = P.persist("XC", [128, 2, DM], F32)
    MV = P.persist("MV", [128, 2 * 2 * 4 * 8], F32)
    LV = P.persist("LV", [128, 2 * 3 * 16], F32)
    QG = P.persist("QG", [128, 2], F32)

    def mvc(l, who, vi, j0=0, n=8):
        o = ((l * 2 + who) * 4 + vi) * 8 + j0
        return MV[:, o:o + n]

    def lvc(l, which, d, n):
        o = (l * 3 + which) * 16 + d * 8 + n
        return LV[:, o:o + 1]

    def pvc(l, off, n=1):
        return pv[:, l * PVL + off:l * PVL + off + n]

    P.dma("sp", ident[:, :], cm_d[:, 0:128], w=[ident]); P.dma("sp", perm[:, :], cm_d[:, 128:256], w=[perm])
    P.dma("sp", pv[:, :], pv_d[:, :], w=[pv]); P.dma("sp", cc[:, :], cc_d[:, :], w=[cc])
    P.dma("sp", XC[:, :, :], ctx_d.h.rearrange("(t p) f -> p t f", p=128), w=[XC])
    P.memset("dve", ones_f[:, :], 1.0, [ones_f]); P.memset("dve", ones_b[:, :], 1.0, [ones_b])
    P.memset("dve", eps_t[:, :], EPS, [eps_t]); P.memset("dve", one_t[:, :], 1.0, [one_t])
    P.actv(scc[:, :], cc[:, :], AF.Silu, [cc], [scc])
    P.cp("dve", perm_b[:, :], perm[:, :], [perm], [perm_b])
    for l in range(2):
        P.ts("dve", QG[:, l:l + 1], pvc(l, 136), 128 ** -0.5, None, ALU.mult, None, [pv], [QG])
        P.ts("dve", LV[:, (l * 3 + 0) * 16:(l * 3 + 0) * 16 + 16], pvc(l, 88, 16), 0.5, None, ALU.mult, None, [pv], [LV])
        P.ts("dve", LV[:, (l * 3 + 1) * 16:(l * 3 + 1) * 16 + 16], pvc(l, 104, 16), 0.5, None, ALU.mult, None, [pv], [LV])
    tmpl = P.persist("tmpl", [128, 32], F32); tser = P.persist("tser", [128, 32], F32)
    ttmp = P.persist("ttmp", [128, 32], F32); tmsk = P.persist("tmsk", [128, 32], F32); tln = P.persist("tln", [128, 32], F32)
    for l in range(2):
        P.actv(tmpl[:, l * 16:l * 16 + 16], pvc(l, 120, 16), AF.Exp, [pv], [tmpl], scale=-1.0)
    P.actv(tln[:, :], tmpl[:, :], AF.Ln, [tmpl, one_t], [tln], bias=one_t[:, 0:1])
    P.memset("dve", tser[:, :], 1.0 / 8, [tser])
    for c in (1.0 / 7, 1.0 / 6, 1.0 / 5, 1.0 / 4, 1.0 / 3, 1.0 / 2, 1.0):
        P.tt("dve", ttmp[:, :], tser[:, :], tmpl[:, :], ALU.mult, [tser, tmpl], [ttmp])
        P.ts("dve", tser[:, :], ttmp[:, :], -1.0, c, ALU.mult, ALU.add, [ttmp], [tser])
    P.tt("dve", tser[:, :], tser[:, :], tmpl[:, :], ALU.mult, [tser, tmpl], [tser])
    P.ts("dve", tmsk[:, :], tmpl[:, :], 0.25, None, ALU.is_lt, None, [tmpl], [tmsk])
    P.tt("dve", ttmp[:, :], tser[:, :], tln[:, :], ALU.subtract, [tser, tln], [ttmp])
    P.tt("dve", ttmp[:, :], ttmp[:, :], tmsk[:, :], ALU.mult, [ttmp, tmsk], [ttmp])
    P.tt("dve", tln[:, :], tln[:, :], ttmp[:, :], ALU.add, [tln, ttmp], [tln])
    for l in range(2):
        P.ts("dve", LV[:, (l * 3 + 2) * 16:(l * 3 + 2) * 16 + 16], tln[:, l * 16:l * 16 + 16], -4.0, None, ALU.mult, None, [tln], [LV])

    win_v = win_d.h.rearrange("l (kc p) n -> l p kc n", p=128)

    def stop(tag):
        return stop_after == tag

    def phase_mod(l):
        P.phase()
        bm = P.sb("bm", [128, 6 * DM], F32)
        P.dma("sp", bm[:, :], mk_ap(bmod_d, l * 6 * DM, [[0, 128], [1, 6 * DM]]), w=[bm])
        wr = Ring([P.sb("wm%d" % i, [128, 8, 512], BF16) for i in range(3)])
        gr = Ring([P.sb("gt%d" % i, [128, 512], F32) for i in range(2)])
        tr_ = Ring([P.sb("tm%d" % i, [128, 512], F32) for i in range(2)])
        jr = Ring([P.sb("jk%d" % i, [128, 128], F32) for i in range(2)])
        wv = wmod_d.h.rearrange("l (kc p) n -> l p kc n", p=128)
        sccb = P.sb("sccb", [128, 16, 128], BF16)
        a0_ = scc[:, 0:16]
        P.cp("dve", sccb[:, :, :], AP(a0_.tensor, a0_.offset, [list(a0_.ap[0]), [1, 16], [0, 128]]), [scc], [sccb])
        for ti in range(12):
            v, hf = divmod(ti, 2)
            wt = wr.next()
            P.dma("pool", wt[:, :, :], wv[l, :, :, ti * 512:(ti + 1) * 512], w=[wt])
            for who in range(2):
                ps = PS[(ti * 2 + who) % 4]
                for kc in range(8):
                    P.mm(ps[:, :], sccb[:, who * 8 + kc, :], wt[:, kc, :], kc == 0, kc == 7, [sccb, wt], [ps])
                bsl = bm[:, ti * 512:(ti + 1) * 512]
                if v in (2, 5):
                    gt = gr.next()
                    P.tt("dve", gt[:, :], ps[:, :], bsl, ALU.add, [ps, bm], [gt])
                    gi = l * 4 + who * 2 + (1 if v == 5 else 0)
                    P.dma("sp", GRD[gi, :, hf * 512:(hf + 1) * 512], gt[:, :], r=[gt], w=[(GRD, (gi, hf))])
                else:
                    vi = {0: 0, 1: 1, 3: 2, 4: 3}[v]
                    tm = tr_.next()
                    P.tt("dve", tm[:, :], ps[:, :], bsl, ALU.add, [ps, bm], [tm])
                    for blk in range(4):
                        jk = jr.next()
                        j = hf * 4 + blk
                        P.stt(jk[:, :], tm[:, blk * 128:(blk + 1) * 128], 1.0, ident[:, :], ALU.mult, ALU.mult,
                              [tm, ident], [jk, (MV, (l, who, vi, j))], accum_out=mvc(l, who, vi, j, 1))
        t8 = P.sb("t8", [128, 8], F32)
        for who in range(2):
            for vi, goff in ((1, 0), (3, 8)):
                P.ts("dve", t8[:, :], mvc(l, who, vi), 1.0, None, ALU.add, None, [MV], [t8])
                P.tt("dve", mvc(l, who, vi), t8[:, :], pvc(l, goff, 8), ALU.mult, [t8, pv], [MV])

    def phase_norm(l, which, src, do_ctx):
        P.phase()
        HT = P.sb_top("HT", [128, 8, NA], BF16)
        vi_sh, vi_gs = (0, 1) if which == 1 else (2, 3)
        xr = Ring([P.sb("xt%d" % i, [128, DM], F32) for i in range(4)])
        xnr = Ring([P.sb("xn%d" % i, [128, DM], F32) for i in range(3)])
        sqj = P.sb("sqj", [128, DM], BF16)
        sm = Ring([P.sb("sm%d" % i, [128, 1], F32) for i in range(12)])
        tiles = ([(1, 0), (1, 1)] if do_ctx else []) + [(0, t) for t in range(32)]

        def stats(n):
            who, tt_ = tiles[n]
            if who == 0:
                xt = xr.next()
                P.dma("sp", xt[:, :], src[tt_ * 128:(tt_ + 1) * 128, :], w=[xt])
                xin, xdep = xt[:, :], xt
            else:
                xin, xdep = XC[:, tt_, :], XC
            ss = sm.next(); sd = sm.next(); rs = sm.next()
            P.actv(sqj[:, :], xin, AF.Square, [xdep], [sqj, ss], accum_out=ss[:, 0:1])
            P.actv(sd[:, :], ss[:, :], AF.Sqrt, [ss, eps_t], [sd], scale=1.0 / DM, bias=eps_t[:, 0:1])
            P.recip(rs[:, :], sd[:, :], [sd], [rs])
            xn = xnr.next()
            P.ts("dve", xn[:, :], xin, rs[:, 0:1], None, ALU.mult, None, [xdep, rs], [xn])
            return xn

        def tpose(n, xn):
            who, tt_ = tiles[n]
            col0 = (256 + tt_ * 128) if who == 0 else tt_ * 128
            for half in range(2):
                ps = PS[(2 * n + half) % 8]
                for q in range(4):
                    kc = half * 4 + q
                    P.tr(ps[:, q * 128:(q + 1) * 128], xn[:, kc * 128:(kc + 1) * 128], ident[:, :], [xn, ident], [ps])
                for q in range(4):
                    kc = half * 4 + q
                    o = HT[:, kc, col0:col0 + 128]
                    scl = mvc(l, who, vi_gs, kc, 1); shf = mvc(l, who, vi_sh, kc, 1)
                    if half == 0:
                        P.actv(o, ps[:, q * 128:(q + 1) * 128], AF.Identity, [ps, MV], [(HT, (n, kc))], scale=scl, bias=shf)
                    else:
                        P.ts("dve", o, ps[:, q * 128:(q + 1) * 128], scl, shf, ALU.mult, ALU.add, [ps, MV], [(HT, (n, kc))])
        prev = stats(0)
        for n in range(1, len(tiles)):
            cur = stats(n)
            tpose(n - 1, prev)
            prev = cur
        tpose(len(tiles) - 1, prev)
        return HT

    def keep_ht():
        return P.sb_top("HT", [128, 8, NA], BF16)

    def group(ps, n, wfn, wdep, HT, a0):
        for kc in range(8):
            P.mm(ps[:, 0:n], wfn(kc), HT[:, kc, a0:a0 + n], kc == 0, kc == 7, [wdep], [ps])

    def wload(buf3, l, col0, ncols, slot=None):
        dst = buf3[:, :, 0:ncols] if slot is None else buf3[:, slot, :, 0:ncols]
        P.dma("pool", dst, win_v[l, :, :, col0:col0 + ncols], w=[buf3])

    def phase_sconv_gates(l, do_ctx):
        P.phase()
        HT = keep_ht()
        tiles = a_tiles(do_ctx)
        W = NA + 4
        Mb = P.sb("Mb", [128, W], F32); Ab = P.sb("Ab", [128, W], F32)
        OBr = Ring([P.sb("OB%d" % i, [128, NA], BF16) for i in range(2)])
        scr = Ring([P.sb("sce%d" % i, [128, 512], F32) for i in range(3)])
        wbufs = [P.sb("ws%d" % i, [128, 3, 8, 128], BF16) for i in range(3)]
        for c in (0, 257, 258, W - 1):
            P.memset("pool", Mb[:, c:c + 1], 0.0, [Mb])

        def mcol(a0, is_lat):
            return (259 + (a0 - 256)) if is_lat else (1 + a0)

        def load(j):
            wb = wbufs[j % 3]
            for s, cb in enumerate((C_SB, C_SC, C_SX)):
                wload(wb, l, cb + j * 128, 128, slot=s)
            return wb
        loaded = {0: load(0), 1: load(1)}
        pi = 0
        for j in range(8):
            wb = loaded[j]
            if j + 2 < 8:
                loaded[j + 2] = load(j + 2)
            for (a0, n, is_lat, t0) in tiles:
                psc = PS[pi % 8]; psx = PS[(pi + 1) % 8]; pi += 2
                group(psc, n, lambda kc: wb[:, 1, kc, :], wb, HT, a0)
                group(psx, n, lambda kc: wb[:, 2, kc, :], wb, HT, a0)
                sce = scr.next()
                P.cp("act", sce[:, 0:n], psc[:, 0:n], [psc], [sce])
                m0 = mcol(a0, is_lat)
                P.tt("dve", Mb[:, m0:m0 + n], psx[:, 0:n], sce[:, 0:n], ALU.mult, [psx, sce], [(Mb, a0)])
            lo, hi = (1, W - 1) if do_ctx else (259, W - 1)
            L = hi - lo
            w0 = pvc(l, 16 + 0 * 8 + j); w1 = pvc(l, 16 + 1 * 8 + j); w2 = pvc(l, 16 + 2 * 8 + j); bb = pvc(l, 40 + j)
            P.ts("dve", Ab[:, lo:hi], Mb[:, lo - 1:hi - 1], w0, bb, ALU.mult, ALU.add, [Mb, pv], [Ab])
            P.stt(Ab[:, lo:hi], Mb[:, lo:hi], w1, Ab[:, lo:hi], ALU.mult, ALU.add, [Mb, pv, Ab], [Ab])
            P.stt(Ab[:, lo:hi], Mb[:, lo + 1:hi + 1], w2, Ab[:, lo:hi], ALU.mult, ALU.add, [Mb, pv, Ab], [Ab])
            OB = OBr.next()
            for (a0, n, is_lat, t0) in tiles:
                psb = PS[pi % 8]; pi += 1
                group(psb, n, lambda kc: wb[:, 0, kc, :], wb, HT, a0)
                m0 = mcol(a0, is_lat)
                P.tt("dve", OB[:, a0:a0 + n], psb[:, 0:n], Ab[:, m0:m0 + n], ALU.mult, [psb, Ab], [(OB, a0)])
            c0 = 0 if do_ctx else 256
            P.dma("sp", SC[j, :, c0:NA], OB[:, c0:NA], r=[OB], w=[(SC, j)])
        gw = [P.sb("gw%d" % i, [128, 8, 512], BF16) for i in range(2)]
        wload(gw[0], l, C_GT, 512)
        for g4 in range(6):
            if g4 + 1 < 6:
                wload(gw[(g4 + 1) % 2], l, C_GT + (g4 + 1) * 512, 512)
            wb = gw[g4 % 2]
            for s in range(4):
                ch = g4 * 4 + s
                OB = OBr.next()
                for (a0, n, is_lat, t0) in tiles:
                    ps = PS[pi % 8]; pi += 1
                    group(ps, n, lambda kc: wb[:, kc, s * 128:(s + 1) * 128], wb, HT, a0)
                    P.actv(OB[:, a0:a0 + n], ps[:, 0:n], AF.Sigmoid, [ps], [(OB, a0)])
                c0 = 0 if do_ctx else 256
                P.dma("sp", GS[ch, :, c0:NA], OB[:, c0:NA], r=[OB], w=[(GS, ch)])

    def phase_lru(l, do_ctx):
        P.phase()
        HT = keep_ht()
        tiles = a_tiles(True)
        W = NA + 6
        U = P.sb("U", [128, W], F32); M = P.sb("M", [128, W], F32)
        A_ = [P.sb("A%d" % d, [128, W], F32) for d in range(2)]
        D_ = [P.sb("D%d" % d, [128, W], F32) for d in range(2)]
        UB = P.sb("UB", [128, W], BF16)
        wrx = [P.sb("wrx%d" % i, [128, 2, 8, 128], BF16) for i in range(2)]
        wl = [P.sb("wl%d" % i, [128, 2, 2, 128], BF16) for i in range(2)]

        def lcol(a0, is_lat):
            return (261 + (a0 - 256)) if is_lat else (2 + a0)

        def load(n):
            wb = wrx[n % 2]; w2 = wl[n % 2]
            wload(wb, l, C_RX + n * 128, 128, slot=0)
            wload(wb, l, C_RG + n * 128, 128, slot=1)
            P.dma("pool", w2[:, 0, :, :], lwa_d.h.rearrange("l d n p e -> l n p d e")[l, n], w=[w2])
            P.dma("pool", w2[:, 1, :, :], lwx_d.h.rearrange("l d n p e -> l n p d e")[l, n], w=[w2])
            return wb, w2

        def rev(b_, c0, nn):
            a_ = b_[:, c0:c0 + nn]
            return AP(a_.tensor, a_.offset + nn - 1, [list(a_.ap[0]), [-1, nn]])
        for b_ in (A_[0], A_[1], D_[0], D_[1]):
            P.memset("pool", b_[:, 0:W], 0.0, [b_])
        nxt = load(0)
        pi = 0
        LO, HI = 2, W - 1
        for n in range(8):
            wb, w2 = nxt
            if n + 1 < 8:
                nxt = load(n + 1)
            RX = D_[1]
            for (c, k) in ((0, 2), (258, 3), (W - 1, 1)):
                P.memset("pool", RX[:, c:c + k], 0.0, [RX])
            for (a0, nn, is_lat, t0) in tiles:
                ps = PS[pi % 8]; pi += 1
                group(ps, nn, lambda kc: wb[:, 0, kc, :], wb, HT, a0)
                c0 = lcol(a0, is_lat)
                if pi % 2:
                    P.cp("act", RX[:, c0:c0 + nn], ps[:, 0:nn], [ps], [(RX, a0)])
                else:
                    P.cp("dve", RX[:, c0:c0 + nn], ps[:, 0:nn], [ps], [(RX, a0)])
            wc = [pvc(l, 48 + k * 8 + n) for k in range(4)]
            P.ts("dve", U[:, LO:HI], RX[:, LO - 2:HI - 2], wc[0], pvc(l, 80 + n), ALU.mult, ALU.add, [RX, pv], [U])
            for k in (1, 2, 3):
                P.stt(U[:, LO:HI], RX[:, LO - 2 + k:HI - 2 + k], wc[k], U[:, LO:HI], ALU.mult, ALU.add, [RX, pv, U], [U])
            P.cp("act", UB[:, LO:HI], U[:, LO:HI], [U], [UB])
            for d in range(2):
                A = A_[d]; D = D_[d]
                for (a0, nn, is_lat, t0) in tiles:
                    c0 = lcol(a0, is_lat)
                    pr = PS[pi % 8]; pz = PS[(pi + 1) % 8]; pi += 2
                    P.mm(pr[:, 0:nn], w2[:, 0, d, :], UB[:, c0:c0 + nn], True, True, [w2, UB], [pr])
                    P.mm(pz[:, 0:nn], w2[:, 1, d, :], UB[:, c0:c0 + nn], True, True, [w2, UB], [pz])
                    P.actv(A[:, c0:c0 + nn], pr[:, 0:nn], AF.Tanh, [pr, LV], [(A, a0)], scale=0.5, bias=lvc(l, 0, d, n))
                    P.actv(D[:, c0:c0 + nn], pz[:, 0:nn], AF.Tanh, [pz, LV], [(D, a0)], scale=0.5, bias=lvc(l, 1, d, n))
                P.actv(A[:, LO:HI], A[:, LO:HI], AF.Exp, [A, LV], [A], scale=lvc(l, 2, d, n), bias=lvc(l, 2, d, n))
                P.stt(D[:, LO:HI], D[:, LO:HI], 1.0, U[:, LO:HI], ALU.add, ALU.mult, [D, U], [D])
                P.actv(M[:, LO:HI], A[:, LO:HI], AF.Square, [A], [M])
                P.ts("dve", M[:, LO:HI], M[:, LO:HI], 1.0, None, ALU.min, None, [M], [M])
                P.actv(M[:, LO:HI], M[:, LO:HI], AF.Sqrt, [M, one_t], [M], scale=-1.0, bias=one_t[:, 0:1])
                P.stt(D[:, LO:HI], D[:, LO:HI], 0.5, M[:, LO:HI], ALU.mult, ALU.mult, [D, M], [D])
                if d == 0:
                    P.scan(D[:, 2:258], A[:, 2:258], D[:, 2:258], 0.0, [A, D], [D])
                    P.scan(D[:, 261:4357], A[:, 261:4357], D[:, 261:4357], D[:, 257:258], [A, D], [D])
                else:
                    P.scan(rev(D, 2, 256), rev(A, 2, 256), rev(D, 2, 256), 0.0, [A, D], [D])
                    P.scan(rev(D, 261, 4096), rev(A, 261, 4096), rev(D, 261, 4096), D[:, 2:3], [A, D], [D])
            G = A_[0]; Y = A_[1]
            gt = tiles if do_ctx else tiles[1:]
            for (a0, nn, is_lat, t0) in gt:
                ps = PS[pi % 8]; pi += 1
                group(ps, nn, lambda kc: wb[:, 1, kc, :], wb, HT, a0)
                c0 = lcol(a0, is_lat)
                P.actv(G[:, c0:c0 + nn], ps[:, 0:nn], AF.Gelu_apprx_tanh, [ps], [(G, a0)])
            P.tt("dve", D_[0][:, LO:HI], D_[0][:, LO:HI], D_[1][:, LO:HI], ALU.add, [D_[0], D_[1]], [D_[0]])
            P.tt("dve", Y[:, LO:HI], D_[0][:, LO:HI], G[:, LO:HI], ALU.mult, [D_[0], G], [Y])
            if do_ctx:
                P.dma("pool", LR[n, :, 0:256], Y[:, 2:258], r=[Y], w=[(LR, (n, 0))])
            P.dma("pool", LR[n, :, 256:NA], Y[:, 261:4357], r=[Y], w=[(LR, (n, 1))])

    def phase_qkv(l, do_ctx):
        P.phase()
        HT = keep_ht()
        cosb = P.sb("cosb", [128, SEQ], F32); sinb = P.sb("sinb", [128, SEQ], F32)
        P.dma("sp", cosb[:, :], cos_d[:, :], w=[cosb]); P.dma("sp", sinb[:, :], sin_d[:, :], w=[sinb])
        OBr = Ring([P.sb("OBq%d" % i, [128, NA], BF16) for i in range(2)])
        VB = P.sb("VB", [128, 34, 256], BF16)
        wv_ = P.sb("wv", [128, 8, 256], BF16)
        wk = P.sb("wk", [128, 8, 256], BF16)
        wq = [P.sb("wq%d" % i, [128, 8, 512], BF16) for i in range(2)]
        mkr = lambda nm, k, dt: Ring([P.sb("%s%d" % (nm, i), [128, 512], dt) for i in range(k)])
        sqr = mkr("sq", 3, BF16); sdr = mkr("sd", 2, F32); rsr = mkr("rs", 2, F32); qnr = mkr("qn", 4, F32)
        qbr = mkr("qnb", 3, BF16); t1r = mkr("t1", 2, F32); t2r = mkr("t2", 2, F32)
        wload(wk, l, C_K, 256); wload(wv_, l, C_V, 256); wload(wq[0], l, C_Q, 512); wload(wq[1], l, C_Q + 512, 512)

        def run_items(items):
            N = len(items)
            stt_ = {}

            def A(i):
                it = items[i]
                ps = PS[i % 4]
                group(ps, it["n"], it["wfn"], it["wdep"], HT, it["a0"])
                sq = sqr.next()
                P.actv(sq[:, 0:it["n"]], ps[:, 0:it["n"]], AF.Square, [ps], [sq])
                stt_[i] = {"ps": ps, "sq": sq}

            def B(i):
                it = items[i]; n = it["n"]; st = stt_[i]
                pss = PS[4 + i % 2]
                P.mm(pss[:, 0:n], ones_b[:, :], st["sq"][:, 0:n], True, True, [ones_b, st["sq"]], [pss])
                sd = sdr.next(); rs = rsr.next()
                P.actv(sd[:, 0:n], pss[:, 0:n], AF.Ln, [pss, eps_t], [sd], scale=1.0 / 128, bias=eps_t[:, 0:1])
                P.actv(rs[:, 0:n], sd[:, 0:n], AF.Exp, [sd], [rs], scale=-0.5)
                if not it["lat"]:
                    P.stt(it["out"], st["ps"][:, 0:n], it["g"], rs[:, 0:n], ALU.mult, ALU.mult, [st["ps"], it["gdep"], rs], it["outw"])
                else:
                    qn = qnr.next(); qb = qbr.next()
                    P.stt(qn[:, 0:n], st["ps"][:, 0:n], it["g"], rs[:, 0:n], ALU.mult, ALU.mult, [st["ps"], it["gdep"], rs], [qn])
                    st["qn"] = qn; st["qb"] = qb

            def Cc(i):
                it = items[i]; n = it["n"]; st = stt_[i]
                if it["lat"]:
                    P.cp("act", st["qb"][:, 0:n], st["qn"][:, 0:n], [st["qn"]], [st["qb"]])

            def C(i):
                it = items[i]; n = it["n"]; st = stt_.pop(i)
                if it["lat"]:
                    t0 = it["t0"]
                    pq = PS[6 + i % 2]
                    P.mm(pq[:, 0:n], perm_b[:, :], st["qb"][:, 0:n], True, True, [perm_b, st["qb"]], [pq])
                    t1 = t1r.next(); t2 = t2r.next()
                    P.tt("pool", t1[:, 0:n], st["qn"][:, 0:n], cosb[:, t0:t0 + n], ALU.mult, [st["qn"], cosb], [t1])
                    P.tt("dve", t2[:, 0:n], pq[:, 0:n], sinb[:, t0:t0 + n], ALU.mult, [pq, sinb], [t2])
                    P.tt("pool", it["out"], t1[:, 0:n], t2[:, 0:n], ALU.add, [t1, t2], it["outw"])
                if it.get("done"):
                    it["done"]()
            for idx in range(N + 3):
                if idx < N:
                    A(idx)
                if 0 <= idx - 1 < N:
                    B(idx - 1)
                if 0 <= idx - 2 < N:
                    Cc(idx - 2)
                if 0 <= idx - 3 < N:
                    C(idx - 3)

        tiles_all = a_tiles(True)
        items = []
        for h in range(2):
            OB = OBr.next()
            for ti, (a0, n, is_lat, t0) in enumerate(tiles_all):
                it = dict(n=n, a0=a0, lat=is_lat, t0=t0, wfn=(lambda kc, h=h: wk[:, kc, h * 128:(h + 1) * 128]), wdep=wk,
                          g=pvc(l, 137), gdep=pv, out=OB[:, a0:a0 + n], outw=[(OB, a0)])
                if ti == len(tiles_all) - 1:
                    it["done"] = (lambda h=h, OB=OB: P.dma("sp", KS[h, :, :], OB[:, :], r=[OB], w=[(KS, h)]))
                items.append(it)
        run_items(items)
        for tt_ in range(34):
            ps = PS[tt_ % 4]
            for kc in range(8):
                P.mm(ps[:, 0:256], HT[:, kc, tt_ * 128:(tt_ + 1) * 128], wv_[:, kc, :], kc == 0, kc == 7, [wv_], [ps])
            if tt_ % 2:
                P.cp("act", VB[:, tt_, :], ps[:, 0:256], [ps], [(VB, tt_)])
            else:
                P.cp("dve", VB[:, tt_, :], ps[:, 0:256], [ps], [(VB, tt_)])
        P.dma("sp", VS[:, :, :], VB[:, :, :], r=[VB], w=[VS])
        tiles_q = a_tiles(do_ctx)
        c0 = 0 if do_ctx else 256
        items = []
        for h in range(8):
            OB = OBr.next() if h % 2 == 0 else OBr.bufs[1]
            OB = OBr.bufs[h % 2]
            wb = wq[h // 4]
            for ti, (a0, n, is_lat, t0) in enumerate(tiles_q):
                it = dict(n=n, a0=a0, lat=is_lat, t0=t0, wfn=(lambda kc, h=h, wb=wb: wb[:, kc, (h % 4) * 128:(h % 4 + 1) * 128]), wdep=wb,
                          g=QG[:, l:l + 1], gdep=QG, out=OB[:, a0:a0 + n], outw=[(OB, a0)])
                if ti == len(tiles_q) - 1:
                    it["done"] = (lambda h=h, OB=OB: P.dma("sp", QS[h, :, c0:NA], OB[:, c0:NA], r=[OB], w=[(QS, h)]))
                items.append(it)
        run_items(items)

    def phase_att(l, do_ctx):
        P.phase()
        KT = P.sb("KT", [128, 2, NA], BF16); V = P.sb("V", [128, 34, 256], BF16)
        P.dma("sp", KT[:, :, :], KS.h.rearrange("h p c -> p h c"), w=[KT])
        P.dma("sp", V[:, :, :], VS[:, :, :], w=[V])
        qr = Ring([P.sb("qb%d" % i, [128, 4, 128], BF16) for i in range(3)])
        ptr = Ring([P.sb("pt%d" % i, [128, 512], BF16) for i in range(10)])
        accr = {e: Ring([P.sb("acc%s%d" % (e, i), [128, 512], F32) for i in range(2)]) for e in ("dve", "pool")}
        rdr = Ring([P.sb("rd%d" % i, [128, 512], F32) for i in range(2)])
        obr = Ring([P.sb("ob%d" % i, [128, 4, 128], BF16) for i in range(2)])
        QSv = QS.h.rearrange("h p c -> p h c"); ATv = AT.h.rearrange("h p c -> p h c")
        blocks = []
        for g in range(2):
            if do_ctx:
                blocks += [(g, 0, 2), (g, 128, 2)]
            blocks += [(g, 256 + 128 * qb, 34) for qb in range(32)]
        qbufs = {}

        def qload(i):
            g, c0, nk = blocks[i]
            qb = qr.next()
            P.dma("sp", qb[:, :, :], QSv[:, 4 * g:4 * g + 4, c0:c0 + 128], w=[qb])
            qbufs[i] = qb
        qload(0); qload(1)
        si = 0

        def finalize(pO, pD, acc, used, g, c0, bi):
            for ui, e in enumerate(used):
                P.mm(pD[:, :], ones_f[:, :], acc[e][:, :], False, ui == len(used) - 1, [ones_f, acc[e]], [pD])
            rd = rdr.next(); ob = obr.next()
            P.recip(rd[:, :], pD[:, :], [pD], [rd])
            P.tt("dve", ob[:, :, :].rearrange("p h c -> p (h c)"), pO[:, :], rd[:, :], ALU.mult, [pO, rd], [ob])
            P.dma("sp", ATv[:, 4 * g:4 * g + 4, c0:c0 + 128], ob[:, :, :], r=[ob], w=[(AT, bi)])
        fin_prev = None
        fin_next = None
        for bi, (g, c0, nk) in enumerate(blocks):
            fin_prev = fin_next
            if bi + 2 < len(blocks):
                qload(bi + 2)
            qb = qbufs.pop(bi)
            qflat = qb[:, :, :].rearrange("p h c -> p (h c)")
            pO = PS[3 + bi % 2]; pD = PS[5 + bi % 2]
            acc = {"dve": accr["dve"].next(), "pool": accr["pool"].next()}
            inited = {"dve": False, "pool": False}
            plan = {}
            j = 0
            for k0 in range(nk):
                if k0 % 4 == 0:
                    plan[k0] = "pe"
                else:
                    plan[k0] = "pool" if j % 2 == 0 else "dve"
                    j += 1
            n_acc = len(set(v for v in plan.values() if v != "pe"))
            last_pe = max(k for k, v in plan.items() if v == "pe")
            pts = {}
            for kc in range(nk + 2):
                if kc == min(6, nk + 1) and fin_prev is not None:
                    finalize(*fin_prev)
                    fin_prev = None
                if kc < nk:
                    pS = PS[si % 3]; si += 1
                    P.mm(pS[:, :], KT[:, g, kc * 128:(kc + 1) * 128], qflat, True, True, [KT, qb], [pS])
                    pt = ptr.next()
                    P.actv(pt[:, :], pS[:, :], AF.Exp, [pS], [pt])
                    pts[kc] = pt
                k0 = kc - 2
                if k0 >= 0:
                    pt0 = pts.pop(k0)
                    P.mm(pO[:, :], V[:, k0, g * 128:(g + 1) * 128], pt0[:, :], k0 == 0, k0 == nk - 1, [V, pt0], [pO])
                    e = plan[k0]
                    if e == "pe":
                        P.mm(pD[:, :], ones_b[:, :], pt0[:, :], k0 == 0, (n_acc == 0 and k0 == last_pe), [ones_b, pt0], [pD])
                    else:
                        ac = acc[e]
                        if not inited[e]:
                            P.cp(e, ac[:, :], pt0[:, :], [pt0], [ac])
                            inited[e] = True
                        else:
                            P.tt(e, ac[:, :], ac[:, :], pt0[:, :], ALU.add, [ac, pt0], [ac])
            used = [e for e in ("dve", "pool") if inited[e]]
            fin_next = (pO, pD, acc, used, g, c0, bi)
            if bi == len(blocks) - 1:
                finalize(*fin_next)

    def residual_tile(l, who, s_rows, pso_fn, Grep, xsrc, last_final, fgrep, sm, xr, xor_, tmr, sqj):
        if who == 0:
            xt = xr.next()
            P.dma("sp", xt[:, :], xsrc[s_rows:s_rows + 128, :], r=[(xsrc, s_rows)], w=[xt])
            xin, xdep = xt, xt
            xo = xor_.next()
            xo_ap = lambda ch: xo[:, ch * 512:(ch + 1) * 512]
            xin_ap = lambda ch: xt[:, ch * 512:(ch + 1) * 512]
            xow = [xo]
        else:
            tt_ = s_rows // 128
            xo = None
            xo_ap = lambda ch: XC[:, tt_, ch * 512:(ch + 1) * 512]
            xin_ap = xo_ap
            xdep = XC
            xow = [XC]
        for ch in range(2):
            pso = pso_fn(ch)
            tm = tmr.next()
            P.tt("dve", tm[:, :], pso[:, :], Grep[:, ch * 512:(ch + 1) * 512], ALU.mult, [pso, Grep], [tm])
            P.tt("pool", xo_ap(ch), tm[:, :], xin_ap(ch), ALU.add, [tm, xdep], xow)
        if who == 0:
            if last_final:
                ss = sm.next(); sd = sm.next(); rs = sm.next()
                P.actv(sqj[:, :], xo[:, :], AF.Square, [xo], [sqj, ss], accum_out=ss[:, 0:1])
                P.actv(sd[:, :], ss[:, :], AF.Sqrt, [ss, eps_t], [sd], scale=1.0 / DM, bias=eps_t[:, 0:1])
                P.recip(rs[:, :], sd[:, :], [sd], [rs])
                yo = xr.next()
                P.stt(yo[:, :], xo[:, :], rs[:, 0:1], fgrep[:, :], ALU.mult, ALU.mult, [xo, rs, fgrep], [yo])
                P.dma("act", y_d[s_rows:s_rows + 128, :], yo[:, :], r=[yo], w=[(y_d, s_rows)])
            else:
                P.dma("act", XS[s_rows:s_rows + 128, :], xo[:, :], r=[xo], w=[(XS, s_rows)])

    def phase_merge(l, do_ctx):
        P.phase()
        Wt = {}
        for nm, d in (("a", wao_d), ("s", wso_d), ("l", wlo_d), ("m", wmo_d)):
            Wt[nm] = P.sb("W" + nm, [128, 8, DM], BF16)
        wviews = {nm: d.h.rearrange("l (kc p) n -> l p kc n", p=128) for nm, d in (("a", wao_d), ("s", wso_d), ("l", wlo_d), ("m", wmo_d))}
        for j4 in range(4):
            for nm in ("a", "s", "l"):
                P.dma("pool", Wt[nm][:, :, j4 * 256:(j4 + 1) * 256], wviews[nm][l, :, :, j4 * 256:(j4 + 1) * 256], w=[(Wt[nm], j4)])
        for hh in range(2):
            P.dma("pool", Wt["m"][:, :, hh * 512:(hh + 1) * 512], wviews["m"][l, :, :, hh * 512:(hh + 1) * 512], w=[(Wt["m"], hh)])
        G1 = [P.sb("G1_%d" % w_, [128, DM], F32) for w_ in range(2)]
        P.dma("sp", G1[0][:, :], GRD[l * 4 + 0, :, :], w=[G1[0]])
        if do_ctx:
            P.dma("sp", G1[1][:, :], GRD[l * 4 + 2, :, :], w=[G1[1]])
        inr = {k: Ring([P.sb("in%s%d" % (k, i), [128, 8, 512], BF16) for i in range(2)]) for k in ("a", "s", "l")}
        gjr = Ring([P.sb("gj%d" % i, [128, 3, 512], BF16) for i in range(3)])
        mr = {k: Ring([P.sb("m%s%d" % (k, i), [128, 512], F32) for i in range(2)]) for k in ("1", "2", "3", "12")}
        mgr = Ring([P.sb("mg%d" % i, [128, 8, 512], BF16) for i in range(2)])
        xr = Ring([P.sb("xt%d" % i, [128, DM], F32) for i in range(3)])
        xor_ = Ring([P.sb("xo%d" % i, [128, DM], F32) for i in range(2)])
        tmr = Ring([P.sb("tm%d" % i, [128, 512], F32) for i in range(2)])
        sm = Ring([P.sb("sm%d" % i, [128, 1], F32) for i in range(6)])
        srcv = {"a": AT.h.rearrange("h p c -> p h c"), "s": SC.h.rearrange("h p c -> p h c"), "l": LR.h.rearrange("h p c -> p h c")}
        GSv = GS.h.rearrange("(b j) p c -> j p b c", b=3)
        xsrc = x_d if l == 0 else XS
        tiles = a_tiles(do_ctx)
        ins = {}

        def tload(i):
            a0, n, is_lat, t0 = tiles[i]
            d = {}
            for k in ("a", "s", "l"):
                b = inr[k].next()
                P.dma("sp", b[:, :, 0:n], srcv[k][:, :, a0:a0 + n], w=[b])
                d[k] = b
            ins[i] = d
        tload(0)
        oi = 0

        def out_stage(mg, n, is_lat, t0):
            who = 0 if is_lat else 1
            for s in range(n // 128):
                def pso_fn(ch, s=s):
                    nonlocal oi
                    ps = PS[6 + oi % 2]; oi += 1
                    for kc in range(8):
                        P.mm(ps[:, :], mg[:, kc, s * 128:(s + 1) * 128], Wt["m"][:, kc, ch * 512:(ch + 1) * 512], kc == 0, kc == 7, [mg, (Wt["m"], ch)], [ps])
                    return ps
                rows = (t0 + s * 128) if is_lat else s * 128
                residual_tile(l, who, rows, pso_fn, G1[who], xsrc, False, None, sm, xr, xor_, tmr, None)
        pending = None
        for ti, (a0, n, is_lat, t0) in enumerate(tiles):
            if ti + 1 < len(tiles):
                tload(ti + 1)
            cur = ins.pop(ti)
            mg = mgr.next()
            for j in range(8):
                gj = gjr.next()
                P.dma("sp", gj[:, :, 0:n], GSv[j, :, :, a0:a0 + n], w=[gj])
                pss = {}
                for bi_, k in enumerate(("a", "s", "l")):
                    ps = PS[(j % 2) * 3 + bi_]
                    for kc in range(8):
                        P.mm(ps[:, 0:n], Wt[k][:, kc, j * 128:(j + 1) * 128], cur[k][:, kc, 0:n], kc == 0, kc == 7, [(Wt[k], j // 2), cur[k]], [ps])
                    pss[k] = ps
                m1 = mr["1"].next(); m2 = mr["2"].next(); m3 = mr["3"].next(); m12 = mr["12"].next()
                P.tt("dve", m1[:, 0:n], pss["a"][:, 0:n], gj[:, 0, 0:n], ALU.mult, [pss["a"], gj], [m1])
                P.tt("dve", m2[:, 0:n], pss["s"][:, 0:n], gj[:, 1, 0:n], ALU.mult, [pss["s"], gj], [m2])
                P.tt("dve", m3[:, 0:n], pss["l"][:, 0:n], gj[:, 2, 0:n], ALU.mult, [pss["l"], gj], [m3])
                P.tt("pool", m12[:, 0:n], m1[:, 0:n], m2[:, 0:n], ALU.add, [m1, m2], [m12])
                P.tt("pool", mg[:, j, 0:n], m12[:, 0:n], m3[:, 0:n], ALU.add, [m12, m3], [(mg, j)])
            if pending is not None:
                out_stage(*pending)
            pending = (mg, n, is_lat, t0)
        out_stage(*pending)

    def phase_ffn1(l, do_ctx):
        P.phase()
        HT = keep_ht()
        tiles = a_tiles(do_ctx)
        wb_ = [P.sb("wf%d" % i, [128, 2, 8, 128], BF16) for i in range(3)]
        OBr = Ring([P.sb("OBf%d" % i, [128, NA], BF16) for i in range(2)])
        sgr = Ring([P.sb("sg%d" % i, [128, 512], F32) for i in range(3)])
        wv = wfi_d.h.rearrange("l (kc p) n -> l p kc n", p=128)

        def load(j):
            wb = wb_[j % 3]
            P.dma("pool", wb[:, 0, :, :], wv[l, :, :, j * 128:(j + 1) * 128], w=[wb])
            P.dma("pool", wb[:, 1, :, :], wv[l, :, :, FFW + j * 128:FFW + (j + 1) * 128], w=[wb])
            return wb
        loaded = {0: load(0), 1: load(1)}
        pi = 0
        for j in range(22):
            wb = loaded.pop(j)
            if j + 2 < 22:
                loaded[j + 2] = load(j + 2)
            OB = OBr.next()
            for (a0, n, is_lat, t0) in tiles:
                pg = PS[pi % 8]; pu = PS[(pi + 1) % 8]; pi += 2
                group(pg, n, lambda kc: wb[:, 0, kc, :], wb, HT, a0)
                group(pu, n, lambda kc: wb[:, 1, kc, :], wb, HT, a0)
                sg = sgr.next()
                P.actv(sg[:, 0:n], pg[:, 0:n], AF.Silu, [pg], [sg])
                P.tt("dve", OB[:, a0:a0 + n], pu[:, 0:n], sg[:, 0:n], ALU.mult, [pu, sg], [(OB, a0)])
            c0 = 0 if do_ctx else 256
            P.dma("sp", FA[j, :, c0:NA], OB[:, c0:NA], r=[OB], w=[(FA, j)])

    def phase_ffn2(l, do_ctx, final):
        P.phase()
        WO = P.sb("WO", [128, 22, DM], BF16)
        vw = wfo_d.h.rearrange("l (kc p) n -> l p kc n", p=128)
        for q4 in range(4):
            P.dma("pool", WO[:, :, q4 * 256:(q4 + 1) * 256], vw[l, :, :, q4 * 256:(q4 + 1) * 256], w=[(WO, q4)])
        G2 = [P.sb("G2_%d" % w_, [128, DM], F32) for w_ in range(2)]
        P.dma("sp", G2[0][:, :], GRD[l * 4 + 1, :, :], w=[G2[0]])
        if do_ctx:
            P.dma("sp", G2[1][:, :], GRD[l * 4 + 3, :, :], w=[G2[1]])
        fgrep = None; sqj = None
        if final:
            fgrep = P.sb("fgrep", [128, DM], F32)
            P.dma("sp", fgrep[:, :], mk_ap(fg_d, 0, [[0, 128], [1, DM]]), w=[fgrep])
            sqj = P.sb("sqj", [128, DM], BF16)
        far = Ring([P.sb("fa%d" % i, [128, 22, 512], BF16) for i in range(2)])
        xr = Ring([P.sb("xt%d" % i, [128, DM], F32) for i in range(4)])
        xor_ = Ring([P.sb("xo%d" % i, [128, DM], F32) for i in range(2)])
        tmr = Ring([P.sb("tm%d" % i, [128, 512], F32) for i in range(2)])
        sm = Ring([P.sb("sm%d" % i, [128, 1], F32) for i in range(9)])
        FAv = FA.h.rearrange("j p c -> p j c")
        tiles = a_tiles(do_ctx)
        ins = {}

        def tload(i):
            a0, n, is_lat, t0 = tiles[i]
            b = far.next()
            P.dma("sp", b[:, :, 0:n], FAv[:, :, a0:a0 + n], w=[b])
            ins[i] = b
        tload(0)
        oi = 0
        for ti, (a0, n, is_lat, t0) in enumerate(tiles):
            if ti + 1 < len(tiles):
                tload(ti + 1)
            fa = ins.pop(ti)
            who = 0 if is_lat else 1
            for s in range(n // 128):
                def pso_fn(ch, s=s):
                    nonlocal oi
                    ps = PS[oi % 8]; oi += 1
                    for kc in range(22):
                        P.mm(ps[:, :], fa[:, kc, s * 128:(s + 1) * 128], WO[:, kc, ch * 512:(ch + 1) * 512], kc == 0, kc == 21, [fa, (WO, 2 * ch), (WO, 2 * ch + 1)], [ps])
                    return ps
                rows = (t0 + s * 128) if is_lat else s * 128
                residual_tile(l, who, rows, pso_fn, G2[who], XS, final, fgrep, sm, xr, xor_, tmr, sqj)

    def dump_ht():
        P.phase()
        HT = keep_ht()
        P.dma("sp", HTD[:, :, :], HT[:, :, :], w=[HTD])
        P.dma("sp", XCD[:, :, :], XC[:, :, :], w=[XCD])

    def run_all():
        if stop('const'):
            P.phase()
            P.dma('sp', GRD[0, :, 0:128], ident[:, :], w=[GRD])
            return
        for l in range(nlayers):
            last = l == 1
            do_ctx = not last
            phase_mod(l)
            if stop("mod%d" % l): return
            phase_norm(l, 1, x_d if l == 0 else XS, True)
            if stop("n1_%d" % l): dump_ht(); return
            phase_sconv_gates(l, do_ctx)
            if stop("sc%d" % l): return
            phase_lru(l, do_ctx)
            if stop("lru%d" % l): return
            phase_qkv(l, do_ctx)
            if stop("qkv%d" % l): return
            phase_att(l, do_ctx)
            if stop("att%d" % l): return
            phase_merge(l, do_ctx)
            if stop("mg%d" % l): dump_ht(); return
            phase_norm(l, 2, XS, do_ctx)
            if stop("n2_%d" % l): dump_ht(); return
            phase_ffn1(l, do_ctx)
            if stop("f1_%d" % l): return
            phase_ffn2(l, do_ctx, last)
            if stop("f2_%d" % l): dump_ht(); return
    run_all()
    nops = {e: len(P.ops[e]) for e in ENGS}
    nc = P.finish()
    return nc, nops


def _rope_tables():
    quarter = 32
    inv_freq = (10000.0 ** (-np.arange(quarter, dtype=np.float32) / quarter)).astype(np.float32)
    t = np.arange(SEQ)
    row = (t // 64).astype(np.float32); colp = (t % 64).astype(np.float32)
    cos = np.zeros((128, SEQ), np.float32); sin = np.zeros((128, SEQ), np.float32)
    for d in range(128):
        pos = row if d < 64 else colp
        j = d % 64
        i = j % 32
        ang = (pos * inv_freq[i]).astype(np.float32)
        cos[d] = np.cos(ang)
        sin[d] = np.sin(ang) * (-1.0 if j < 32 else 1.0)
    return cos, sin


def _consts():
    cm = np.zeros((128, 256), np.float32)
    cm[:, :128] = np.eye(128, dtype=np.float32)
    for m in range(128):
        j = m % 64
        k = m + 32 if j < 32 else m - 32
        cm[k, 128 + m] = 1.0
    return cm


def _fm(v):
    v = np.asarray(v, np.float32).reshape(-1, 8, 128)
    return np.ascontiguousarray(v.transpose(2, 0, 1).reshape(128, -1))


def make_in_maps(inp):
    f = lambda a: np.ascontiguousarray(np.asarray(a, np.float32))
    pvs = []
    for l in range(2):
        cols = [_fm(inp["norm1_g"][l]), _fm(inp["norm2_g"][l]), _fm(inp["sconv_w"][l]), _fm(inp["sconv_b"][l]),
                _fm(inp["lru_conv_w"][l]), _fm(inp["lru_conv_b"][l]), _fm(inp["lru_ba"][l]), _fm(inp["lru_bx"][l]),
                _fm(inp["lru_lambda"][l]), f(inp["q_norm_g"][l]).reshape(128, 1), f(inp["k_norm_g"][l]).reshape(128, 1)]
        pvs.append(np.concatenate(cols, axis=1))
        assert pvs[-1].shape[1] == PVL
    pv = np.ascontiguousarray(np.concatenate(pvs + [_fm(inp["final_g"])], axis=1))
    cos, sin = _rope_tables()
    shared = {
        "pv": pv, "bmod": f(inp["b_mod"]), "fgrow": f(inp["final_g"]).reshape(1, DM), "cost": cos, "sint": sin, "cm": _consts(),
        "w_mod": f(inp["w_mod"]), "w_in": f(inp["w_in"]), "w_ao": f(inp["w_attn_out"]), "w_so": f(inp["w_sconv_out"]),
        "w_lo": f(inp["w_lru_out"]), "w_mo": f(inp["w_merge_out"]), "w_fi": f(inp["w_ffn_in"]), "w_fo": f(inp["w_ffn_out"]),
        "lwa": f(inp["lru_wa"]), "lwx": f(inp["lru_wx"]),
    }
    maps = []
    ccx = _fm(inp["c_ctx"])
    for b in range(8):
        m = dict(shared)
        m["x"] = f(inp["x"][b]); m["ctx"] = f(inp["ctx"][b])
        m["cc"] = np.ascontiguousarray(np.concatenate([_fm(inp["c"][b]), ccx], axis=1))
        maps.append(m)
    return maps


def kernel(**inputs):
    nc = build()[0]
    maps = make_in_maps(inputs)
    res = run_bass_kernel_spmd(nc, maps, core_ids=list(range(8)))
    return np.stack([np.asarray(r["y"], np.float32) for r in res.results], axis=0)
```

```python
import contextlib
import numpy as np
import concourse.bass as bass
import concourse.mybir as mybir
from concourse.ap import AP
from concourse.bass_utils import run_bass_kernel_spmd

F32 = mybir.dt.float32
BF16 = mybir.dt.bfloat16
ALU = mybir.AluOpType
AF = mybir.ActivationFunctionType

ENGS = ("pe", "act", "dve", "pool", "sp")
N_DMA_SEMS = 48
SB_LO = 16512
SB_HI = 229376 - 4096


class T:
    __slots__ = ("w", "r")

    def __init__(self):
        self.w = None
        self.r = {}


class Buf:
    def __init__(self, name, handle, space):
        self.name = name
        self.h = handle
        self.space = space
        self.whole = T()
        self.cells = {}
        self.epoch = -1

    def __getitem__(self, k):
        return self.h[k]


class Op:
    __slots__ = ("eng", "fn", "deps", "need_inc", "val", "dma")

    def __init__(self, eng, fn, deps, dma=None):
        self.eng = eng
        self.fn = fn
        self.deps = deps
        self.need_inc = False
        self.val = None
        self.dma = dma


class Prog:
    def __init__(self):
        self.nc = bass.Bass("TRN2", target_bir_lowering=False)
        self.stack = contextlib.ExitStack()
        self.ops = {e: [] for e in ENGS}
        self.dma_nextq = {}
        self.top_cache = {}
        self.dma_cnt = [0] * N_DMA_SEMS
        self.dma_last_ev = [None] * N_DMA_SEMS
        self.epoch = 0
        self.persist_ptr = SB_LO
        self.ptr = SB_LO
        self.top = SB_HI
        self.nalloc = 0

    def dram(self, name, shape, dtype, kind):
        return Buf(name, self.nc.dram_tensor(name, list(shape), dtype, kind=kind), "dram")

    @staticmethod
    def _bytes(shape, dtype):
        n = 1
        for s in shape[1:]:
            n *= s
        return n * (4 if dtype == F32 else 2)

    def _at(self, name, shape, dtype, off):
        self.nalloc += 1
        h = self.nc.alloc_sbuf_tensor_at("%s_%d" % (name, self.nalloc), list(shape), dtype, offset=off)
        return Buf(name, h, "sbuf")

    def persist(self, name, shape, dtype):
        assert self.ptr == self.persist_ptr
        nb = (self._bytes(shape, dtype) + 63) // 64 * 64
        b = self._at(name, shape, dtype, self.persist_ptr)
        self.persist_ptr += nb
        self.ptr = self.persist_ptr
        return b

    def sb(self, name, shape, dtype):
        nb = (self._bytes(shape, dtype) + 63) // 64 * 64
        assert self.ptr + nb <= self.top, "SBUF overflow at %s: %d + %d > %d" % (name, self.ptr, nb, self.top)
        b = self._at(name, shape, dtype, self.ptr)
        self.ptr += nb
        return b

    def sb_top(self, name, shape, dtype):
        nb = (self._bytes(shape, dtype) + 63) // 64 * 64
        self.top -= nb
        assert self.top >= self.ptr
        key = (name, tuple(shape), str(dtype), self.top)
        if key not in self.top_cache:
            self.top_cache[key] = self._at(name, shape, dtype, self.top)
        return self.top_cache[key]

    def psum(self, name, shape, dtype):
        h = self.stack.enter_context(self.nc.psum_tensor(name, list(shape), dtype))
        return Buf(name, h, "psum")

    def _norm(self, x):
        if isinstance(x, Buf):
            b, key = x, None
        else:
            b, key = x
        if b.epoch != self.epoch:
            b.epoch = self.epoch
            b.whole = T()
            b.cells = {}
        return b, key

    def _collect(self, reads, writes):
        deps = {}

        def add(ev):
            if ev is not None and deps.get(ev[0], -1) < ev[1]:
                deps[ev[0]] = ev[1]

        def add_r(t):
            for k, v in t.r.items():
                if deps.get(k, -1) < v:
                    deps[k] = v

        for x in reads:
            b, key = self._norm(x)
            add(b.whole.w)
            if key is None:
                for t in b.cells.values():
                    add(t.w)
            else:
                t = b.cells.get(key)
                if t is not None:
                    add(t.w)
        for x in writes:
            b, key = self._norm(x)
            add(b.whole.w)
            add_r(b.whole)
            if key is None:
                for t in b.cells.values():
                    add(t.w)
                    add_r(t)
            else:
                t = b.cells.get(key)
                if t is not None:
                    add(t.w)
                    add_r(t)
        return deps

    def _record(self, reads, writes, ev):
        k, v = ev
        for x in reads:
            b, key = self._norm(x)
            t = b.whole if key is None else b.cells.setdefault(key, T())
            if t.r.get(k, -1) < v:
                t.r[k] = v
        for x in writes:
            b, key = self._norm(x)
            if key is None:
                b.whole.w = ev
                b.whole.r = {}
                b.cells = {}
            else:
                t = b.cells.setdefault(key, T())
                t.w = ev
                t.r = {}

    def op(self, eng, fn, r=(), w=(), extra=()):
        deps = self._collect(r, w)
        for k, v in extra:
            if deps.get(k, -1) < v:
                deps[k] = v
        lst = self.ops[eng]
        if eng == "pe":
            deps.pop("pe", None)
        o = Op(eng, fn, list(deps.items()))
        self._record(r, w, (eng, len(lst)))
        lst.append(o)
        return o

    def dma(self, q, out, in_, r=(), w=(), **kw):
        deps = self._collect(r, w)
        lo, hi = {"sp": (0, 24), "act": (24, 36)}.get(q, (36, N_DMA_SEMS))
        i = self.dma_nextq.get(q, lo)
        self.dma_nextq[q] = lo + (i + 1 - lo) % (hi - lo)
        prev = self.dma_last_ev[i]
        if prev is not None and deps.get(prev[0], -1) < prev[1]:
            deps[prev[0]] = prev[1]
        self.dma_cnt[i] += 1
        val = 16 * self.dma_cnt[i]
        ev = (("dma", i), val)
        self.dma_last_ev[i] = ev
        fn = (lambda e: e.dma_start(out=out, in_=in_, **kw))
        o = Op(q, fn, list(deps.items()), dma=(i, val))
        self.ops[q].append(o)
        self._record(r, w, ev)
        return o

    def phase(self):
        B = Buf("bar", None, "none")
        for e in ENGS:
            if e == "sp":
                extra = [ev for ev in self.dma_last_ev if ev is not None]
                self.op(e, lambda eng: eng.nop(), w=[(B, e)], extra=extra)
            else:
                self.op(e, lambda eng: eng.drain(), w=[(B, e)])
        for e in ENGS:
            self.op(e, lambda eng: eng.nop(), r=[B])
        self.epoch += 1
        self.ptr = self.persist_ptr
        self.top = SB_HI

    def mm(self, out, lhsT, rhs, start, stop, r, w):
        return self.op("pe", lambda e: e.matmul(out, lhsT=lhsT, rhs=rhs, start=start, stop=stop), r, w)

    def tr(self, out, in_, ident, r, w):
        return self.op("pe", lambda e: e.transpose(out=out, in_=in_, identity=ident), r, w)

    def actv(self, out, in_, func, r, w, scale=None, bias=None, accum_out=None):
        kw = {}
        if scale is not None:
            kw["scale"] = scale
        if bias is not None:
            kw["bias"] = bias
        if accum_out is not None:
            kw["accum_out"] = accum_out
        return self.op("act", lambda e: e.activation(out=out, in_=in_, func=func, **kw), r, w)

    def tt(self, eng, out, in0, in1, op, r, w):
        return self.op(eng, lambda e: e.tensor_tensor(out=out, in0=in0, in1=in1, op=op), r, w)

    def ts(self, eng, out, in0, s1, s2, op0, op1, r, w):
        if s2 is None:
            return self.op(eng, lambda e: e.tensor_scalar(out=out, in0=in0, scalar1=s1, scalar2=None, op0=op0), r, w)
        return self.op(eng, lambda e: e.tensor_scalar(out=out, in0=in0, scalar1=s1, scalar2=s2, op0=op0, op1=op1), r, w)

    def stt(self, out, in0, scalar, in1, op0, op1, r, w, accum_out=None):
        if accum_out is not None:
            return self.op("dve", lambda e: e.scalar_tensor_tensor(out=out, in0=in0, scalar=scalar, in1=in1, op0=op0,
                                                                   op1=op1, accum_out=accum_out), r, w)
        return self.op("dve", lambda e: e.scalar_tensor_tensor(out=out, in0=in0, scalar=scalar, in1=in1, op0=op0, op1=op1), r, w)

    def scan(self, out, d0, d1, init, r, w):
        return self.op("dve", lambda e: e.tensor_tensor_scan(out=out, data0=d0, data1=d1, initial=init,
                                                             op0=ALU.mult, op1=ALU.add), r, w)

    def cp(self, eng, out, in_, r, w):
        if eng == "act":
            return self.op("act", lambda e: e.activation(out=out, in_=in_, func=AF.Copy), r, w)
        return self.op(eng, lambda e: e.tensor_copy(out=out, in_=in_), r, w)

    def recip(self, out, in_, r, w):
        return self.op("dve", lambda e: e.reciprocal(out=out, in_=in_), r, w)

    def memset(self, eng, ap, val, w):
        return self.op(eng, lambda e: e.memset(ap, val), (), w)

    def finish(self):
        nc = self.nc
        for e in ENGS:
            for o in self.ops[e]:
                for k, v in o.deps:
                    if isinstance(k, str):
                        self.ops[k][v].need_inc = True
        for e in ENGS:
            c = 0
            for o in self.ops[e]:
                if o.dma is None and o.need_inc:
                    c += 1
                    o.val = c
        sems = {e: self.stack.enter_context(nc.semaphore("s_" + e)) for e in ENGS}
        dsems = [self.stack.enter_context(nc.semaphore("s_dma%d" % i)) for i in range(N_DMA_SEMS)]
        final_waits = [(("dma", i), 16 * self.dma_cnt[i]) for i in range(N_DMA_SEMS) if self.dma_cnt[i]]

        def emit(ename, eobj):
            seen = {}
            for o in self.ops[ename]:
                for k, v in o.deps:
                    if isinstance(k, str):
                        val = self.ops[k][v].val
                        sem = sems[k]
                    else:
                        val = v
                        sem = dsems[k[1]]
                    if seen.get(k, -1) >= val:
                        continue
                    seen[k] = val
                    eobj.wait_ge(sem, val)
                ins = o.fn(eobj)
                if o.dma is not None:
                    ins.then_inc(dsems[o.dma[0]], 16)
                elif o.need_inc:
                    ins.then_inc(sems[ename], 1)
            if ename == "sp":
                for k, val in final_waits:
                    if seen.get(k, -1) < val:
                        eobj.wait_ge(dsems[k[1]], val)

        with nc.Block() as block:
            @block.tensor
            def _(eng):
                emit("pe", eng)

            @block.scalar
            def _(eng):
                emit("act", eng)

            @block.vector
            def _(eng):
                emit("dve", eng)

            @block.gpsimd
            def _(eng):
                emit("pool", eng)

            @block.sync
            def _(eng):
                emit("sp", eng)
        self.stack.close()
        return nc


def mk_ap(buf, offset, pattern):
    return AP(buf.h, offset, [list(p) for p in pattern])


def bcast_col(col_ap, n):
    return AP(col_ap.tensor, col_ap.offset, [list(col_ap.ap[0]), [0, n]])


class Ring:
    def __init__(self, bufs):
        self.bufs = bufs
        self.i = 0

    def next(self):
        b = self.bufs[self.i % len(self.bufs)]
        self.i += 1
        return b


DM = 1024
SEQ = 4096
NCTX = 256
NA = SEQ + NCTX
NIN = 9728
FFW = 2816
PVL = 138
NPV = 2 * PVL + 8
C_Q, C_K, C_V, C_SB, C_SC, C_SX, C_RX, C_RG, C_GT = 0, 1024, 1280, 1536, 2560, 3584, 4608, 5632, 6656
EPS = 1e-6


def a_tiles(do_ctx):
    tl = []
    if do_ctx:
        tl.append((0, 256, False, 0))
    for i in range(8):
        tl.append((256 + 512 * i, 512, True, 512 * i))
    return tl


def build(nlayers=2, stop_after=None, dbg=()):
    P = Prog()
    kin = "ExternalInput"

    def DI(name, shape, dt=F32):
        return P.dram(name, shape, dt, kin)

    def DS(name, shape, dt):
        return P.dram(name, shape, dt, "ExternalOutput" if name in dbg else "Internal")

    x_d = DI("x", [SEQ, DM]); ctx_d = DI("ctx", [NCTX, DM]); cc_d = DI("cc", [128, 16]); pv_d = DI("pv", [128, NPV])
    bmod_d = DI("bmod", [2, 6 * DM]); fg_d = DI("fgrow", [1, DM]); cos_d = DI("cost", [128, SEQ]); sin_d = DI("sint", [128, SEQ])
    cm_d = DI("cm", [128, 256])
    wmod_d = DI("w_mod", [2, DM, 6 * DM]); win_d = DI("w_in", [2, DM, NIN])
    wao_d = DI("w_ao", [2, DM, DM]); wso_d = DI("w_so", [2, DM, DM]); wlo_d = DI("w_lo", [2, DM, DM]); wmo_d = DI("w_mo", [2, DM, DM])
    wfi_d = DI("w_fi", [2, DM, 2 * FFW]); wfo_d = DI("w_fo", [2, FFW, DM])
    lwa_d = DI("lwa", [2, 2, 8, 128, 128]); lwx_d = DI("lwx", [2, 2, 8, 128, 128])
    y_d = P.dram("y", [SEQ, DM], F32, "ExternalOutput")

    XS = DS("XS", [SEQ, DM], F32)
    QS = DS("QS", [8, 128, NA], BF16); KS = DS("KS", [2, 128, NA], BF16); VS = DS("VS", [128, 34, 256], BF16)
    AT = DS("ATs", [8, 128, NA], BF16); SC = DS("SCs", [8, 128, NA], BF16); LR = DS("LRs", [8, 128, NA], BF16)
    GS = DS("GSs", [24, 128, NA], BF16); FA = DS("FAs", [22, 128, NA], BF16); GRD = DS("GRD", [8, 128, DM], F32)
    HTD = DS("HTD", [128, 8, NA], BF16)
    XCD = DS("XCD", [128, 2, DM], F32)

    PS = [P.psum("ps%d" % i, [128, 512], F32) for i in range(8)]

    ident = P.persist("ident", [128, 128], F32); perm = P.persist("perm", [128, 128], F32)
    ones_f = P.persist("ones_f", [128, 128], F32); ones_b = P.persist("ones_b", [128, 128], BF16)
    eps_t = P.persist("eps_t", [128, 1], F32); one_t = P.persist("one_t", [128, 1], F32)
    perm_b = P.persist("perm_b", [128, 128], BF16)
    pv = P.persist("pv", [128, NPV], F32); cc = P.persist("cc", [128, 16], F32); scc = P.persist("scc", [128, 16], F32)
    XC = P.persist("XC", [128, 2, DM], F32)
    MV = P.persist("MV", [128, 2 * 2 * 4 * 8], F32)
    LV = P.persist("LV", [128, 2 * 3 * 16], F32)
    QG = P.persist("QG", [128, 2], F32)

    def mvc(l, who, vi, j0=0, n=8):
        o = ((l * 2 + who) * 4 + vi) * 8 + j0
        return MV[:, o:o + n]

    def lvc(l, which, d, n):
        o = (l * 3 + which) * 16 + d * 8 + n
        return LV[:, o:o + 1]

    def pvc(l, off, n=1):
        return pv[:, l * PVL + off:l * PVL + off + n]

    P.dma("sp", ident[:, :], cm_d[:, 0:128], w=[ident]); P.dma("sp", perm[:, :], cm_d[:, 128:256], w=[perm])
    P.dma("sp", pv[:, :], pv_d[:, :], w=[pv]); P.dma("sp", cc[:, :], cc_d[:, :], w=[cc])
    P.dma("sp", XC[:, :, :], ctx_d.h.rearrange("(t p) f -> p t f", p=128), w=[XC])
    P.memset("dve", ones_f[:, :], 1.0, [ones_f]); P.memset("dve", ones_b[:, :], 1.0, [ones_b])
    P.memset("dve", eps_t[:, :], EPS, [eps_t]); P.memset("dve", one_t[:, :], 1.0, [one_t])
    P.actv(scc[:, :], cc[:, :], AF.Silu, [cc], [scc])
    P.cp("dve", perm_b[:, :], perm[:, :], [perm], [perm_b])
    for l in range(2):
        P.ts("dve", QG[:, l:l + 1], pvc(l, 136), 128 ** -0.5, None, ALU.mult, None, [pv], [QG])
        P.ts("dve", LV[:, (l * 3 + 0) * 16:(l * 3 + 0) * 16 + 16], pvc(l, 88, 16), 0.5, None, ALU.mult, None, [pv], [LV])
        P.ts("dve", LV[:, (l * 3 + 1) * 16:(l * 3 + 1) * 16 + 16], pvc(l, 104, 16), 0.5, None, ALU.mult, None, [pv], [LV])
    tmpl = P.persist("tmpl", [128, 32], F32); tser = P.persist("tser", [128, 32], F32)
    ttmp = P.persist("ttmp", [128, 32], F32); tmsk = P.persist("tmsk", [128, 32], F32); tln = P.persist("tln", [128, 32], F32)
    for l in range(2):
        P.actv(tmpl[:, l * 16:l * 16 + 16], pvc(l, 120, 16), AF.Exp, [pv], [tmpl], scale=-1.0)
    P.actv(tln[:, :], tmpl[:, :], AF.Ln, [tmpl, one_t], [tln], bias=one_t[:, 0:1])
    P.memset("dve", tser[:, :], 1.0 / 8, [tser])
    for c in (1.0 / 7, 1.0 / 6, 1.0 / 5, 1.0 / 4, 1.0 / 3, 1.0 / 2, 1.0):
        P.tt("dve", ttmp[:, :], tser[:, :], tmpl[:, :], ALU.mult, [tser, tmpl], [ttmp])
        P.ts("dve", tser[:, :], ttmp[:, :], -1.0, c, ALU.mult, ALU.add, [ttmp], [tser])
    P.tt("dve", tser[:, :], tser[:, :], tmpl[:, :], ALU.mult, [tser, tmpl], [tser])
    P.ts("dve", tmsk[:, :], tmpl[:, :], 0.25, None, ALU.is_lt, None, [tmpl], [tmsk])
    P.tt("dve", ttmp[:, :], tser[:, :], tln[:, :], ALU.subtract, [tser, tln], [ttmp])
    P.tt("dve", ttmp[:, :], ttmp[:, :], tmsk[:, :], ALU.mult, [ttmp, tmsk], [ttmp])
    P.tt("dve", tln[:, :], tln[:, :], ttmp[:, :], ALU.add, [tln, ttmp], [tln])
    for l in range(2):
        P.ts("dve", LV[:, (l * 3 + 2) * 16:(l * 3 + 2) * 16 + 16], tln[:, l * 16:l * 16 + 16], -4.0, None, ALU.mult, None, [tln], [LV])

    win_v = win_d.h.rearrange("l (kc p) n -> l p kc n", p=128)

    def stop(tag):
        return stop_after == tag

    def phase_mod(l):
        P.phase()
        bm = P.sb("bm", [128, 6 * DM], F32)
        P.dma("sp", bm[:, :], mk_ap(bmod_d, l * 6 * DM, [[0, 128], [1, 6 * DM]]), w=[bm])
        wr = Ring([P.sb("wm%d" % i, [128, 8, 512], BF16) for i in range(3)])
        gr = Ring([P.sb("gt%d" % i, [128, 512], F32) for i in range(2)])
        tr_ = Ring([P.sb("tm%d" % i, [128, 512], F32) for i in range(2)])
        jr = Ring([P.sb("jk%d" % i, [128, 128], F32) for i in range(2)])
        wv = wmod_d.h.rearrange("l (kc p) n -> l p kc n", p=128)
        sccb = P.sb("sccb", [128, 16, 128], BF16)
        a0_ = scc[:, 0:16]
        P.cp("dve", sccb[:, :, :], AP(a0_.tensor, a0_.offset, [list(a0_.ap[0]), [1, 16], [0, 128]]), [scc], [sccb])
        for ti in range(12):
            v, hf = divmod(ti, 2)
            wt = wr.next()
            P.dma("pool", wt[:, :, :], wv[l, :, :, ti * 512:(ti + 1) * 512], w=[wt])
            for who in range(2):
                ps = PS[(ti * 2 + who) % 4]
                for kc in range(8):
                    P.mm(ps[:, :], sccb[:, who * 8 + kc, :], wt[:, kc, :], kc == 0, kc == 7, [sccb, wt], [ps])
                bsl = bm[:, ti * 512:(ti + 1) * 512]
                if v in (2, 5):
                    gt = gr.next()
                    P.tt("dve", gt[:, :], ps[:, :], bsl, ALU.add, [ps, bm], [gt])
                    gi = l * 4 + who * 2 + (1 if v == 5 else 0)
                    P.dma("sp", GRD[gi, :, hf * 512:(hf + 1) * 512], gt[:, :], r=[gt], w=[(GRD, (gi, hf))])
                else:
                    vi = {0: 0, 1: 1, 3: 2, 4: 3}[v]
                    tm = tr_.next()
                    P.tt("dve", tm[:, :], ps[:, :], bsl, ALU.add, [ps, bm], [tm])
                    for blk in range(4):
                        jk = jr.next()
                        j = hf * 4 + blk
                        P.stt(jk[:, :], tm[:, blk * 128:(blk + 1) * 128], 1.0, ident[:, :], ALU.mult, ALU.mult,
                              [tm, ident], [jk, (MV, (l, who, vi, j))], accum_out=mvc(l, who, vi, j, 1))
        t8 = P.sb("t8", [128, 8], F32)
        for who in range(2):
            for vi, goff in ((1, 0), (3, 8)):
                P.ts("dve", t8[:, :], mvc(l, who, vi), 1.0, None, ALU.add, None, [MV], [t8])
                P.tt("dve", mvc(l, who, vi), t8[:, :], pvc(l, goff, 8), ALU.mult, [t8, pv], [MV])

    def phase_norm(l, which, src, do_ctx):
        P.phase()
        HT = P.sb_top("HT", [128, 8, NA], BF16)
        vi_sh, vi_gs = (0, 1) if which == 1 else (2, 3)
        xr = Ring([P.sb("xt%d" % i, [128, DM], F32) for i in range(4)])
        xnr = Ring([P.sb("xn%d" % i, [128, DM], F32) for i in range(3)])
        sqj = P.sb("sqj", [128, DM], BF16)
        sm = Ring([P.sb("sm%d" % i, [128, 1], F32) for i in range(12)])
        tiles = ([(1, 0), (1, 1)] if do_ctx else []) + [(0, t) for t in range(32)]

        def stats(n):
            who, tt_ = tiles[n]
            if who == 0:
                xt = xr.next()
                P.dma("sp", xt[:, :], src[tt_ * 128:(tt_ + 1) * 128, :], w=[xt])
                xin, xdep = xt[:, :], xt
            else:
                xin, xdep = XC[:, tt_, :], XC
            ss = sm.next(); sd = sm.next(); rs = sm.next()
            P.actv(sqj[:, :], xin, AF.Square, [xdep], [sqj, ss], accum_out=ss[:, 0:1])
            P.actv(sd[:, :], ss[:, :], AF.Sqrt, [ss, eps_t], [sd], scale=1.0 / DM, bias=eps_t[:, 0:1])
            P.recip(rs[:, :], sd[:, :], [sd], [rs])
            xn = xnr.next()
            P.ts("dve", xn[:, :], xin, rs[:, 0:1], None, ALU.mult, None, [xdep, rs], [xn])
            return xn

        def tpose(n, xn):
            who, tt_ = tiles[n]
            col0 = (256 + tt_ * 128) if who == 0 else tt_ * 128
            for half in range(2):
                ps = PS[(2 * n + half) % 8]
                for q in range(4):
                    kc = half * 4 + q
                    P.tr(ps[:, q * 128:(q + 1) * 128], xn[:, kc * 128:(kc + 1) * 128], ident[:, :], [xn, ident], [ps])
                for q in range(4):
                    kc = half * 4 + q
                    o = HT[:, kc, col0:col0 + 128]
                    scl = mvc(l, who, vi_gs, kc, 1); shf = mvc(l, who, vi_sh, kc, 1)
                    if half == 0:
                        P.actv(o, ps[:, q * 128:(q + 1) * 128], AF.Identity, [ps, MV], [(HT, (n, kc))], scale=scl, bias=shf)
                    else:
                        P.ts("dve", o, ps[:, q * 128:(q + 1) * 128], scl, shf, ALU.mult, ALU.add, [ps, MV], [(HT, (n, kc))])
        prev = stats(0)
        for n in range(1, len(tiles)):
            cur = stats(n)
            tpose(n - 1, prev)
            prev = cur
        tpose(len(tiles) - 1, prev)
        return HT

    def keep_ht():
        return P.sb_top("HT", [128, 8, NA], BF16)

    def group(ps, n, wfn, wdep, HT, a0):
        for kc in range(8):
            P.mm(ps[:, 0:n], wfn(kc), HT[:, kc, a0:a0 + n], kc == 0, kc == 7, [wdep], [ps])

    def wload(buf3, l, col0, ncols, slot=None):
        dst = buf3[:, :, 0:ncols] if slot is None else buf3[:, slot, :, 0:ncols]
        P.dma("pool", dst, win_v[l, :, :, col0:col0 + ncols], w=[buf3])

    def phase_sconv_gates(l, do_ctx):
        P.phase()
        HT = keep_ht()
        tiles = a_tiles(do_ctx)
        W = NA + 4
        Mb = P.sb("Mb", [128, W], F32); Ab = P.sb("Ab", [128, W], F32)
        OBr = Ring([P.sb("OB%d" % i, [128, NA], BF16) for i in range(2)])
        scr = Ring([P.sb("sce%d" % i, [128, 512], F32) for i in range(3)])
        wbufs = [P.sb("ws%d" % i, [128, 3, 8, 128], BF16) for i in range(3)]
        for c in (0, 257, 258, W - 1):
            P.memset("pool", Mb[:, c:c + 1], 0.0, [Mb])

        def mcol(a0, is_lat):
            return (259 + (a0 - 256)) if is_lat else (1 + a0)

        def load(j):
            wb = wbufs[j % 3]
            for s, cb in enumerate((C_SB, C_SC, C_SX)):
                wload(wb, l, cb + j * 128, 128, slot=s)
            return wb
        loaded = {0: load(0), 1: load(1)}
        pi = 0
        for j in range(8):
            wb = loaded[j]
            if j + 2 < 8:
                loaded[j + 2] = load(j + 2)
            for (a0, n, is_lat, t0) in tiles:
                psc = PS[pi % 8]; psx = PS[(pi + 1) % 8]; pi += 2
                group(psc, n, lambda kc: wb[:, 1, kc, :], wb, HT, a0)
                group(psx, n, lambda kc: wb[:, 2, kc, :], wb, HT, a0)
                sce = scr.next()
                P.cp("act", sce[:, 0:n], psc[:, 0:n], [psc], [sce])
                m0 = mcol(a0, is_lat)
                P.tt("dve", Mb[:, m0:m0 + n], psx[:, 0:n], sce[:, 0:n], ALU.mult, [psx, sce], [(Mb, a0)])
            lo, hi = (1, W - 1) if do_ctx else (259, W - 1)
            L = hi - lo
            w0 = pvc(l, 16 + 0 * 8 + j); w1 = pvc(l, 16 + 1 * 8 + j); w2 = pvc(l, 16 + 2 * 8 + j); bb = pvc(l, 40 + j)
            P.ts("dve", Ab[:, lo:hi], Mb[:, lo - 1:hi - 1], w0, bb, ALU.mult, ALU.add, [Mb, pv], [Ab])
            P.stt(Ab[:, lo:hi], Mb[:, lo:hi], w1, Ab[:, lo:hi], ALU.mult, ALU.add, [Mb, pv, Ab], [Ab])
            P.stt(Ab[:, lo:hi], Mb[:, lo + 1:hi + 1], w2, Ab[:, lo:hi], ALU.mult, ALU.add, [Mb, pv, Ab], [Ab])
            OB = OBr.next()
            for (a0, n, is_lat, t0) in tiles:
                psb = PS[pi % 8]; pi += 1
                group(psb, n, lambda kc: wb[:, 0, kc, :], wb, HT, a0)
                m0 = mcol(a0, is_lat)
                P.tt("dve", OB[:, a0:a0 + n], psb[:, 0:n], Ab[:, m0:m0 + n], ALU.mult, [psb, Ab], [(OB, a0)])
            c0 = 0 if do_ctx else 256
            P.dma("sp", SC[j, :, c0:NA], OB[:, c0:NA], r=[OB], w=[(SC, j)])
        gw = [P.sb("gw%d" % i, [128, 8, 512], BF16) for i in range(2)]
        wload(gw[0], l, C_GT, 512)
        for g4 in range(6):
            if g4 + 1 < 6:
                wload(gw[(g4 + 1) % 2], l, C_GT + (g4 + 1) * 512, 512)
            wb = gw[g4 % 2]
            for s in range(4):
                ch = g4 * 4 + s
                OB = OBr.next()
                for (a0, n, is_lat, t0) in tiles:
                    ps = PS[pi % 8]; pi += 1
                    group(ps, n, lambda kc: wb[:, kc, s * 128:(s + 1) * 128], wb, HT, a0)
                    P.actv(OB[:, a0:a0 + n], ps[:, 0:n], AF.Sigmoid, [ps], [(OB, a0)])
                c0 = 0 if do_ctx else 256
                P.dma("sp", GS[ch, :, c0:NA], OB[:, c0:NA], r=[OB], w=[(GS, ch)])

    def phase_lru(l, do_ctx):
        P.phase()
        HT = keep_ht()
        tiles = a_tiles(True)
        W = NA + 6
        U = P.sb("U", [128, W], F32); M = P.sb("M", [128, W], F32)
        A_ = [P.sb("A%d" % d, [128, W], F32) for d in range(2)]
        D_ = [P.sb("D%d" % d, [128, W], F32) for d in range(2)]
        UB = P.sb("UB", [128, W], BF16)
        wrx = [P.sb("wrx%d" % i, [128, 2, 8, 128], BF16) for i in range(2)]
        wl = [P.sb("wl%d" % i, [128, 2, 2, 128], BF16) for i in range(2)]

        def lcol(a0, is_lat):
            return (261 + (a0 - 256)) if is_lat else (2 + a0)

        def load(n):
            wb = wrx[n % 2]; w2 = wl[n % 2]
            wload(wb, l, C_RX + n * 128, 128, slot=0)
            wload(wb, l, C_RG + n * 128, 128, slot=1)
            P.dma("pool", w2[:, 0, :, :], lwa_d.h.rearrange("l d n p e -> l n p d e")[l, n], w=[w2])
            P.dma("pool", w2[:, 1, :, :], lwx_d.h.rearrange("l d n p e -> l n p d e")[l, n], w=[w2])
            return wb, w2

        def rev(b_, c0, nn):
            a_ = b_[:, c0:c0 + nn]
            return AP(a_.tensor, a_.offset + nn - 1, [list(a_.ap[0]), [-1, nn]])
        for b_ in (A_[0], A_[1], D_[0], D_[1]):
            P.memset("pool", b_[:, 258:261], 0.0, [b_])
        nxt = load(0)
        pi = 0
        LO, HI = 2, W - 1
        for n in range(8):
            wb, w2 = nxt
            if n + 1 < 8:
                nxt = load(n + 1)
            RX = D_[1]
            for (c, k) in ((0, 2), (258, 3), (W - 1, 1)):
                P.memset("pool", RX[:, c:c + k], 0.0, [RX])
            for (a0, nn, is_lat, t0) in tiles:
                ps = PS[pi % 8]; pi += 1
                group(ps, nn, lambda kc: wb[:, 0, kc, :], wb, HT, a0)
                c0 = lcol(a0, is_lat)
                if pi % 2:
                    P.cp("act", RX[:, c0:c0 + nn], ps[:, 0:nn], [ps], [(RX, a0)])
                else:
                    P.cp("dve", RX[:, c0:c0 + nn], ps[:, 0:nn], [ps], [(RX, a0)])
            wc = [pvc(l, 48 + k * 8 + n) for k in range(4)]
            P.ts("dve", U[:, LO:HI], RX[:, LO - 2:HI - 2], wc[0], pvc(l, 80 + n), ALU.mult, ALU.add, [RX, pv], [U])
            for k in (1, 2, 3):
                P.stt(U[:, LO:HI], RX[:, LO - 2 + k:HI - 2 + k], wc[k], U[:, LO:HI], ALU.mult, ALU.add, [RX, pv, U], [U])
            P.cp("act", UB[:, LO:HI], U[:, LO:HI], [U], [UB])
            for d in range(2):
                A = A_[d]; D = D_[d]
                for (a0, nn, is_lat, t0) in tiles:
                    c0 = lcol(a0, is_lat)
                    pr = PS[pi % 8]; pz = PS[(pi + 1) % 8]; pi += 2
                    P.mm(pr[:, 0:nn], w2[:, 0, d, :], UB[:, c0:c0 + nn], True, True, [w2, UB], [pr])
                    P.mm(pz[:, 0:nn], w2[:, 1, d, :], UB[:, c0:c0 + nn], True, True, [w2, UB], [pz])
                    P.actv(A[:, c0:c0 + nn], pr[:, 0:nn], AF.Tanh, [pr, LV], [(A, a0)], scale=0.5, bias=lvc(l, 0, d, n))
                    P.actv(D[:, c0:c0 + nn], pz[:, 0:nn], AF.Tanh, [pz, LV], [(D, a0)], scale=0.5, bias=lvc(l, 1, d, n))
                P.actv(A[:, LO:HI], A[:, LO:HI], AF.Exp, [A, LV], [A], scale=lvc(l, 2, d, n), bias=lvc(l, 2, d, n))
                P.stt(D[:, LO:HI], D[:, LO:HI], 1.0, U[:, LO:HI], ALU.add, ALU.mult, [D, U], [D])
                P.actv(M[:, LO:HI], A[:, LO:HI], AF.Square, [A], [M])
                P.ts("dve", M[:, LO:HI], M[:, LO:HI], 1.0, None, ALU.min, None, [M], [M])
                P.actv(M[:, LO:HI], M[:, LO:HI], AF.Sqrt, [M, one_t], [M], scale=-1.0, bias=one_t[:, 0:1])
                P.stt(D[:, LO:HI], D[:, LO:HI], 0.5, M[:, LO:HI], ALU.mult, ALU.mult, [D, M], [D])
                if d == 0:
                    P.scan(D[:, 2:258], A[:, 2:258], D[:, 2:258], 0.0, [A, D], [D])
                    P.scan(D[:, 261:4357], A[:, 261:4357], D[:, 261:4357], D[:, 257:258], [A, D], [D])
                else:
                    P.scan(rev(D, 2, 256), rev(A, 2, 256), rev(D, 2, 256), 0.0, [A, D], [D])
                    P.scan(rev(D, 261, 4096), rev(A, 261, 4096), rev(D, 261, 4096), D[:, 2:3], [A, D], [D])
            G = A_[0]; Y = A_[1]
            gt = tiles if do_ctx else tiles[1:]
            for (a0, nn, is_lat, t0) in gt:
                ps = PS[pi % 8]; pi += 1
                group(ps, nn, lambda kc: wb[:, 1, kc, :], wb, HT, a0)
                c0 = lcol(a0, is_lat)
                P.actv(G[:, c0:c0 + nn], ps[:, 0:nn], AF.Gelu_apprx_tanh, [ps], [(G, a0)])
            P.tt("dve", D_[0][:, LO:HI], D_[0][:, LO:HI], D_[1][:, LO:HI], ALU.add, [D_[0], D_[1]], [D_[0]])
            P.tt("dve", Y[:, LO:HI], D_[0][:, LO:HI], G[:, LO:HI], ALU.mult, [D_[0], G], [Y])
            if do_ctx:
                P.dma("pool", LR[n, :, 0:256], Y[:, 2:258], r=[Y], w=[(LR, (n, 0))])
            P.dma("pool", LR[n, :, 256:NA], Y[:, 261:4357], r=[Y], w=[(LR, (n, 1))])

    def phase_qkv(l, do_ctx):
        P.phase()
        HT = keep_ht()
        cosb = P.sb("cosb", [128, SEQ], F32); sinb = P.sb("sinb", [128, SEQ], F32)
        P.dma("sp", cosb[:, :], cos_d[:, :], w=[cosb]); P.dma("sp", sinb[:, :], sin_d[:, :], w=[sinb])
        OBr = Ring([P.sb("OBq%d" % i, [128, NA], BF16) for i in range(2)])
        VB = P.sb("VB", [128, 34, 256], BF16)
        wv_ = P.sb("wv", [128, 8, 256], BF16)
        wk = P.sb("wk", [128, 8, 256], BF16)
        wq = [P.sb("wq%d" % i, [128, 8, 512], BF16) for i in range(2)]
        mkr = lambda nm, k, dt: Ring([P.sb("%s%d" % (nm, i), [128, 512], dt) for i in range(k)])
        sqr = mkr("sq", 3, BF16); sdr = mkr("sd", 2, F32); rsr = mkr("rs", 2, F32); qnr = mkr("qn", 4, F32)
        qbr = mkr("qnb", 3, BF16); t1r = mkr("t1", 2, F32); t2r = mkr("t2", 2, F32)
        wload(wk, l, C_K, 256); wload(wv_, l, C_V, 256); wload(wq[0], l, C_Q, 512); wload(wq[1], l, C_Q + 512, 512)

        def run_items(items):
            N = len(items)
            stt_ = {}

            def A(i):
                it = items[i]
                ps = PS[i % 4]
                group(ps, it["n"], it["wfn"], it["wdep"], HT, it["a0"])
                sq = sqr.next()
                P.actv(sq[:, 0:it["n"]], ps[:, 0:it["n"]], AF.Square, [ps], [sq])
                stt_[i] = {"ps": ps, "sq": sq}

            def B(i):
                it = items[i]; n = it["n"]; st = stt_[i]
                pss = PS[4 + i % 2]
                P.mm(pss[:, 0:n], ones_b[:, :], st["sq"][:, 0:n], True, True, [ones_b, st["sq"]], [pss])
                sd = sdr.next(); rs = rsr.next()
                P.actv(sd[:, 0:n], pss[:, 0:n], AF.Ln, [pss, eps_t], [sd], scale=1.0 / 128, bias=eps_t[:, 0:1])
                P.actv(rs[:, 0:n], sd[:, 0:n], AF.Exp, [sd], [rs], scale=-0.5)
                if not it["lat"]:
                    P.stt(it["out"], st["ps"][:, 0:n], it["g"], rs[:, 0:n], ALU.mult, ALU.mult, [st["ps"], it["gdep"], rs], it["outw"])
                else:
                    qn = qnr.next(); qb = qbr.next()
                    P.stt(qn[:, 0:n], st["ps"][:, 0:n], it["g"], rs[:, 0:n], ALU.mult, ALU.mult, [st["ps"], it["gdep"], rs], [qn])
                    st["qn"] = qn; st["qb"] = qb

            def Cc(i):
                it = items[i]; n = it["n"]; st = stt_[i]
                if it["lat"]:
                    P.cp("act", st["qb"][:, 0:n], st["qn"][:, 0:n], [st["qn"]], [st["qb"]])

            def C(i):
                it = items[i]; n = it["n"]; st = stt_.pop(i)
                if it["lat"]:
                    t0 = it["t0"]
                    pq = PS[6 + i % 2]
                    P.mm(pq[:, 0:n], perm_b[:, :], st["qb"][:, 0:n], True, True, [perm_b, st["qb"]], [pq])
                    t1 = t1r.next(); t2 = t2r.next()
                    P.tt("pool", t1[:, 0:n], st["qn"][:, 0:n], cosb[:, t0:t0 + n], ALU.mult, [st["qn"], cosb], [t1])
                    P.tt("dve", t2[:, 0:n], pq[:, 0:n], sinb[:, t0:t0 + n], ALU.mult, [pq, sinb], [t2])
                    P.tt("pool", it["out"], t1[:, 0:n], t2[:, 0:n], ALU.add, [t1, t2], it["outw"])
                if it.get("done"):
                    it["done"]()
            for idx in range(N + 3):
                if idx < N:
                    A(idx)
                if 0 <= idx - 1 < N:
                    B(idx - 1)
                if 0 <= idx - 2 < N:
                    Cc(idx - 2)
                if 0 <= idx - 3 < N:
                    C(idx - 3)

        tiles_all = a_tiles(True)
        items = []
        for h in range(2):
            OB = OBr.next()
            for ti, (a0, n, is_lat, t0) in enumerate(tiles_all):
                it = dict(n=n, a0=a0, lat=is_lat, t0=t0, wfn=(lambda kc, h=h: wk[:, kc, h * 128:(h + 1) * 128]), wdep=wk,
                          g=pvc(l, 137), gdep=pv, out=OB[:, a0:a0 + n], outw=[(OB, a0)])
                if ti == len(tiles_all) - 1:
                    it["done"] = (lambda h=h, OB=OB: P.dma("sp", KS[h, :, :], OB[:, :], r=[OB], w=[(KS, h)]))
                items.append(it)
        run_items(items)
        for tt_ in range(34):
            ps = PS[tt_ % 4]
            for kc in range(8):
                P.mm(ps[:, 0:256], HT[:, kc, tt_ * 128:(tt_ + 1) * 128], wv_[:, kc, :], kc == 0, kc == 7, [wv_], [ps])
            if tt_ % 2:
                P.cp("act", VB[:, tt_, :], ps[:, 0:256], [ps], [(VB, tt_)])
            else:
                P.cp("dve", VB[:, tt_, :], ps[:, 0:256], [ps], [(VB, tt_)])
        P.dma("sp", VS[:, :, :], VB[:, :, :], r=[VB], w=[VS])
        tiles_q = a_tiles(do_ctx)
        c0 = 0 if do_ctx else 256
        items = []
        for h in range(8):
            OB = OBr.next() if h % 2 == 0 else OBr.bufs[1]
            OB = OBr.bufs[h % 2]
            wb = wq[h // 4]
            for ti, (a0, n, is_lat, t0) in enumerate(tiles_q):
                it = dict(n=n, a0=a0, lat=is_lat, t0=t0, wfn=(lambda kc, h=h, wb=wb: wb[:, kc, (h % 4) * 128:(h % 4 + 1) * 128]), wdep=wb,
                          g=QG[:, l:l + 1], gdep=QG, out=OB[:, a0:a0 + n], outw=[(OB, a0)])
                if ti == len(tiles_q) - 1:
                    it["done"] = (lambda h=h, OB=OB: P.dma("sp", QS[h, :, c0:NA], OB[:, c0:NA], r=[OB], w=[(QS, h)]))
                items.append(it)
        run_items(items)

    def phase_att(l, do_ctx):
        P.phase()
        KT = P.sb("KT", [128, 2, NA], BF16); V = P.sb("V", [128, 34, 256], BF16)
        P.dma("sp", KT[:, :, :], KS.h.rearrange("h p c -> p h c"), w=[KT])
        P.dma("sp", V[:, :, :], VS[:, :, :], w=[V])
        qr = Ring([P.sb("qb%d" % i, [128, 4, 128], BF16) for i in range(3)])
        ptr = Ring([P.sb("pt%d" % i, [128, 512], BF16) for i in range(10)])
        accr = {e: Ring([P.sb("acc%s%d" % (e, i), [128, 512], F32) for i in range(2)]) for e in ("dve", "pool")}
        rdr = Ring([P.sb("rd%d" % i, [128, 512], F32) for i in range(2)])
        obr = Ring([P.sb("ob%d" % i, [128, 4, 128], BF16) for i in range(2)])
        QSv = QS.h.rearrange("h p c -> p h c"); ATv = AT.h.rearrange("h p c -> p h c")
        blocks = []
        for g in range(2):
            if do_ctx:
                blocks += [(g, 0, 2), (g, 128, 2)]
            blocks += [(g, 256 + 128 * qb, 34) for qb in range(32)]
        qbufs = {}

        def qload(i):
            g, c0, nk = blocks[i]
            qb = qr.next()
            P.dma("sp", qb[:, :, :], QSv[:, 4 * g:4 * g + 4, c0:c0 + 128], w=[qb])
            qbufs[i] = qb
        qload(0); qload(1)
        si = 0

        def finalize(pO, pD, acc, used, g, c0, bi):
            for ui, e in enumerate(used):
                P.mm(pD[:, :], ones_f[:, :], acc[e][:, :], False, ui == len(used) - 1, [ones_f, acc[e]], [pD])
            rd = rdr.next(); ob = obr.next()
            P.recip(rd[:, :], pD[:, :], [pD], [rd])
            P.tt("dve", ob[:, :, :].rearrange("p h c -> p (h c)"), pO[:, :], rd[:, :], ALU.mult, [pO, rd], [ob])
            P.dma("sp", ATv[:, 4 * g:4 * g + 4, c0:c0 + 128], ob[:, :, :], r=[ob], w=[(AT, bi)])
        fin_prev = None
        fin_next = None
        for bi, (g, c0, nk) in enumerate(blocks):
            fin_prev = fin_next
            if bi + 2 < len(blocks):
                qload(bi + 2)
            qb = qbufs.pop(bi)
            qflat = qb[:, :, :].rearrange("p h c -> p (h c)")
            pO = PS[3 + bi % 2]; pD = PS[5 + bi % 2]
            acc = {"dve": accr["dve"].next(), "pool": accr["pool"].next()}
            inited = {"dve": False, "pool": False}
            plan = {}
            j = 0
            for k0 in range(nk):
                if k0 % 4 == 0:
                    plan[k0] = "pe"
                else:
                    plan[k0] = "pool" if j % 2 == 0 else "dve"
                    j += 1
            n_acc = len(set(v for v in plan.values() if v != "pe"))
            last_pe = max(k for k, v in plan.items() if v == "pe")
            pts = {}
            for kc in range(nk + 2):
                if kc == min(6, nk + 1) and fin_prev is not None:
                    finalize(*fin_prev)
                    fin_prev = None
                if kc < nk:
                    pS = PS[si % 3]; si += 1
                    P.mm(pS[:, :], KT[:, g, kc * 128:(kc + 1) * 128], qflat, True, True, [KT, qb], [pS])
                    pt = ptr.next()
                    P.actv(pt[:, :], pS[:, :], AF.Exp, [pS], [pt])
                    pts[kc] = pt
                k0 = kc - 2
                if k0 >= 0:
                    pt0 = pts.pop(k0)
                    P.mm(pO[:, :], V[:, k0, g * 128:(g + 1) * 128], pt0[:, :], k0 == 0, k0 == nk - 1, [V, pt0], [pO])
                    e = plan[k0]
                    if e == "pe":
                        P.mm(pD[:, :], ones_b[:, :], pt0[:, :], k0 == 0, (n_acc == 0 and k0 == last_pe), [ones_b, pt0], [pD])
                    else:
                        ac = acc[e]
                        if not inited[e]:
                            P.cp(e, ac[:, :], pt0[:, :], [pt0], [ac])
                            inited[e] = True
                        else:
                            P.tt(e, ac[:, :], ac[:, :], pt0[:, :], ALU.add, [ac, pt0], [ac])
            used = [e for e in ("dve", "pool") if inited[e]]
            fin_next = (pO, pD, acc, used, g, c0, bi)
            if bi == len(blocks) - 1:
                finalize(*fin_next)

    def residual_tile(l, who, s_rows, pso_fn, Grep, xsrc, last_final, fgrep, sm, xr, xor_, tmr, sqj):
        if who == 0:
            xt = xr.next()
            P.dma("sp", xt[:, :], xsrc[s_rows:s_rows + 128, :], r=[(xsrc, s_rows)], w=[xt])
            xin, xdep = xt, xt
            xo = xor_.next()
            xo_ap = lambda ch: xo[:, ch * 512:(ch + 1) * 512]
            xin_ap = lambda ch: xt[:, ch * 512:(ch + 1) * 512]
            xow = [xo]
        else:
            tt_ = s_rows // 128
            xo = None
            xo_ap = lambda ch: XC[:, tt_, ch * 512:(ch + 1) * 512]
            xin_ap = xo_ap
            xdep = XC
            xow = [XC]
        for ch in range(2):
            pso = pso_fn(ch)
            tm = tmr.next()
            P.tt("dve", tm[:, :], pso[:, :], Grep[:, ch * 512:(ch + 1) * 512], ALU.mult, [pso, Grep], [tm])
            P.tt("pool", xo_ap(ch), tm[:, :], xin_ap(ch), ALU.add, [tm, xdep], xow)
        if who == 0:
            if last_final:
                ss = sm.next(); sd = sm.next(); rs = sm.next()
                P.actv(sqj[:, :], xo[:, :], AF.Square, [xo], [sqj, ss], accum_out=ss[:, 0:1])
                P.actv(sd[:, :], ss[:, :], AF.Sqrt, [ss, eps_t], [sd], scale=1.0 / DM, bias=eps_t[:, 0:1])
                P.recip(rs[:, :], sd[:, :], [sd], [rs])
                yo = xr.next()
                P.stt(yo[:, :], xo[:, :], rs[:, 0:1], fgrep[:, :], ALU.mult, ALU.mult, [xo, rs, fgrep], [yo])
                P.dma("act", y_d[s_rows:s_rows + 128, :], yo[:, :], r=[yo], w=[(y_d, s_rows)])
            else:
                P.dma("act", XS[s_rows:s_rows + 128, :], xo[:, :], r=[xo], w=[(XS, s_rows)])

    def phase_merge(l, do_ctx):
        P.phase()
        Wt = {}
        for nm, d in (("a", wao_d), ("s", wso_d), ("l", wlo_d), ("m", wmo_d)):
            Wt[nm] = P.sb("W" + nm, [128, 8, DM], BF16)
        wviews = {nm: d.h.rearrange("l (kc p) n -> l p kc n", p=128) for nm, d in (("a", wao_d), ("s", wso_d), ("l", wlo_d), ("m", wmo_d))}
        for j4 in range(4):
            for nm in ("a", "s", "l"):
                P.dma("pool", Wt[nm][:, :, j4 * 256:(j4 + 1) * 256], wviews[nm][l, :, :, j4 * 256:(j4 + 1) * 256], w=[(Wt[nm], j4)])
        for hh in range(2):
            P.dma("pool", Wt["m"][:, :, hh * 512:(hh + 1) * 512], wviews["m"][l, :, :, hh * 512:(hh + 1) * 512], w=[(Wt["m"], hh)])
        G1 = [P.sb("G1_%d" % w_, [128, DM], F32) for w_ in range(2)]
        P.dma("sp", G1[0][:, :], GRD[l * 4 + 0, :, :], w=[G1[0]])
        if do_ctx:
            P.dma("sp", G1[1][:, :], GRD[l * 4 + 2, :, :], w=[G1[1]])
        inr = {k: Ring([P.sb("in%s%d" % (k, i), [128, 8, 512], BF16) for i in range(2)]) for k in ("a", "s", "l")}
        gjr = Ring([P.sb("gj%d" % i, [128, 3, 512], BF16) for i in range(3)])
        mr = {k: Ring([P.sb("m%s%d" % (k, i), [128, 512], F32) for i in range(2)]) for k in ("1", "2", "3", "12")}
        mgr = Ring([P.sb("mg%d" % i, [128, 8, 512], BF16) for i in range(2)])
        xr = Ring([P.sb("xt%d" % i, [128, DM], F32) for i in range(3)])
        xor_ = Ring([P.sb("xo%d" % i, [128, DM], F32) for i in range(2)])
        tmr = Ring([P.sb("tm%d" % i, [128, 512], F32) for i in range(2)])
        sm = Ring([P.sb("sm%d" % i, [128, 1], F32) for i in range(6)])
        srcv = {"a": AT.h.rearrange("h p c -> p h c"), "s": SC.h.rearrange("h p c -> p h c"), "l": LR.h.rearrange("h p c -> p h c")}
        GSv = GS.h.rearrange("(b j) p c -> j p b c", b=3)
        xsrc = x_d if l == 0 else XS
        tiles = a_tiles(do_ctx)
        ins = {}

        def tload(i):
            a0, n, is_lat, t0 = tiles[i]
            d = {}
            for k in ("a", "s", "l"):
                b = inr[k].next()
                P.dma("sp", b[:, :, 0:n], srcv[k][:, :, a0:a0 + n], w=[b])
                d[k] = b
            ins[i] = d
        tload(0)
        oi = 0

        def out_stage(mg, n, is_lat, t0):
            who = 0 if is_lat else 1
            for s in range(n // 128):
                def pso_fn(ch, s=s):
                    nonlocal oi
                    ps = PS[6 + oi % 2]; oi += 1
                    for kc in range(8):
                        P.mm(ps[:, :], mg[:, kc, s * 128:(s + 1) * 128], Wt["m"][:, kc, ch * 512:(ch + 1) * 512], kc == 0, kc == 7, [mg, (Wt["m"], ch)], [ps])
                    return ps
                rows = (t0 + s * 128) if is_lat else s * 128
                residual_tile(l, who, rows, pso_fn, G1[who], xsrc, False, None, sm, xr, xor_, tmr, None)
        pending = None
        for ti, (a0, n, is_lat, t0) in enumerate(tiles):
            if ti + 1 < len(tiles):
                tload(ti + 1)
            cur = ins.pop(ti)
            mg = mgr.next()
            for j in range(8):
                gj = gjr.next()
                P.dma("sp", gj[:, :, 0:n], GSv[j, :, :, a0:a0 + n], w=[gj])
                pss = {}
                for bi_, k in enumerate(("a", "s", "l")):
                    ps = PS[(j % 2) * 3 + bi_]
                    for kc in range(8):
                        P.mm(ps[:, 0:n], Wt[k][:, kc, j * 128:(j + 1) * 128], cur[k][:, kc, 0:n], kc == 0, kc == 7, [(Wt[k], j // 2), cur[k]], [ps])
                    pss[k] = ps
                m1 = mr["1"].next(); m2 = mr["2"].next(); m3 = mr["3"].next(); m12 = mr["12"].next()
                P.tt("dve", m1[:, 0:n], pss["a"][:, 0:n], gj[:, 0, 0:n], ALU.mult, [pss["a"], gj], [m1])
                P.tt("dve", m2[:, 0:n], pss["s"][:, 0:n], gj[:, 1, 0:n], ALU.mult, [pss["s"], gj], [m2])
                P.tt("dve", m3[:, 0:n], pss["l"][:, 0:n], gj[:, 2, 0:n], ALU.mult, [pss["l"], gj], [m3])
                P.tt("pool", m12[:, 0:n], m1[:, 0:n], m2[:, 0:n], ALU.add, [m1, m2], [m12])
                P.tt("pool", mg[:, j, 0:n], m12[:, 0:n], m3[:, 0:n], ALU.add, [m12, m3], [(mg, j)])
            if pending is not None:
                out_stage(*pending)
            pending = (mg, n, is_lat, t0)
        out_stage(*pending)

    def phase_ffn1(l, do_ctx):
        P.phase()
        HT = keep_ht()
        tiles = a_tiles(do_ctx)
        wb_ = [P.sb("wf%d" % i, [128, 2, 8, 128], BF16) for i in range(3)]
        OBr = Ring([P.sb("OBf%d" % i, [128, NA], BF16) for i in range(2)])
        sgr = Ring([P.sb("sg%d" % i, [128, 512], F32) for i in range(3)])
        wv = wfi_d.h.rearrange("l (kc p) n -> l p kc n", p=128)

        def load(j):
            wb = wb_[j % 3]
            P.dma("pool", wb[:, 0, :, :], wv[l, :, :, j * 128:(j + 1) * 128], w=[wb])
            P.dma("pool", wb[:, 1, :, :], wv[l, :, :, FFW + j * 128:FFW + (j + 1) * 128], w=[wb])
            return wb
        loaded = {0: load(0), 1: load(1)}
        pi = 0
        for j in range(22):
            wb = loaded.pop(j)
            if j + 2 < 22:
                loaded[j + 2] = load(j + 2)
            OB = OBr.next()
            for (a0, n, is_lat, t0) in tiles:
                pg = PS[pi % 8]; pu = PS[(pi + 1) % 8]; pi += 2
                group(pg, n, lambda kc: wb[:, 0, kc, :], wb, HT, a0)
                group(pu, n, lambda kc: wb[:, 1, kc, :], wb, HT, a0)
                sg = sgr.next()
                P.actv(sg[:, 0:n], pg[:, 0:n], AF.Silu, [pg], [sg])
                P.tt("dve", OB[:, a0:a0 + n], pu[:, 0:n], sg[:, 0:n], ALU.mult, [pu, sg], [(OB, a0)])
            c0 = 0 if do_ctx else 256
            P.dma("sp", FA[j, :, c0:NA], OB[:, c0:NA], r=[OB], w=[(FA, j)])

    def phase_ffn2(l, do_ctx, final):
        P.phase()
        WO = P.sb("WO", [128, 22, DM], BF16)
        vw = wfo_d.h.rearrange("l (kc p) n -> l p kc n", p=128)
        for q4 in range(4):
            P.dma("pool", WO[:, :, q4 * 256:(q4 + 1) * 256], vw[l, :, :, q4 * 256:(q4 + 1) * 256], w=[(WO, q4)])
        G2 = [P.sb("G2_%d" % w_, [128, DM], F32) for w_ in range(2)]
        P.dma("sp", G2[0][:, :], GRD[l * 4 + 1, :, :], w=[G2[0]])
        if do_ctx:
            P.dma("sp", G2[1][:, :], GRD[l * 4 + 3, :, :], w=[G2[1]])
        fgrep = None; sqj = None
        if final:
            fgrep = P.sb("fgrep", [128, DM], F32)
            P.dma("sp", fgrep[:, :], mk_ap(fg_d, 0, [[0, 128], [1, DM]]), w=[fgrep])
            sqj = P.sb("sqj", [128, DM], BF16)
        far = Ring([P.sb("fa%d" % i, [128, 22, 512], BF16) for i in range(2)])
        xr = Ring([P.sb("xt%d" % i, [128, DM], F32) for i in range(4)])
        xor_ = Ring([P.sb("xo%d" % i, [128, DM], F32) for i in range(2)])
        tmr = Ring([P.sb("tm%d" % i, [128, 512], F32) for i in range(2)])
        sm = Ring([P.sb("sm%d" % i, [128, 1], F32) for i in range(9)])
        FAv = FA.h.rearrange("j p c -> p j c")
        tiles = a_tiles(do_ctx)
        ins = {}

        def tload(i):
            a0, n, is_lat, t0 = tiles[i]
            b = far.next()
            P.dma("sp", b[:, :, 0:n], FAv[:, :, a0:a0 + n], w=[b])
            ins[i] = b
        tload(0)
        oi = 0
        for ti, (a0, n, is_lat, t0) in enumerate(tiles):
            if ti + 1 < len(tiles):
                tload(ti + 1)
            fa = ins.pop(ti)
            who = 0 if is_lat else 1
            for s in range(n // 128):
                def pso_fn(ch, s=s):
                    nonlocal oi
                    ps = PS[oi % 8]; oi += 1
                    for kc in range(22):
                        P.mm(ps[:, :], fa[:, kc, s * 128:(s + 1) * 128], WO[:, kc, ch * 512:(ch + 1) * 512], kc == 0, kc == 21, [fa, (WO, 2 * ch), (WO, 2 * ch + 1)], [ps])
                    return ps
                rows = (t0 + s * 128) if is_lat else s * 128
                residual_tile(l, who, rows, pso_fn, G2[who], XS, final, fgrep, sm, xr, xor_, tmr, sqj)

    def dump_ht():
        P.phase()
        HT = keep_ht()
        P.dma("sp", HTD[:, :, :], HT[:, :, :], w=[HTD])
        P.dma("sp", XCD[:, :, :], XC[:, :, :], w=[XCD])

    def run_all():
        if stop('const'):
            P.phase()
            P.dma('sp', GRD[0, :, 0:128], ident[:, :], w=[GRD])
            return
        for l in range(nlayers):
            last = l == 1
            do_ctx = not last
            phase_mod(l)
            if stop("mod%d" % l): return
            phase_norm(l, 1, x_d if l == 0 else XS, True)
            if stop("n1_%d" % l): dump_ht(); return
            phase_sconv_gates(l, do_ctx)
            if stop("sc%d" % l): return
            phase_lru(l, do_ctx)
            if stop("lru%d" % l): return
            phase_qkv(l, do_ctx)
            if stop("qkv%d" % l): return
            phase_att(l, do_ctx)
            if stop("att%d" % l): return
            phase_merge(l, do_ctx)
            if stop("mg%d" % l): dump_ht(); return
            phase_norm(l, 2, XS, do_ctx)
            if stop("n2_%d" % l): dump_ht(); return
            phase_ffn1(l, do_ctx)
            if stop("f1_%d" % l): return
            phase_ffn2(l, do_ctx, last)
            if stop("f2_%d" % l): dump_ht(); return
    run_all()
    nops = {e: len(P.ops[e]) for e in ENGS}
    nc = P.finish()
    return nc, nops


def _rope_tables():
    quarter = 32
    inv_freq = (10000.0 ** (-np.arange(quarter, dtype=np.float32) / quarter)).astype(np.float32)
    t = np.arange(SEQ)
    row = (t // 64).astype(np.float32); colp = (t % 64).astype(np.float32)
    cos = np.zeros((128, SEQ), np.float32); sin = np.zeros((128, SEQ), np.float32)
    for d in range(128):
        pos = row if d < 64 else colp
        j = d % 64
        i = j % 32
        ang = (pos * inv_freq[i]).astype(np.float32)
        cos[d] = np.cos(ang)
        sin[d] = np.sin(ang) * (-1.0 if j < 32 else 1.0)
    return cos, sin


def _consts():
    cm = np.zeros((128, 256), np.float32)
    cm[:, :128] = np.eye(128, dtype=np.float32)
    for m in range(128):
        j = m % 64
        k = m + 32 if j < 32 else m - 32
        cm[k, 128 + m] = 1.0
    return cm


def _fm(v):
    v = np.asarray(v, np.float32).reshape(-1, 8, 128)
    return np.ascontiguousarray(v.transpose(2, 0, 1).reshape(128, -1))


def make_in_maps(inp):
    f = lambda a: np.ascontiguousarray(np.asarray(a, np.float32))
    pvs = []
    for l in range(2):
        cols = [_fm(inp["norm1_g"][l]), _fm(inp["norm2_g"][l]), _fm(inp["sconv_w"][l]), _fm(inp["sconv_b"][l]),
                _fm(inp["lru_conv_w"][l]), _fm(inp["lru_conv_b"][l]), _fm(inp["lru_ba"][l]), _fm(inp["lru_bx"][l]),
                _fm(inp["lru_lambda"][l]), f(inp["q_norm_g"][l]).reshape(128, 1), f(inp["k_norm_g"][l]).reshape(128, 1)]
        pvs.append(np.concatenate(cols, axis=1))
        assert pvs[-1].shape[1] == PVL
    pv = np.ascontiguousarray(np.concatenate(pvs + [_fm(inp["final_g"])], axis=1))
    cos, sin = _rope_tables()
    shared = {
        "pv": pv, "bmod": f(inp["b_mod"]), "fgrow": f(inp["final_g"]).reshape(1, DM), "cost": cos, "sint": sin, "cm": _consts(),
        "w_mod": f(inp["w_mod"]), "w_in": f(inp["w_in"]), "w_ao": f(inp["w_attn_out"]), "w_so": f(inp["w_sconv_out"]),
        "w_lo": f(inp["w_lru_out"]), "w_mo": f(inp["w_merge_out"]), "w_fi": f(inp["w_ffn_in"]), "w_fo": f(inp["w_ffn_out"]),
        "lwa": f(inp["lru_wa"]), "lwx": f(inp["lru_wx"]),
    }
    maps = []
    ccx = _fm(inp["c_ctx"])
    for b in range(8):
        m = dict(shared)
        m["x"] = f(inp["x"][b]); m["ctx"] = f(inp["ctx"][b])
        m["cc"] = np.ascontiguousarray(np.concatenate([_fm(inp["c"][b]), ccx], axis=1))
        maps.append(m)
    return maps


def kernel(**inputs):
    nc = build()[0]
    maps = make_in_maps(inputs)
    res = run_bass_kernel_spmd(nc, maps, core_ids=list(range(8)))
    return np.stack([np.asarray(r["y"], np.float32) for r in res.results], axis=0)
```
